# Optimizing a Trainium2 kernel written in Bass

```python
import math
import jax, jax.numpy as jnp
from jax import lax
import numpy as np

D_MODEL = 1024
BATCH = 8
SEQ = 4096
DEPTH = 2

D_MIX = D_MODEL
A_WIDTH = D_MIX // 2
A_GROUPS = 8
A_GDIM = A_WIDTH // A_GROUPS
CHUNK = 128
B_HEADS = 8
HEAD_DIM = (D_MIX - A_WIDTH) // B_HEADS
B_WIDTH = B_HEADS * HEAD_DIM
IDX_HEADS = 8
IDX_DIM = HEAD_DIM
TOPK_MAX = 256
Q_BLOCK = 128
ROPE_THETA = 10000.0
LN_EPS = 1e-5
ALPHA = (2.0 * DEPTH) ** 0.25
BETA = (8.0 * DEPTH) ** -0.25

SPLITS = (A_WIDTH, A_WIDTH, A_WIDTH, B_WIDTH, HEAD_DIM, HEAD_DIM, B_WIDTH,
          IDX_HEADS * IDX_DIM, IDX_DIM, IDX_HEADS)
D_IN = sum(SPLITS)
SPLIT_POINTS = tuple(int(p) for p in np.cumsum(SPLITS)[:-1])

kernel_name = "hymba_gmlp_dsa_deepnorm_adaln"


def layer_norm(x, g=None, b=None):
    xf = x.astype(jnp.float32)
    mu = jnp.mean(xf, axis=-1, keepdims=True)
    var = jnp.mean(jnp.square(xf - mu), axis=-1, keepdims=True)
    y = (xf - mu) * lax.rsqrt(var + LN_EPS)
    if g is not None:
        y = y * g.astype(jnp.float32) + b.astype(jnp.float32)
    return y.astype(x.dtype)


def rope_tables(positions):
    inv_freq = ROPE_THETA ** (-jnp.arange(0, HEAD_DIM, 2, dtype=jnp.float32) / HEAD_DIM)
    ang = positions.astype(jnp.float32)[..., None] * inv_freq
    return jnp.cos(ang), jnp.sin(ang)


def apply_rope(x, cos, sin):
    x1, x2 = jnp.split(x.astype(jnp.float32), 2, axis=-1)
    return jnp.concatenate([x1 * cos - x2 * sin, x2 * cos + x1 * sin], axis=-1).astype(x.dtype)


def spatial_gating_unit(u, v, g_v, b_v, w_s, b_s):
    bn, s, _ = v.shape
    vn = layer_norm(v, g_v, b_v).reshape(bn, s // CHUNK, CHUNK, A_GROUPS, A_GDIM)
    causal = jnp.tril(jnp.ones((CHUNK, CHUNK), dtype=bool))
    w_m = jnp.where(causal[None], w_s, jnp.zeros_like(w_s))
    mixed = jnp.einsum('gts,bcsgd->bctgd', w_m, vn) + b_s.T[:, :, None]
    return u * mixed.reshape(bn, s, A_WIDTH)


def sparse_attention(q, k, v, q_idx, k_idx, w_idx):
    bn, s, _, _ = q.shape
    top = min(TOPK_MAX, s // 4)
    n_blk = s // Q_BLOCK
    key_pos = jnp.arange(s)
    attn_scale = HEAD_DIM ** -0.5
    w_scaled = w_idx * (IDX_HEADS ** -0.5)
    gather = jax.vmap(lambda tab, idx: tab[idx])

    def block(i):
        start = i * Q_BLOCK
        qb = lax.dynamic_slice_in_dim(q, start, Q_BLOCK, axis=1)
        qib = lax.dynamic_slice_in_dim(q_idx, start, Q_BLOCK, axis=1)
        wb = lax.dynamic_slice_in_dim(w_scaled, start, Q_BLOCK, axis=1)
        qpos = start + jnp.arange(Q_BLOCK)
        causal = key_pos[None, :] <= qpos[:, None]
        logits = jnp.einsum('bthd,bsd->bths', qib, k_idx) * (IDX_DIM ** -0.5)
        score = jnp.einsum('bth,bths->bts', wb, jax.nn.relu(logits)).astype(jnp.float32)
        score = jnp.where(causal[None], score, -jnp.inf)
        _, idx = lax.top_k(score, top)
        valid = idx <= qpos[None, :, None]
        k_sel = gather(k, idx)
        v_sel = gather(v, idx)
        sc = jnp.einsum('bthd,btkd->bthk', qb, k_sel).astype(jnp.float32) * attn_scale
        sc = jnp.where(valid[:, :, None, :], sc, -jnp.inf)
        p = jax.nn.softmax(sc, axis=-1).astype(v.dtype)
        return jnp.einsum('bthk,btkd->bthd', p, v_sel)

    out = lax.map(block, jnp.arange(n_blk))
    return out.transpose(1, 0, 2, 3, 4).reshape(bn, s, B_WIDTH)


def hybrid_layer(x, cond, cos, sin, w_ada, b_ada, w_in, v_norm_g, v_norm_b,
                 w_spatial, b_spatial, w_out, ln_g, ln_b):
    bn, s, _ = x.shape
    mod = cond @ w_ada + b_ada
    shift, scale, gate = jnp.split(mod, 3, axis=-1)
    h = layer_norm(x) * (1.0 + scale[:, None, :]) + shift[:, None, :]
    proj = h @ w_in
    u, v, z_a, q, k, val, z_b, q_idx, k_idx, w_idx = jnp.split(proj, SPLIT_POINTS, axis=-1)
    y_a = jax.nn.silu(z_a) * spatial_gating_unit(u, v, v_norm_g, v_norm_b, w_spatial, b_spatial)
    cq, sq = cos[:, :, None, :], sin[:, :, None, :]
    q = apply_rope(q.reshape(bn, s, B_HEADS, HEAD_DIM), cq, sq)
    k = apply_rope(k, cos, sin)
    q_idx = apply_rope(q_idx.reshape(bn, s, IDX_HEADS, IDX_DIM), cq, sq)
    k_idx = apply_rope(k_idx, cos, sin)
    y_b = jax.nn.silu(z_b) * sparse_attention(q, k, val, q_idx, k_idx, w_idx)
    y = jnp.concatenate([y_a, y_b], axis=-1) @ w_out
    return layer_norm(ALPHA * x + gate[:, None, :] * y, ln_g, ln_b)


def setup_inputs(seed: int = 0) -> dict:
    key = jax.random.key(seed)
    ks = jax.random.split(key, 14)
    f32 = jnp.float32
    x = jax.random.normal(ks[0], (BATCH, SEQ, D_MODEL), f32)
    c = jax.random.normal(ks[1], (BATCH, D_MODEL), f32)
    offs = jax.random.randint(ks[2], (BATCH, 1), 0, 1024, dtype=jnp.int32)
    positions = (offs + jnp.arange(SEQ, dtype=jnp.int32)[None, :]).astype(jnp.int32)
    w_ada = 0.5 * D_MODEL ** -0.5 * jax.random.normal(ks[3], (DEPTH, D_MODEL, 3 * D_MODEL), f32)
    gate_one = jnp.concatenate([jnp.zeros((2 * D_MODEL,), f32), jnp.ones((D_MODEL,), f32)])
    b_ada = gate_one[None] + 0.02 * jax.random.normal(ks[4], (DEPTH, 3 * D_MODEL), f32)
    w_in = D_MODEL ** -0.5 * jax.random.normal(ks[5], (DEPTH, D_MODEL, D_IN), f32)
    v_norm_g = 1.0 + 0.02 * jax.random.normal(ks[6], (DEPTH, A_WIDTH), f32)
    v_norm_b = 0.02 * jax.random.normal(ks[7], (DEPTH, A_WIDTH), f32)
    w_spatial = CHUNK ** -0.5 * jax.random.normal(ks[8], (DEPTH, A_GROUPS, CHUNK, CHUNK), f32)
    b_spatial = 1.0 + 0.02 * jax.random.normal(ks[9], (DEPTH, A_GROUPS, CHUNK), f32)
    xavier = math.sqrt(2.0 / (D_MIX + D_MODEL))
    w_out = BETA * xavier * jax.random.normal(ks[10], (DEPTH, D_MIX, D_MODEL), f32)
    ln_g = 1.0 + 0.02 * jax.random.normal(ks[11], (DEPTH, D_MODEL), f32)
    ln_b = 0.02 * jax.random.normal(ks[12], (DEPTH, D_MODEL), f32)
    return {"x": x, "c": c, "positions": positions, "w_ada": w_ada, "b_ada": b_ada,
            "w_in": w_in, "v_norm_g": v_norm_g, "v_norm_b": v_norm_b,
            "w_spatial": w_spatial, "b_spatial": b_spatial, "w_out": w_out,
            "ln_g": ln_g, "ln_b": ln_b}


def reference(x, c, positions, w_ada, b_ada, w_in, v_norm_g, v_norm_b,
              w_spatial, b_spatial, w_out, ln_g, ln_b):
    cos, sin = rope_tables(positions)
    cond = jax.nn.silu(c)
    for l in range(DEPTH):
        x = hybrid_layer(x, cond, cos, sin, w_ada[l], b_ada[l], w_in[l], v_norm_g[l], v_norm_b[l],
                         w_spatial[l], b_spatial[l], w_out[l], ln_g[l], ln_b[l])
    return x
```

```python
import numpy as np
from contextlib import ExitStack
import concourse.bass as bass
import concourse.mybir as mybir
from concourse.bass_utils import run_bass_kernel_spmd

F32 = mybir.dt.float32
BF16 = mybir.dt.bfloat16
I32 = mybir.dt.int32
ALU = mybir.AluOpType
AF = mybir.ActivationFunctionType
AX = mybir.AxisListType

D = 1024
DIN = 3272
NL = 2
NCORES = 8
LN_EPS = 1e-5
ALPHA = (2.0 * NL) ** 0.25
KITER = 20
import os as _os
_SEQ = True
_MODE = int(_os.environ.get('KERNEL_MODE', '0'))
NEG = -1.0e30

C_U, C_V, C_ZA, C_Q, C_QI, C_ZB, C_K, C_KI, C_VAL, C_W = 0, 512, 1024, 1536, 2048, 2560, 3072, 3136, 3200, 3264
PERM = [(0, 0, 2048), (2688, C_QI, 512), (2176, C_ZB, 512), (2048, C_K, 64),
        (3200, C_KI, 64), (2112, C_VAL, 64), (3264, C_W, 8)]


class Eng:
    def __init__(self, name, eng, sem):
        self.name, self.eng, self.sem = name, eng, sem
        self.cnt = 0
        self.waited = {}
        self.real = 0
        self.rmap = {}


class Buf:
    __slots__ = ("name", "w", "r")

    def __init__(self, name):
        self.name = name
        self.w = None
        self.r = {}


class Sch:
    def __init__(self, nc, es):
        self.nc = nc
        self.es = es
        self.engs = {}
        for name, eng in (("pe", nc.tensor), ("act", nc.scalar), ("dve", nc.vector),
                          ("pool", nc.gpsimd), ("sp", nc.sync)):
            self.engs[name] = Eng(name, eng, es.enter_context(nc.semaphore("s_" + name)))
        self.dq = {}
        self.dry = False
        self.mode = "plan"
        self.needed = set()
        self.B = {}

    def reset(self, mode):
        self.mode = mode
        self.B = {}
        for E in list(self.engs.values()) + list(self.dq.values()):
            E.cnt = 0
            E.waited = {}
            E.real = 0
            E.rmap = {}

    def buf(self, name):
        if name not in self.B:
            self.B[name] = Buf(name)
        return self.B[name]

    def dmaq(self, name):
        if name not in self.dq:
            q = Eng(name, None, self.es.enter_context(self.nc.semaphore("q_" + name)))
            q.real = 0
            q.rmap = {}
            self.dq[name] = q
        return self.dq[name]

    def _need(self, E, deps, dep):
        e, c = dep
        if deps.get(e.name, (None, 0))[1] < c:
            deps[e.name] = (e, c)

    def _deps(self, E, R, W):
        deps = {}
        for b in R:
            if b.w is not None:
                self._need(E, deps, b.w)
        for b in W:
            if b.w is not None and b.w[0] is not E:
                self._need(E, deps, b.w)
            for (e, c) in b.r.values():
                if e is not E:
                    self._need(E, deps, (e, c))
        return deps

    def _emit_waits(self, E, deps):
        for (e, c) in deps.values():
            if E.waited.get(e.name, 0) >= c:
                continue
            if self.mode == "plan":
                self.needed.add((e.name, c))
            else:
                E.eng.wait_ge(e.sem, e.rmap[c])
            E.waited[e.name] = c

    def wait_all(self, ename, qname):
        E = self.engs[ename]
        Q = self.dmaq(qname)
        if Q.cnt > 0:
            self._emit_waits(E, {Q.name: (Q, Q.cnt)})

    def op(self, ename, fn, R=(), W=()):
        if self.dry:
            return
        E = self.engs[ename]
        self._emit_waits(E, self._deps(E, R, W))
        E.cnt += 1
        if self.mode == "emit":
            inst = fn(E.eng)
            if (E.name, E.cnt) in self.needed:
                E.real += 1
                E.rmap[E.cnt] = E.real
                inst.then_inc(E.sem, 1)
        for b in R:
            b.r[E.name] = (E, E.cnt)
        for b in W:
            b.w = (E, E.cnt)
            b.r = {}

    def dma(self, issuer, qname, out, in_, R=(), W=()):
        if self.dry:
            return
        E = self.engs[issuer]
        Q = self.dmaq(qname)
        deps = self._deps(Q, R, W)
        if Q.cnt > 0:
            deps[Q.name] = (Q, Q.cnt)
        self._emit_waits(E, deps)
        Q.cnt += 1
        if self.mode == "emit":
            Q.real += 16
            Q.rmap[Q.cnt] = Q.real
            E.eng.dma_start(out=out, in_=in_).then_inc(Q.sem, 16)
        for b in R:
            b.r[Q.name] = (Q, Q.cnt)
        for b in W:
            b.w = (Q, Q.cnt)
            b.r = {}


def build(NB, dbg=False):
    S = NB * 128
    TOP = min(256, S // 4)
    nc = bass.Bass("TRN2", target_bir_lowering=False)

    def din(name, shape, dt=F32):
        return nc.dram_tensor(name, list(shape), dt, kind="ExternalInput").ap()

    x_d = din("x", [S, D])
    c_d = din("c", [8, 128])
    pos_d = din("pos", [NB, 128], I32)
    wada_d = din("w_ada", [NL, D, 3 * D])
    bada_d = din("b_ada", [NL, 24, 128])
    win_d = din("w_in", [NL, D, DIN])
    vg_d = din("v_norm_g", [NL, 512])
    vb_d = din("v_norm_b", [NL, 512])
    wsp_d = din("w_spatial", [NL, 8, 128, 128])
    bsp_d = din("b_spatial", [NL, 8, 128])
    wout_d = din("w_out", [NL, D, D])
    lng_d = din("ln_g", [NL, D])
    lnb_d = din("ln_b", [NL, D])
    ident_d = din("ident", [128, 128])
    invf_d = din("invf", [128, 32])
    pow2_d = din("pow2", [128, KITER + 1])
    out_d = nc.dram_tensor("out", [S, D], F32, kind="ExternalOutput").ap()
    x1_d = nc.dram_tensor("x1_scratch", [S, D], F32, kind="Internal").ap()

    es = ExitStack()
    with es:
        sch = Sch(nc, es)
        op, dma = sch.op, sch.dma

        def sb(name, shape, dt=F32):
            return es.enter_context(nc.sbuf_tensor("sb_" + name, list(shape), dt))

        w_in_bf = sb("w_in_bf", [128, 8, DIN], BF16)
        w_out_bf = sb("w_out_bf", [128, 8, D], BF16)
        rows = sb("rows", [33, DIN], BF16)
        ones33 = sb("ones33", [33, 128], BF16)
        KK = sb("KK", [128, S], BF16)
        Vst = sb("Vst", [128, NB, 65], BF16)
        WT = sb("WT", [128, 8, 128], BF16)
        bsT = sb("bsT", [128, 8], F32)
        cosT = sb("cosT", [128, NB, 32], F32)
        sinT = sb("sinT", [128, NB, 32], F32)
        vg_bc = sb("vg_bc", [128, 512], F32)
        vb_bc = sb("vb_bc", [128, 512], F32)
        lng_bc = sb("lng_bc", [128, D], F32)
        lnb_bc = sb("lnb_bc", [128, D], F32)
        ident_f = sb("ident_f", [128, 128], F32)
        ident_b = sb("ident_b", [128, 128], BF16)
        pow2 = sb("pow2", [128, KITER + 1], F32)
        condT2 = sb("condT2", [128, 8, 2], F32)
        SW = max(S, DIN)
        io = sb("io", [128, 4096], F32)
        xblk = [io[:, 0:1024], io[:, 1024:2048]]
        res = io[:, 2048:3072]
        rn = io[:, 3072:4096]
        IO = ["xblk0", "xblk1", "res", "rn"]
        ew = sb("ew", [128, 6 * 512], F32)
        u_sb, vn0, sza, szb0, szb1, t1 = [ew[:, k * 512:(k + 1) * 512] for k in range(6)]
        szb = [szb0, szb1]
        EW = ["u_sb", "vn0", "sza", "szb0", "szb1", "t1"]
        att = sb("att", [128, 3072], BF16)
        MT = [att[:, 0:512].rearrange("p (j t) -> p j t", j=4), att[:, 512:1024].rearrange("p (j t) -> p j t", j=4)]
        Ee = [att[:, 1024:2048].rearrange("p (h t) -> p h t", h=8), att[:, 2048:3072].rearrange("p (h t) -> p h t", h=8)]
        Mbuf = [sb("Mbuf%d" % k, [128, SW], BF16) for k in range(2)]
        ATT = ["Mbuf0"]
        score = sb("score", [128, SW], F32)
        xnT = sb("xnT", [128, D], BF16)
        st12 = sb("st12", [128, 12], F32)
        mv = sb("mv", [128, 2], F32)
        rstd = sb("rstd", [128, 1], F32)
        vn_bf = sb("vn_bf", [128, 512], BF16)
        ra = sb("ra", [128, 8, 32], F32)
        rb = sb("rb", [128, 8, 32], F32)
        rc = sb("rc", [128, 8, 32], F32)
        rd = sb("rd", [128, 8, 32], F32)
        Z = sb("Z", [128, 8, 128], BF16)
        xn = Z[:].rearrange("p h d -> p (h d)")
        KZ = sb("KZ", [128, 128], BF16)
        QQ = [sb("QQ%d" % k, [128, 8, 128], BF16) for k in range(2)]
        w8 = sb("w8", [128, 8], F32)
        diagW = sb("diagW", [128, 8, 128], BF16)
        Rr = [sb("Rr%d" % i, [128, 512], BF16) for i in range(4)]
        st12C = sb("st12C", [128, 12], F32)
        mvC = sb("mvC", [128, 2], F32)
        rstdC = sb("rstdC", [128, 1], F32)
        cmax = sb("cmax", [128, 8], F32)
        cmin = sb("cmin", [128, 8], F32)
        rmax = sb("rmax", [128, 1], F32)
        rmin = sb("rmin", [128, 1], F32)
        w0 = sb("w0", [128, 1], F32)
        HWt = sb("HWt", [128, KITER + 1], F32)
        mid = [sb("mid%d" % i, [128, 1], F32) for i in range(2)]
        cnt = sb("cnt", [128, 1], F32)
        stp = sb("stp", [128, 1], F32)
        tau = sb("tau", [128, 1], F32)
        rden = sb("rden", [128, 8], F32)
        yb0 = res[:, 0:512].rearrange("p (h d) -> p h d", h=8)
        y_bf = [sb("y_bf%d" % k, [128, D], BF16) for k in range(2)]
        yT = sb("yT", [128, D], BF16)
        stage = [score[:, 0:DIN], io[:, 0:DIN]]
        STG = [["score"], IO]
        gate_bc = ew[:, 0:1024]
        bgate_bc = ew[:, 1024:2048]
        wsp_f = ew[:, 2048:3072].rearrange("p (g s) -> p g s", g=8)
        wsp_b = Z
        c0src = score[32:33, 0:DIN]
        c0dst = score[0:1, 0:DIN]
        c0hi = Mbuf[0][0:1, 0:DIN]
        ang = ew[:, 0:NB * 32].rearrange("p (n f) -> p n f", f=32)
        ang2 = ew[:, 1024:1024 + NB * 32].rearrange("p (n f) -> p n f", f=32)
        small = sb("small", [32, 128], F32)
        modT = sb("modT", [128, 16], F32)
        badaT = sb("badaT", [128, 24], F32)
        shiftT2 = sb("shiftT2", [128, 8, 2], F32)
        scaleT1 = sb("scaleT1", [128, 8], F32)
        posf = sb("posf", [128, NB], F32)
        posi = sb("posi", [32, 128], I32)
        posr = sb("posr", [32, 128], F32)
        invf = sb("invf", [128, 32], F32)
        negpi = sb("negpi", [128, 1], F32)

        PB = [es.enter_context(nc.psum_tensor("PB%d" % i, [128, 512], F32)) for i in range(7)]
        PT = es.enter_context(nc.psum_tensor("PT", [128, 1024], BF16))

        def b(name):
            return sch.buf(name)

        def bl(names):
            return [b(nm) for nm in names]

        condrep = ew[:, 2048:3072].rearrange("p (k m) -> p k m", k=8)

        neg_reg = nc.gpsimd.to_reg(NEG)
        zero_reg = nc.gpsimd.to_reg(0.0)
        for _pass in ("plan", "emit"):
            sch.reset(_pass)
            dma("sp", "ld0", ident_f[:], ident_d[:, :], W=[b("ident_f")])
            dma("sp", "ld1", invf[:], invf_d[:, :], W=[b("invf")])
            dma("sp", "ld0", pow2[:], pow2_d[:, :], W=[b("pow2")])
            dma("sp", "ld1", small[0:8, :], c_d[:, :], W=[b("small")])
            dma("sp", "ld0", posi[0:NB, :], pos_d[:, :], W=[b("posi")])
            op("dve", lambda e: e.tensor_copy(out=ident_b[:], in_=ident_f[:]), R=[b("ident_f")], W=[b("ident_b")])
            op("pool", lambda e: e.memset(ones33[:], 1.0), W=[b("ones33")])
            op("pool", lambda e: e.memset(rows[:], 0.0), W=[b("rows")])
            op("pool", lambda e: e.memset(Vst[:], 1.0), W=[b("Vst")])

            op("act", lambda e: e.activation(out=small[0:8, :], in_=small[0:8, :], func=AF.Silu),
               R=[b("small")], W=[b("small")])
            op("pe", lambda e: e.transpose(PB[0][:, 0:8], small[0:8, :], ident_f[0:8, 0:8]),
               R=[b("small"), b("ident_f")], W=[b("PB0")])
            for j in range(2):
                op("dve", lambda e, j=j: e.tensor_copy(out=condT2[:, :, j], in_=PB[0][:, 0:8]),
                   R=[b("PB0")], W=[b("condT2")])

            op("dve", lambda e: e.tensor_copy(out=posr[0:NB, :], in_=posi[0:NB, :]), R=[b("posi")], W=[b("posr")])
            op("pe", lambda e: e.transpose(PB[1][:, 0:NB], posr[0:NB, :], ident_f[0:NB, 0:NB]),
               R=[b("posr"), b("ident_f")], W=[b("PB1")])
            op("dve", lambda e: e.tensor_copy(out=posf[:], in_=PB[1][:, 0:NB]), R=[b("PB1")], W=[b("posf")])
            op("dve", lambda e: e.tensor_tensor(out=ang[:], in0=posf[:, :, None].to_broadcast([128, NB, 32]),
                                                in1=invf[:, None, :].to_broadcast([128, NB, 32]), op=ALU.mult),
               R=[b("posf"), b("invf")], W=bl(EW))
            TWO_PI = float(2.0 * np.pi)
            tmpf = ew[:, 2048:2048 + NB * 32].rearrange("p (n f) -> p n f", f=32)
            angi = Mbuf[0][:, 0:2 * NB * 32].bitcast(I32).rearrange("p (n f) -> p n f", f=32)
            op("dve", lambda e: e.tensor_scalar(out=ang2[:], in0=ang[:], scalar1=float(np.pi / 2), scalar2=None, op0=ALU.add),
               R=bl(EW), W=bl(EW))

            def reduce_angle(a_):
                op("dve", lambda e: e.tensor_scalar(out=tmpf, in0=a_, scalar1=float(1.0 / TWO_PI), scalar2=None, op0=ALU.mult),
                   R=bl(EW), W=bl(EW))
                op("dve", lambda e: e.tensor_copy(out=angi, in_=tmpf), R=bl(EW), W=bl(ATT))
                op("dve", lambda e: e.tensor_copy(out=tmpf, in_=angi), R=bl(ATT), W=bl(EW))
                op("dve", lambda e: e.scalar_tensor_tensor(out=a_, in0=tmpf, scalar=-TWO_PI, in1=a_, op0=ALU.mult, op1=ALU.add),
                   R=bl(EW), W=bl(EW))
                op("dve", lambda e: e.tensor_scalar(out=tmpf, in0=a_, scalar1=3.14159, scalar2=-TWO_PI, op0=ALU.is_gt, op1=ALU.mult),
                   R=bl(EW), W=bl(EW))
                op("dve", lambda e: e.tensor_tensor(out=a_, in0=a_, in1=tmpf, op=ALU.add), R=bl(EW), W=bl(EW))
                op("dve", lambda e: e.tensor_scalar(out=tmpf, in0=a_, scalar1=-3.14159, scalar2=TWO_PI, op0=ALU.is_lt, op1=ALU.mult),
                   R=bl(EW), W=bl(EW))
                op("dve", lambda e: e.tensor_tensor(out=a_, in0=a_, in1=tmpf, op=ALU.add), R=bl(EW), W=bl(EW))

            reduce_angle(ang)
            reduce_angle(ang2)
            op("act", lambda e: e.activation(out=sinT[:], in_=ang, func=AF.Sin), R=bl(EW), W=[b("sinT")])
            op("act", lambda e: e.activation(out=cosT[:], in_=ang2, func=AF.Sin), R=bl(EW), W=[b("cosT")])

            def layer_setup(l):
                dma("sp", "ld0", small[0:24, :], bada_d[l, :, :], W=[b("small")])
                op("pe", lambda e: e.transpose(PB[0][:, 0:24], small[0:24, :], ident_f[0:24, 0:24]),
                   R=[b("small"), b("ident_f")], W=[b("PB0")])
                op("dve", lambda e: e.tensor_copy(out=badaT[:], in_=PB[0][:, 0:24]), R=[b("PB0")], W=[b("badaT")])
                bg = bada_d[l, 16:24, :].rearrange("a b -> (a b)").unsqueeze(0).to_broadcast([128, D])
                op("dve", lambda e: e.tensor_copy(out=condrep, in_=condT2[:, :, 0:1].to_broadcast([128, 8, 128])),
                   R=[b("condT2")], W=bl(EW))
                dma("sp", "ld1", bgate_bc, bg, W=bl(EW))
                dma("sp", "ld0", vg_bc[:], vg_d[l:l + 1, :].to_broadcast([128, 512]), W=[b("vg_bc")])
                dma("sp", "ld1", vb_bc[:], vb_d[l:l + 1, :].to_broadcast([128, 512]), W=[b("vb_bc")])
                dma("sp", "ld0", lng_bc[:], lng_d[l:l + 1, :].to_broadcast([128, D]), W=[b("lng_bc")])
                dma("sp", "ld1", lnb_bc[:], lnb_d[l:l + 1, :].to_broadcast([128, D]), W=[b("lnb_bc")])
                for kc in range(8):
                    stg = stage[kc % 2]
                    SN = bl(STG[kc % 2])
                    dma("sp", "st%d" % (kc % 2), stg[:, 0:3 * D], wada_d[l, kc * 128:(kc + 1) * 128, :], W=SN)
                    for j in range(16):
                        op("pe", lambda e, j=j, kc=kc, stg=stg: e.matmul(
                            PB[0][:, 2 * j:2 * j + 2], lhsT=stg[:, j * 128:(j + 1) * 128], rhs=condT2[:, kc, :],
                            start=(kc == 0 and j == 0), stop=(kc == 7), skip_group_check=True),
                           R=SN + [b("condT2")], W=[b("PB0")])
                    for n2 in range(2):
                        op("pe", lambda e, n2=n2, kc=kc, stg=stg: e.matmul(
                            PB[1 + n2][:, :], lhsT=condrep[:, kc, :], rhs=stg[:, 2048 + n2 * 512:2048 + (n2 + 1) * 512],
                            start=(kc == 0), stop=(kc == 7)),
                           R=SN + bl(EW), W=[b("PB%d" % (1 + n2))])
                op("dve", lambda e: e.tensor_copy(out=modT[:], in_=PB[0][:, 0:32].rearrange("p (j t) -> p j t", t=2)[:, :, 0]),
                   R=[b("PB0")], W=[b("modT")])
                for j in range(2):
                    op("dve", lambda e, j=j: e.tensor_tensor(out=shiftT2[:, :, j], in0=modT[:, 0:8], in1=badaT[:, 0:8], op=ALU.add),
                       R=[b("modT"), b("badaT")], W=[b("shiftT2")])
                op("dve", lambda e: e.scalar_tensor_tensor(out=scaleT1[:], in0=modT[:, 8:16], scalar=1.0, in1=badaT[:, 8:16],
                                                           op0=ALU.add, op1=ALU.add),
                   R=[b("modT"), b("badaT")], W=[b("scaleT1")])
                for n2 in range(2):
                    op("dve", lambda e, n2=n2: e.tensor_tensor(out=gate_bc[:, n2 * 512:(n2 + 1) * 512], in0=PB[1 + n2][:, :],
                                                               in1=bgate_bc[:, n2 * 512:(n2 + 1) * 512], op=ALU.add),
                       R=[b("PB%d" % (1 + n2))] + bl(EW), W=bl(EW))
                for kc in range(8):
                    stg = stage[kc % 2]
                    SN = bl(STG[kc % 2])
                    dma("sp", "st%d" % (kc % 2), stg[:, 0:D], wout_d[l, kc * 128:(kc + 1) * 128, :], W=SN)
                    op(("dve", "pool")[kc % 2], lambda e, kc=kc, stg=stg: e.tensor_tensor(
                        out=w_out_bf[:, kc, :], in0=stg[:, 0:D], in1=gate_bc[:], op=ALU.mult),
                       R=SN + bl(EW), W=[b("w_out_bf")])
                ntl = [(i * 512, min(512, DIN - i * 512)) for i in range(7)]
                for kc in range(8):
                    stg = stage[kc % 2]
                    SN = bl(STG[kc % 2])
                    dma("sp", "st%d" % (kc % 2), stg[:, :], win_d[l, kc * 128:(kc + 1) * 128, :], W=SN)
                    for nt, (cs, cw) in enumerate(ntl):
                        op("pe", lambda e, nt=nt, cs=cs, cw=cw, kc=kc, stg=stg: e.matmul(
                            PB[nt][0:2, 0:cw], lhsT=shiftT2[:, kc, :], rhs=stg[:, cs:cs + cw],
                            start=(kc == 0), stop=(kc == 7)),
                           R=SN + [b("shiftT2")], W=[b("PB%d" % nt)])
                    for pi, (s0, d0, wd) in enumerate(PERM):
                        en = ("dve", "pool", "act")[pi % 3] if wd >= 512 else "pool"
                        if en == "act":
                            op("act", lambda e, s0=s0, d0=d0, wd=wd, kc=kc, stg=stg: e.activation(
                                out=w_in_bf[:, kc, d0:d0 + wd], in_=stg[:, s0:s0 + wd], func=AF.Copy, scale=scaleT1[:, kc:kc + 1]),
                               R=SN + [b("scaleT1")], W=[b("w_in_bf")])
                        else:
                            op(en, lambda e, s0=s0, d0=d0, wd=wd, kc=kc, stg=stg: e.tensor_scalar(
                                out=w_in_bf[:, kc, d0:d0 + wd], in0=stg[:, s0:s0 + wd], scalar1=scaleT1[:, kc:kc + 1],
                                scalar2=None, op0=ALU.mult),
                               R=SN + [b("scaleT1")], W=[b("w_in_bf")])
                for nt, (cs, cw) in enumerate(ntl):
                    op("dve", lambda e, nt=nt, cs=cs, cw=cw: e.tensor_copy(out=c0src[:, cs:cs + cw], in_=PB[nt][0:1, 0:cw]),
                       R=[b("PB%d" % nt)], W=[b("score")])
                for (s0, d0, wd) in PERM:
                    op("dve", lambda e, s0=s0, d0=d0, wd=wd: e.tensor_copy(out=c0dst[0:1, d0:d0 + wd], in_=c0src[:, s0:s0 + wd]),
                       R=[b("score")], W=[b("score")])
                op("dve", lambda e: e.tensor_copy(out=c0hi[0:1, :], in_=c0dst[0:1, :]), R=[b("score")], W=bl(ATT))
                op("dve", lambda e: e.tensor_copy(out=rows[0:1, :], in_=c0hi[0:1, :]), R=bl(ATT), W=[b("rows")])
                op("dve", lambda e: e.tensor_tensor(out=c0src[:, :], in0=c0dst[0:1, :], in1=c0hi[0:1, :], op=ALU.subtract),
                   R=[b("score")] + bl(ATT), W=[b("score")])
                op("dve", lambda e: e.tensor_copy(out=rows[32:33, :], in_=c0src[:, :]), R=[b("score")], W=[b("rows")])
                dma("sp", "ld0", wsp_f[:], wsp_d[l].rearrange("g t s -> t g s"), W=bl(EW))
                op("pool", lambda e: e.affine_select(out=wsp_f[:], in_=wsp_f[:], pattern=[[0, 8], [-1, 128]],
                                                     compare_op=ALU.is_ge, fill=zero_reg, base=0, channel_multiplier=1),
                   R=bl(EW), W=bl(EW))
                op("pool", lambda e: e.tensor_copy(out=wsp_b[:], in_=wsp_f[:]), R=bl(EW), W=[b("Z")])
                for g in range(8):
                    op("pe", lambda e, g=g: e.transpose(PT[:, g * 128:(g + 1) * 128], wsp_b[:, g, :], ident_b[:]),
                       R=[b("Z"), b("ident_b")], W=[b("PT")])
                op("dve", lambda e: e.tensor_copy(out=WT[:], in_=PT[:, :].rearrange("p (g t) -> p g t", g=8)),
                   R=[b("PT")], W=[b("WT")])
                dma("sp", "ld1", small[0:8, :], bsp_d[l, :, :], W=[b("small")])
                op("pe", lambda e: e.transpose(PB[0][:, 0:8], small[0:8, :], ident_f[0:8, 0:8]),
                   R=[b("small"), b("ident_f")], W=[b("PB0")])
                op("dve", lambda e: e.tensor_copy(out=bsT[:], in_=PB[0][:, 0:8]), R=[b("PB0")], W=[b("bsT")])

            def rope(src4, dst_lo, dst_hi, nh, blk):
                cb = cosT[:, blk:blk + 1, :].to_broadcast([128, nh, 32])
                sbb = sinT[:, blk:blk + 1, :].to_broadcast([128, nh, 32])
                x1, x2 = src4[:, :, 0, :], src4[:, :, 1, :]
                return cb, sbb, x1, x2

            def ln_stats(src_ap_halves, srcbufs, C=False):
                st_, mv_, rs_ = (st12C, mvC, rstdC) if C else (st12, mv, rstd)
                sn, mn, rsn = ("st12C", "mvC", "rstdC") if C else ("st12", "mv", "rstd")
                for hh, ap in enumerate(src_ap_halves):
                    op("dve", lambda e, hh=hh, ap=ap: e.bn_stats(out=st_[:, hh * 6:(hh + 1) * 6], in_=ap),
                       R=srcbufs, W=[b(sn)])
                nst = 6 * len(src_ap_halves)
                op("dve", lambda e: e.bn_aggr(out=mv_[:], in_=st_[:, 0:nst]), R=[b(sn)], W=[b(mn)])
                op("dve", lambda e: e.tensor_scalar(out=rs_[:], in0=mv_[:, 1:2], scalar1=LN_EPS, scalar2=None, op0=ALU.add),
                   R=[b(mn)], W=[b(rsn)])
                op("act", lambda e: e.activation(out=rs_[:], in_=rs_[:], func=AF.Sqrt), R=[b(rsn)], W=[b(rsn)])
                op("dve", lambda e: e.reciprocal(out=rs_[:], in_=rs_[:]), R=[b(rsn)], W=[b(rsn)])

            def kkn(blks):
                return [b("KK%d" % j) for j in blks]

            def stageAB(l, i, src_d):
                n = (i + 1) * 128
                p2 = i % 2
                xb, xbn = xblk[p2], "xblk%d" % p2
                qq, qqn = QQ[p2], "QQ%d" % p2
                szb_, szbn = szb[p2], "szb%d" % p2
                yb, ybn = y_bf[p2], "y_bf%d" % p2
                Mb, Mbn = Mbuf[p2], "Mbuf%d" % p2
                dma("sp", "xq%d" % p2, xb, src_d[i * 128:(i + 1) * 128, :], W=[b(xbn)])
                ln_stats([xb[:, 0:512], xb[:, 512:1024]], [b(xbn)])
                op("dve", lambda e: e.tensor_scalar(out=xn, in0=xb, scalar1=mv[:, 0:1], scalar2=rstd[:],
                                                    op0=ALU.subtract, op1=ALU.mult),
                   R=[b(xbn), b("mv"), b("rstd")], W=[b("Z")])
                for kc in range(8):
                    op("pe", lambda e, kc=kc: e.transpose(PT[:, kc * 128:(kc + 1) * 128], xn[:, kc * 128:(kc + 1) * 128], ident_b[:]),
                       R=[b("Z"), b("ident_b")], W=[b("PT")])
                op("act", lambda e: e.copy(out=xnT[:], in_=PT[:, :]), R=[b("PT")], W=[b("xnT")])
                yield 4.0

                def proj_tile(nt, cs, cw):
                    pb = PB[nt % 2]
                    pbn = "PB%d" % (nt % 2)
                    for kc in range(8):
                        op("pe", lambda e, kc=kc: e.matmul(pb[:, 0:cw], lhsT=xnT[:, kc * 128:(kc + 1) * 128],
                                                           rhs=w_in_bf[:, kc, cs:cs + cw], start=(kc == 0), stop=False),
                           R=[b("xnT"), b("w_in_bf")], W=[b(pbn)])
                    op("pe", lambda e: e.matmul(pb[:, 0:cw], lhsT=ones33[:, :], rhs=rows[:, cs:cs + cw], start=False, stop=True),
                       R=[b("ones33"), b("rows")], W=[b(pbn)])
                    return pb, pbn

                pb, pbn = proj_tile(0, C_U, 512)
                op("act", lambda e: e.copy(out=u_sb, in_=pb[:, :]), R=[b(pbn)], W=[b("u_sb")])
                yield 3.0
                pb, pbn = proj_tile(1, C_V, 512)
                ln_stats([pb[:, 0:512]], [b(pbn)])
                op("dve", lambda e: e.tensor_scalar(out=vn0, in0=pb[:, :], scalar1=mv[:, 0:1], scalar2=rstd[:],
                                                    op0=ALU.subtract, op1=ALU.mult),
                   R=[b(pbn), b("mv"), b("rstd")], W=[b("vn0")])
                op("pool", lambda e: e.tensor_tensor(out=vn0, in0=vn0, in1=vg_bc[:], op=ALU.mult),
                   R=[b("vn0"), b("vg_bc")], W=[b("vn0")])
                op("pool", lambda e: e.tensor_tensor(out=vn_bf[:], in0=vn0, in1=vb_bc[:], op=ALU.add),
                   R=[b("vn0"), b("vb_bc")], W=[b("vn_bf")])
                yield 4.0
                pb, pbn = proj_tile(2, C_ZA, 512)
                op("act", lambda e: e.activation(out=sza, in_=pb[:, :], func=AF.Silu), R=[b(pbn)], W=[b("sza")])
                for g in range(8):
                    op("pe", lambda e, g=g: e.matmul(PB[2][:, g * 64:(g + 1) * 64], lhsT=WT[:, g, :], rhs=vn_bf[:, g * 64:(g + 1) * 64],
                                                     start=True, stop=True),
                       R=[b("WT"), b("vn_bf")], W=[b("PB2")])
                op("dve", lambda e: e.tensor_tensor(out=t1.rearrange("p (g d) -> p g d", g=8),
                                                    in0=PB[2][:, :].rearrange("p (g d) -> p g d", g=8),
                                                    in1=bsT[:, :, None].to_broadcast([128, 8, 64]), op=ALU.add),
                   R=[b("PB2"), b("bsT")], W=[b("t1")])
                op("pool", lambda e: e.tensor_tensor(out=t1, in0=t1, in1=u_sb, op=ALU.mult),
                   R=[b("t1"), b("u_sb")], W=[b("t1")])
                op("pool", lambda e: e.tensor_tensor(out=yb[:, 0:512], in0=t1, in1=sza, op=ALU.mult),
                   R=[b("t1"), b("sza")], W=[b(ybn)])
                yield 4.0
                for (nt, cs, off) in ((3, C_Q, 0), (4, C_QI, 64)):
                    pb, pbn = proj_tile(nt, cs, 512)
                    s4 = pb[:, :].rearrange("p (h a d) -> p h a d", h=8, a=2)
                    cb, sbb, x1, x2 = rope(s4, None, None, 8, i)
                    op("dve", lambda e: e.tensor_tensor(out=ra[:], in0=x1, in1=cb, op=ALU.mult), R=[b(pbn), b("cosT")], W=[b("ra")])
                    op("dve", lambda e: e.tensor_tensor(out=rb[:], in0=x2, in1=sbb, op=ALU.mult), R=[b(pbn), b("sinT")], W=[b("rb")])
                    op("dve", lambda e: e.tensor_tensor(out=rc[:], in0=x2, in1=cb, op=ALU.mult), R=[b(pbn), b("cosT")], W=[b("rc")])
                    op("dve", lambda e: e.tensor_tensor(out=rd[:], in0=x1, in1=sbb, op=ALU.mult), R=[b(pbn), b("sinT")], W=[b("rd")])
                    op("pool", lambda e, off=off: e.tensor_tensor(out=Z[:, :, off:off + 32], in0=ra[:], in1=rb[:], op=ALU.subtract),
                       R=[b("ra"), b("rb")], W=[b("Z")])
                    op("pool", lambda e, off=off: e.tensor_tensor(out=Z[:, :, off + 32:off + 64], in0=rc[:], in1=rd[:], op=ALU.add),
                       R=[b("rc"), b("rd")], W=[b("Z")])
                    yield 4.0
                pb, pbn = proj_tile(5, C_ZB, 512)
                op("act", lambda e: e.activation(out=szb_, in_=pb[:, :], func=AF.Silu), R=[b(pbn)], W=[b(szbn)])
                yield 3.0
                pb, pbn = proj_tile(6, C_K, 200)
                s4 = pb[:, 0:128].rearrange("p (h a d) -> p h a d", h=2, a=2)
                cb, sbb, x1, x2 = rope(s4, None, None, 2, i)
                op("dve", lambda e: e.tensor_tensor(out=ra[:, 0:2, :], in0=x1, in1=cb, op=ALU.mult), R=[b(pbn), b("cosT")], W=[b("ra")])
                op("dve", lambda e: e.tensor_tensor(out=rb[:, 0:2, :], in0=x2, in1=sbb, op=ALU.mult), R=[b(pbn), b("sinT")], W=[b("rb")])
                op("dve", lambda e: e.tensor_tensor(out=rc[:, 0:2, :], in0=x2, in1=cb, op=ALU.mult), R=[b(pbn), b("cosT")], W=[b("rc")])
                op("dve", lambda e: e.tensor_tensor(out=rd[:, 0:2, :], in0=x1, in1=sbb, op=ALU.mult), R=[b(pbn), b("sinT")], W=[b("rd")])
                kz3 = KZ[:, :].rearrange("p (h d) -> p h d", h=2)
                op("pool", lambda e: e.tensor_tensor(out=kz3[:, :, 0:32], in0=ra[:, 0:2, :], in1=rb[:, 0:2, :], op=ALU.subtract),
                   R=[b("ra"), b("rb")], W=[b("KZ")])
                op("pool", lambda e: e.tensor_tensor(out=kz3[:, :, 32:64], in0=rc[:, 0:2, :], in1=rd[:, 0:2, :], op=ALU.add),
                   R=[b("rc"), b("rd")], W=[b("KZ")])
                op("act", lambda e: e.copy(out=Vst[:, i, 0:64], in_=pb[:, 128:192]), R=[b(pbn)], W=[b("V%d" % i)])
                op("dve", lambda e: e.tensor_copy(out=w8[:], in_=pb[:, 192:200]), R=[b(pbn)], W=[b("w8")])
                op("dve", lambda e: e.tensor_tensor(out=diagW[:], in0=ident_b[:, None, :].to_broadcast([128, 8, 128]),
                                                    in1=w8[:, :, None].to_broadcast([128, 8, 128]), op=ALU.mult),
                   R=[b("ident_b"), b("w8")], W=[b("diagW")])
                op("pe", lambda e: e.transpose(PT[:, 0:128], KZ[:, :], ident_b[:]), R=[b("KZ"), b("ident_b")], W=[b("PT")])
                op("act", lambda e: e.copy(out=KK[:, i * 128:(i + 1) * 128], in_=PT[:, 0:128]), R=[b("PT")], W=[b("KK%d" % i)])
                for h in range(8):
                    op("pe", lambda e, h=h: e.transpose(PT[:, h * 128:(h + 1) * 128], Z[:, h, :], ident_b[:]),
                       R=[b("Z"), b("ident_b")], W=[b("PT")])
                op("act", lambda e: e.copy(out=qq[:], in_=PT[:, :].rearrange("p (h t) -> p h t", h=8)), R=[b("PT")], W=[b(qqn)])
                yield 5.0

                nch = (n + 511) // 512
                steps = [(c, h) for c in range(nch) for h in range(8)]

                def ix_mm1(k):
                    c, h = steps[k]
                    cw = min(512, n - c * 512)
                    pb, pbn = PB[k % 2], "PB%d" % (k % 2)
                    op("pe", lambda e: e.matmul(pb[:, 0:cw], lhsT=qq[64:128, h, :], rhs=KK[64:128, c * 512:c * 512 + cw],
                                                start=True, stop=True),
                       R=[b(qqn)] + kkn(range(4 * c, 4 * c + cw // 128)), W=[b(pbn)])

                def ix_rest(k):
                    c, h = steps[k]
                    cw = min(512, n - c * 512)
                    pb, pbn = PB[k % 2], "PB%d" % (k % 2)
                    rr, rrn = Rr[k % 4], "Rr%d" % (k % 4)
                    if h % 4 != 3:
                        op("act", lambda e: e.activation(out=rr[:, 0:cw], in_=pb[:, 0:cw], func=AF.Relu), R=[b(pbn)], W=[b(rrn)])
                    else:
                        op("dve", lambda e: e.tensor_scalar(out=rr[:, 0:cw], in0=pb[:, 0:cw], scalar1=0.0, scalar2=None, op0=ALU.max),
                           R=[b(pbn)], W=[b(rrn)])
                    op("pe", lambda e: e.matmul(PB[2][:, 0:cw], lhsT=diagW[:, h, :], rhs=rr[:, 0:cw], start=(h == 0), stop=(h == 7)),
                       R=[b("diagW"), b(rrn)], W=[b("PB2")])
                    if h == 7:
                        op("dve", lambda e: e.tensor_scalar(out=score[:, c * 512:c * 512 + cw], in0=PB[2][:, 0:cw], scalar1=1.0, scalar2=-3.0e38,
                                                            op0=ALU.mult, op1=ALU.max, accum_out=cmax[:, c:c + 1]),
                           R=[b("PB2")], W=[b("score"), b("cmax")])
                        op("dve", lambda e: e.tensor_scalar(out=Mb[:, c * 512:c * 512 + cw], in0=score[:, c * 512:c * 512 + cw], scalar1=1.0,
                                                            scalar2=3.0e38, op0=ALU.mult, op1=ALU.min, accum_out=cmin[:, c:c + 1]),
                           R=[b("score")], W=[b(Mbn), b("cmin")])

                ix_mm1(0)
                for k in range(len(steps)):
                    if k + 1 < len(steps):
                        ix_mm1(k + 1)
                    ix_rest(k)
                    yield 0.7
                op("dve", lambda e: e.tensor_reduce(out=rmax[:], in_=cmax[:, 0:nch], axis=AX.X, op=ALU.max), R=[b("cmax")], W=[b("rmax")])
                op("dve", lambda e: e.tensor_reduce(out=rmin[:], in_=cmin[:, 0:nch], axis=AX.X, op=ALU.min), R=[b("cmin")], W=[b("rmin")])
                op("dve", lambda e: e.tensor_tensor(out=w0[:], in0=rmax[:], in1=rmin[:], op=ALU.subtract),
                   R=[b("rmax"), b("rmin")], W=[b("w0")])
                op("dve", lambda e: e.tensor_scalar(out=w0[:], in0=w0[:], scalar1=1.001, scalar2=1.0e-6, op0=ALU.mult, op1=ALU.add),
                   R=[b("w0")], W=[b("w0")])
                op("dve", lambda e: e.tensor_scalar(out=HWt[:], in0=pow2[:], scalar1=w0[:], scalar2=None, op0=ALU.mult),
                   R=[b("pow2"), b("w0")], W=[b("HWt")])
                op("dve", lambda e: e.tensor_tensor(out=mid[0][:], in0=rmin[:], in1=HWt[:, 0:1], op=ALU.add),
                   R=[b("rmin"), b("HWt")], W=[b("mid0")])
                op("pool", lambda e: e.affine_select(out=score[:, i * 128:(i + 1) * 128], in_=score[:, i * 128:(i + 1) * 128],
                                                     pattern=[[-1, 128]], compare_op=ALU.is_ge, fill=neg_reg, base=0, channel_multiplier=1),
                   R=[b("score")], W=[b("score")])
                yield 1.5
                yield -1.0
                for k in range(KITER):
                    mc, mn = mid[k % 2], mid[(k + 1) % 2]
                    mcn, mnn = "mid%d" % (k % 2), "mid%d" % ((k + 1) % 2)
                    op("dve", lambda e: e.tensor_scalar(out=Mb[:, 0:n], in0=score[:, 0:n], scalar1=mc[:], scalar2=0.0,
                                                        op0=ALU.is_ge, op1=ALU.add, accum_out=cnt[:]),
                       R=[b("score"), b(mcn)], W=[b(Mbn), b("cnt")])
                    op("dve", lambda e: e.tensor_scalar(out=stp[:], in0=cnt[:], scalar1=float(TOP) - 0.5, scalar2=0.5,
                                                        op0=ALU.is_ge, op1=ALU.subtract), R=[b("cnt")], W=[b("stp")])
                    op("dve", lambda e: e.scalar_tensor_tensor(out=mn[:], in0=stp[:], scalar=HWt[:, k:k + 1], in1=mc[:],
                                                               op0=ALU.mult, op1=ALU.add),
                       R=[b("stp"), b("HWt"), b(mcn)], W=[b(mnn)])
                    yield 0.5 + n / 1900.0
                mfin, mfinn = mid[KITER % 2], "mid%d" % (KITER % 2)
                op("dve", lambda e: e.tensor_tensor(out=tau[:], in0=mfin[:], in1=HWt[:, KITER:KITER + 1], op=ALU.subtract),
                   R=[b(mfinn), b("HWt")], W=[b("tau")])
                op("dve", lambda e: e.tensor_scalar(out=Mb[:, 0:n], in0=score[:, 0:n], scalar1=tau[:], scalar2=None, op0=ALU.is_ge),
                   R=[b("score"), b("tau")], W=[b(Mbn)])
                yield 0.5 + n / 1900.0

            def stageC(l, i, dst_d):
                n = (i + 1) * 128
                p2 = i % 2
                xb, xbn = xblk[p2], "xblk%d" % p2
                qq, qqn = QQ[p2], "QQ%d" % p2
                szb_, szbn = szb[p2], "szb%d" % p2
                yb, ybn = y_bf[p2], "y_bf%d" % p2
                Mb, Mbn = Mbuf[p2], "Mbuf%d" % p2
                nch = (n + 511) // 512

                def at_prep(c):
                    cw = min(512, n - c * 512)
                    nb = cw // 128
                    mt, mtn = MT[c % 2], "MT%d" % (c % 2)
                    for jj in range(nb):
                        op("pe", lambda e, jj=jj: e.transpose(PT[:, jj * 128:(jj + 1) * 128],
                                                              Mb[:, c * 512 + jj * 128:c * 512 + (jj + 1) * 128], ident_b[:]),
                           R=[b(Mbn), b("ident_b")], W=[b("PT")])
                    op("act", lambda e: e.activation(out=mt[:, 0:nb, :], in_=PT[:, 0:cw].rearrange("p (j t) -> p j t", j=nb),
                                                     func=AF.Identity, scale=30000.0, bias=-30000.0),
                       R=[b("PT")], W=[b(mtn)])

                def at_st(j):
                    ee, een = Ee[j % 2], "Ee%d" % (j % 2)
                    mt, mtn = MT[(j // 4) % 2], "MT%d" % ((j // 4) % 2)
                    jj = j % 4
                    for hf in range(2):
                        op("pe", lambda e, hf=hf: e.matmul(PB[3 + hf][:, :], lhsT=KK[0:64, j * 128:(j + 1) * 128],
                                                           rhs=qq[0:64, 4 * hf:4 * hf + 4, :], start=True, stop=False),
                           R=[b("KK%d" % j), b(qqn)], W=[b("PB%d" % (3 + hf))])
                        op("pe", lambda e, hf=hf: e.matmul(PB[3 + hf][:, :], lhsT=ident_b[:, :],
                                                           rhs=mt[:, jj:jj + 1, :].to_broadcast([128, 4, 128]), start=False, stop=True),
                           R=[b("ident_b"), b(mtn)], W=[b("PB%d" % (3 + hf))])
                        op("act", lambda e, hf=hf: e.activation(out=ee[:, 4 * hf:4 * hf + 4, :],
                                                                in_=PB[3 + hf][:, :].rearrange("p (h t) -> p h t", h=4),
                                                                func=AF.Exp, scale=0.125),
                           R=[b("PB%d" % (3 + hf))], W=[b(een)])

                def at_pv(j):
                    c, jj = j // 4, j % 4
                    ee, een = Ee[j % 2], "Ee%d" % (j % 2)
                    mt, mtn = MT[c % 2], "MT%d" % (c % 2)
                    for h in range(8):
                        ob_ = PB[5 + h // 4]
                        op("pe", lambda e, h=h, ob_=ob_: e.matmul(
                            ob_[:, (h % 4) * 65:(h % 4) * 65 + 65], lhsT=ee[:, h, :], rhs=Vst[:, j, :],
                            start=(j == 0 and h % 4 == 0), stop=(j == i), skip_group_check=True),
                           R=[b(een), b("V%d" % j)], W=[b("PB%d" % (5 + h // 4))])

                at_prep(0)
                at_st(0)
                yield 1.5
                for j in range(i + 1):
                    if j % 4 == 0 and j // 4 + 1 < nch:
                        at_prep(j // 4 + 1)
                    if j + 1 <= i:
                        at_st(j + 1)
                    at_pv(j)
                    yield 2.0
                for hf in range(2):
                    o3 = PB[5 + hf][:, 0:260].rearrange("p (h d) -> p h d", h=4)
                    op("dve", lambda e, hf=hf, o3=o3: e.reciprocal(out=rden[:, 4 * hf:4 * hf + 4], in_=o3[:, :, 64]),
                       R=[b("PB%d" % (5 + hf))], W=[b("rden")])
                    op("dve", lambda e, hf=hf, o3=o3: e.tensor_tensor(out=yb0[:, 4 * hf:4 * hf + 4, :], in0=o3[:, :, 0:64],
                                                                      in1=rden[:, 4 * hf:4 * hf + 4, None].to_broadcast([128, 4, 64]),
                                                                      op=ALU.mult),
                       R=[b("PB%d" % (5 + hf)), b("rden")], W=[b("res")])
                op("dve", lambda e: e.tensor_tensor(out=yb[:, 512:1024], in0=res[:, 0:512], in1=szb_, op=ALU.mult),
                   R=[b("res"), b(szbn)], W=[b(ybn)])
                yield 2.0
                for kc in range(8):
                    op("pe", lambda e, kc=kc: e.transpose(PT[:, kc * 128:(kc + 1) * 128], yb[:, kc * 128:(kc + 1) * 128], ident_b[:]),
                       R=[b(ybn), b("ident_b")], W=[b("PT")])
                op("act", lambda e: e.copy(out=yT[:], in_=PT[:, :]), R=[b("PT")], W=[b("yT")])
                yield 2.0
                for nt in range(2):
                    for kc in range(8):
                        op("pe", lambda e, kc=kc, nt=nt: e.matmul(PB[3 + nt][:, :], lhsT=yT[:, kc * 128:(kc + 1) * 128],
                                                                  rhs=w_out_bf[:, kc, nt * 512:(nt + 1) * 512], start=(kc == 0), stop=(kc == 7)),
                           R=[b("yT"), b("w_out_bf")], W=[b("PB%d" % (3 + nt))])
                    op("dve", lambda e, nt=nt: e.scalar_tensor_tensor(out=res[:, nt * 512:(nt + 1) * 512], in0=xb[:, nt * 512:(nt + 1) * 512],
                                                                      scalar=ALPHA, in1=PB[3 + nt][:, :], op0=ALU.mult, op1=ALU.add),
                       R=[b(xbn), b("PB%d" % (3 + nt))], W=[b("res")])
                    yield 2.5
                ln_stats([res[:, 0:512], res[:, 512:1024]], [b("res")], C=True)
                op("dve", lambda e: e.tensor_scalar(out=rn, in0=res, scalar1=mvC[:, 0:1], scalar2=rstdC[:],
                                                    op0=ALU.subtract, op1=ALU.mult),
                   R=[b("res"), b("mvC"), b("rstdC")], W=[b("rn")])
                op("dve", lambda e: e.tensor_tensor(out=res, in0=rn, in1=lng_bc[:], op=ALU.mult),
                   R=[b("rn"), b("lng_bc")], W=[b("res")])
                op("dve", lambda e: e.tensor_tensor(out=rn, in0=res, in1=lnb_bc[:], op=ALU.add),
                   R=[b("res"), b("lnb_bc")], W=[b("rn")])
                dma("sp", "oq0", dst_d[i * 128:(i + 1) * 128, :], rn, R=[b("rn")], W=[b("hbm_out")])
                yield 4.0

            def dry_costs(mk):
                sch.dry = True
                cs = list(mk())
                sch.dry = False
                return cs

            def run_seq(g):
                for c in g:
                    if c < 0:
                        return True
                return False

            def interleave(gens, tots):
                acc = [0.0] * len(gens)
                live = [True] * len(gens)
                while any(live):
                    k = min((j for j in range(len(gens)) if live[j]), key=lambda j: acc[j] / tots[j])
                    try:
                        c = next(gens[k])
                        if c > 0:
                            acc[k] += c
                    except StopIteration:
                        live[k] = False

            def pipeline_step(mkC, mkAB):
                gAB = mkAB() if mkAB is not None else None
                if _SEQ or gAB is None or mkC is None:
                    if mkC is not None:
                        run_seq(mkC())
                    if gAB is not None:
                        run_seq(gAB)
                        run_seq(gAB)
                    return
                if _MODE == 1:
                    cAB = [c for c in dry_costs(mkAB) if c > 0]
                    cC = dry_costs(mkC)
                    interleave([mkC(), gAB], [max(sum(cC), 1e-6), max(sum(cAB), 1e-6)])
                    return
                cs = dry_costs(mkAB)
                kb = cs.index(-1.0)
                totB = max(sum(cs[kb + 1:]), 1e-6)
                totC = max(sum(dry_costs(mkC)), 1e-6)
                run_seq(gAB)
                interleave([mkC(), gAB], [totC, totB])

            for l in range(NL):
                layer_setup(l)
                src = x_d if l == 0 else x1_d
                dst = x1_d if l == 0 else out_d
                if l == 1:
                    sch.wait_all("sp", "oq0")
                pipeline_step(None, lambda: stageAB(l, 0, src))
                for i in range(1, NB):
                    pipeline_step(lambda: stageC(l, i - 1, dst), lambda: stageAB(l, i, src))
                pipeline_step(lambda: stageC(l, NB - 1, dst), None)
            sch.engs["sp"].waited.pop("oq0", None)
            sch.wait_all("sp", "oq0")
    return nc


def _consts():
    ident = np.eye(128, dtype=np.float32)
    invf = (np.float32(10000.0) ** (-np.arange(0, 64, 2, dtype=np.float32) / np.float32(64))).astype(np.float32)
    invf = np.ascontiguousarray(np.broadcast_to(invf[None, :], (128, 32))).astype(np.float32)
    pow2 = np.ascontiguousarray(np.broadcast_to((2.0 ** -(np.arange(KITER + 1) + 1.0))[None, :], (128, KITER + 1))).astype(np.float32)
    return ident, invf, pow2


def make_in_maps(NB, x, c, positions, w_ada, b_ada, w_in, v_norm_g, v_norm_b, w_spatial, b_spatial, w_out, ln_g, ln_b, ncores):
    S = NB * 128
    ident, invf, pow2 = _consts()
    f = lambda a: np.ascontiguousarray(np.asarray(a, dtype=np.float32))
    shared = {
        "w_ada": f(w_ada), "b_ada": f(b_ada).reshape(NL, 24, 128), "w_in": f(w_in),
        "v_norm_g": f(v_norm_g), "v_norm_b": f(v_norm_b), "w_spatial": f(w_spatial), "b_spatial": f(b_spatial),
        "w_out": f(w_out), "ln_g": f(ln_g), "ln_b": f(ln_b), "ident": ident, "invf": invf, "pow2": pow2,
    }
    maps = []
    for bi in range(ncores):
        m = dict(shared)
        m["x"] = f(x[bi, :S])
        m["c"] = f(c[bi]).reshape(8, 128)
        m["pos"] = np.ascontiguousarray(np.asarray(positions[bi, :S], dtype=np.int32)).reshape(NB, 128)
        maps.append(m)
    return maps


_NC_CACHE = {}


def kernel(x, c, positions, w_ada, b_ada, w_in, v_norm_g, v_norm_b, w_spatial, b_spatial, w_out, ln_g, ln_b):
    x = np.asarray(x)
    Bn, S, _ = x.shape
    NB = S // 128
    if NB not in _NC_CACHE:
        _NC_CACHE[NB] = build(NB)
    nc = _NC_CACHE[NB]
    maps = make_in_maps(NB, x, c, positions, w_ada, b_ada, w_in, v_norm_g, v_norm_b, w_spatial, b_spatial, w_out, ln_g, ln_b, Bn)
    res = run_bass_kernel_spmd(nc, maps, core_ids=list(range(Bn)))
    out = np.stack([np.asarray(r["out"]) for r in res.results], axis=0).astype(np.float32)
    return out
```

```python
import numpy as np
from contextlib import ExitStack
import concourse.bass as bass
import concourse.mybir as mybir
from concourse.bass_utils import run_bass_kernel_spmd

F32 = mybir.dt.float32
BF16 = mybir.dt.bfloat16
I32 = mybir.dt.int32
ALU = mybir.AluOpType
AF = mybir.ActivationFunctionType
AX = mybir.AxisListType

D = 1024
DIN = 3272
NL = 2
NCORES = 8
LN_EPS = 1e-5
ALPHA = (2.0 * NL) ** 0.25
KITER = 20
import os as _os
_SEQ = _os.environ.get('KERNEL_SEQ', '0') == '1'
_PSER = int(_os.environ.get('KERNEL_PSER', '3'))
_ALLINC = _os.environ.get('KERNEL_ALLINC', '0') == '1'
_MODE = int(_os.environ.get('KERNEL_MODE', '1'))
NEG = -1.0e30

C_U, C_V, C_ZA, C_Q, C_QI, C_ZB, C_K, C_KI, C_VAL, C_W = 0, 512, 1024, 1536, 2048, 2560, 3072, 3136, 3200, 3264
PERM = [(0, 0, 2048), (2688, C_QI, 512), (2176, C_ZB, 512), (2048, C_K, 64),
        (3200, C_KI, 64), (2112, C_VAL, 64), (3264, C_W, 8)]


class Eng:
    def __init__(self, name, eng, sem):
        self.name, self.eng, self.sem = name, eng, sem
        self.cnt = 0
        self.waited = {}
        self.real = 0
        self.rmap = {}


class Buf:
    __slots__ = ("name", "w", "r")

    def __init__(self, name):
        self.name = name
        self.w = None
        self.r = {}


class Sch:
    def __init__(self, nc, es):
        self.nc = nc
        self.es = es
        self.engs = {}
        for name, eng in (("pe", nc.tensor), ("act", nc.scalar), ("dve", nc.vector),
                          ("pool", nc.gpsimd), ("sp", nc.sync)):
            self.engs[name] = Eng(name, eng, es.enter_context(nc.semaphore("s_" + name)))
        self.dq = {}
        self.dry = False
        self.mode = "plan"
        self.needed = set()
        self.B = {}

    def reset(self, mode):
        self.mode = mode
        self.B = {}
        for E in list(self.engs.values()) + list(self.dq.values()):
            E.cnt = 0
            E.waited = {}
            E.real = 0
            E.rmap = {}

    def buf(self, name):
        if name not in self.B:
            self.B[name] = Buf(name)
        return self.B[name]

    def dmaq(self, name):
        if name not in self.dq:
            q = Eng(name, None, self.es.enter_context(self.nc.semaphore("q_" + name)))
            q.real = 0
            q.rmap = {}
            self.dq[name] = q
        return self.dq[name]

    def _need(self, E, deps, dep):
        e, c = dep
        if deps.get(e.name, (None, 0))[1] < c:
            deps[e.name] = (e, c)

    def _deps(self, E, R, W):
        deps = {}
        for b in R:
            if b.w is not None:
                self._need(E, deps, b.w)
        for b in W:
            if b.w is not None and b.w[0] is not E:
                self._need(E, deps, b.w)
            for (e, c) in b.r.values():
                if e is not E:
                    self._need(E, deps, (e, c))
        return deps

    def _emit_waits(self, E, deps):
        for (e, c) in deps.values():
            if E.waited.get(e.name, 0) >= c:
                continue
            if self.mode == "plan":
                self.needed.add((e.name, c))
            else:
                E.eng.wait_ge(e.sem, e.rmap[c])
            E.waited[e.name] = c

    def wait_all(self, ename, qname):
        E = self.engs[ename]
        Q = self.dmaq(qname)
        if Q.cnt > 0:
            self._emit_waits(E, {Q.name: (Q, Q.cnt)})

    def op(self, ename, fn, R=(), W=()):
        if self.dry:
            return
        E = self.engs[ename]
        if _PSER:
            ps = [x for x in list(R) + list(W) if x.name.startswith("PB") or x.name == "PT"]
            if ps and _PSER == 1:
                W = list(W) + [self.buf("PSUM_ALL")]
            elif ps and _PSER == 2 and ename != "pe":
                W = list(W) + [self.buf("PSUM_RD")]
            elif ps and _PSER == 3 and ename != "pe":
                W = list(W) + [self.buf("RD_" + x.name) for x in ps]
        self._emit_waits(E, self._deps(E, R, W))
        E.cnt += 1
        if self.mode == "emit":
            inst = fn(E.eng)
            if _ALLINC or (E.name, E.cnt) in self.needed:
                E.real += 1
                E.rmap[E.cnt] = E.real
                inst.then_inc(E.sem, 1)
        for b in R:
            b.r[E.name] = (E, E.cnt)
        for b in W:
            b.w = (E, E.cnt)
            b.r = {}

    def dma(self, issuer, qname, out, in_, R=(), W=()):
        if self.dry:
            return
        E = self.engs[issuer]
        Q = self.dmaq(qname)
        deps = self._deps(Q, R, W)
        if Q.cnt > 0:
            deps[Q.name] = (Q, Q.cnt)
        self._emit_waits(E, deps)
        Q.cnt += 1
        if self.mode == "emit":
            Q.real += 16
            Q.rmap[Q.cnt] = Q.real
            E.eng.dma_start(out=out, in_=in_).then_inc(Q.sem, 16)
        for b in R:
            b.r[Q.name] = (Q, Q.cnt)
        for b in W:
            b.w = (Q, Q.cnt)
            b.r = {}


def build(NB, dbg=False):
    S = NB * 128
    TOP = min(256, S // 4)
    nc = bass.Bass("TRN2", target_bir_lowering=False)

    def din(name, shape, dt=F32):
        return nc.dram_tensor(name, list(shape), dt, kind="ExternalInput").ap()

    x_d = din("x", [S, D])
    c_d = din("c", [8, 128])
    pos_d = din("pos", [NB, 128], I32)
    wada_d = din("w_ada", [NL, D, 3 * D])
    bada_d = din("b_ada", [NL, 24, 128])
    win_d = din("w_in", [NL, D, DIN])
    vg_d = din("v_norm_g", [NL, 512])
    vb_d = din("v_norm_b", [NL, 512])
    wsp_d = din("w_spatial", [NL, 8, 128, 128])
    bsp_d = din("b_spatial", [NL, 8, 128])
    wout_d = din("w_out", [NL, D, D])
    lng_d = din("ln_g", [NL, D])
    lnb_d = din("ln_b", [NL, D])
    ident_d = din("ident", [128, 128])
    invf_d = din("invf", [128, 32])
    pow2_d = din("pow2", [128, KITER + 1])
    out_d = nc.dram_tensor("out", [S, D], F32, kind="ExternalOutput").ap()
    x1_d = nc.dram_tensor("x1_scratch", [S, D], F32, kind="Internal").ap()

    es = ExitStack()
    with es:
        sch = Sch(nc, es)
        op, dma = sch.op, sch.dma

        def sb(name, shape, dt=F32):
            return es.enter_context(nc.sbuf_tensor("sb_" + name, list(shape), dt))

        w_in_bf = sb("w_in_bf", [128, 8, DIN], BF16)
        w_out_bf = sb("w_out_bf", [128, 8, D], BF16)
        rows = sb("rows", [33, DIN], BF16)
        ones33 = sb("ones33", [33, 128], BF16)
        KK = sb("KK", [128, S], BF16)
        Vst = sb("Vst", [128, NB, 65], BF16)
        WT = sb("WT", [128, 8, 128], BF16)
        bsT = sb("bsT", [128, 8], F32)
        cosT = sb("cosT", [128, NB, 32], F32)
        sinT = sb("sinT", [128, NB, 32], F32)
        vg_bc = sb("vg_bc", [128, 512], F32)
        vb_bc = sb("vb_bc", [128, 512], F32)
        lng_bc = sb("lng_bc", [128, D], F32)
        lnb_bc = sb("lnb_bc", [128, D], F32)
        ident_f = sb("ident_f", [128, 128], F32)
        ident_b = sb("ident_b", [128, 128], BF16)
        pow2 = sb("pow2", [128, KITER + 1], F32)
        condT2 = sb("condT2", [128, 8, 2], F32)
        SW = max(S, DIN)
        io = sb("io", [128, 4096], F32)
        xblk = [io[:, 0:1024], io[:, 1024:2048]]
        res = io[:, 2048:3072]
        rn = io[:, 3072:4096]
        IO = ["xblk0", "xblk1", "res", "rn"]
        ew = sb("ew", [128, 6 * 512], F32)
        u_sb, vn0, sza, szb0, szb1, t1 = [ew[:, k * 512:(k + 1) * 512] for k in range(6)]
        szb = [szb0, szb1]
        EW = ["u_sb", "vn0", "sza", "szb0", "szb1", "t1"]
        att = sb("att", [128, 3072], BF16)
        MT = [att[:, 0:512].rearrange("p (j t) -> p j t", j=4), att[:, 512:1024].rearrange("p (j t) -> p j t", j=4)]
        Ee = [att[:, 1024:2048].rearrange("p (h t) -> p h t", h=8), att[:, 2048:3072].rearrange("p (h t) -> p h t", h=8)]
        Mbuf = [sb("Mbuf%d" % k, [128, SW], BF16) for k in range(2)]
        ATT = ["Mbuf0"]
        score = sb("score", [128, SW], F32)
        xnT = sb("xnT", [128, D], BF16)
        st12 = sb("st12", [128, 12], F32)
        mv = sb("mv", [128, 2], F32)
        rstd = sb("rstd", [128, 1], F32)
        vn_bf = sb("vn_bf", [128, 512], BF16)
        ra = sb("ra", [128, 8, 32], F32)
        rb = sb("rb", [128, 8, 32], F32)
        rc = sb("rc", [128, 8, 32], F32)
        rd = sb("rd", [128, 8, 32], F32)
        Z = sb("Z", [128, 8, 128], BF16)
        xn = Z[:].rearrange("p h d -> p (h d)")
        KZ = sb("KZ", [128, 128], BF16)
        QQ = [sb("QQ%d" % k, [128, 8, 128], BF16) for k in range(2)]
        w8 = sb("w8", [128, 8], F32)
        diagW = sb("diagW", [128, 8, 128], BF16)
        Rr = [sb("Rr%d" % i, [128, 512], BF16) for i in range(4)]
        st12C = sb("st12C", [128, 12], F32)
        mvC = sb("mvC", [128, 2], F32)
        rstdC = sb("rstdC", [128, 1], F32)
        cmax = sb("cmax", [128, 8], F32)
        cmin = sb("cmin", [128, 8], F32)
        rmax = sb("rmax", [128, 1], F32)
        rmin = sb("rmin", [128, 1], F32)
        w0 = sb("w0", [128, 1], F32)
        HWt = sb("HWt", [128, KITER + 1], F32)
        mid = [sb("mid%d" % i, [128, 1], F32) for i in range(2)]
        cnt = sb("cnt", [128, 1], F32)
        stp = sb("stp", [128, 1], F32)
        tau = sb("tau", [128, 1], F32)
        rden = sb("rden", [128, 8], F32)
        yb0 = res[:, 0:512].rearrange("p (h d) -> p h d", h=8)
        y_bf = [sb("y_bf%d" % k, [128, D], BF16) for k in range(2)]
        yT = sb("yT", [128, D], BF16)
        stage = [score[:, 0:DIN], io[:, 0:DIN]]
        STG = [["score"], IO]
        gate_bc = ew[:, 0:1024]
        bgate_bc = ew[:, 1024:2048]
        wsp_f = ew[:, 2048:3072].rearrange("p (g s) -> p g s", g=8)
        wsp_b = Z
        c0src = score[32:33, 0:DIN]
        c0dst = score[0:1, 0:DIN]
        c0hi = Mbuf[0][0:1, 0:DIN]
        ang = ew[:, 0:NB * 32].rearrange("p (n f) -> p n f", f=32)
        ang2 = ew[:, 1024:1024 + NB * 32].rearrange("p (n f) -> p n f", f=32)
        small = sb("small", [32, 128], F32)
        modT = sb("modT", [128, 16], F32)
        badaT = sb("badaT", [128, 24], F32)
        shiftT2 = sb("shiftT2", [128, 8, 2], F32)
        scaleT1 = sb("scaleT1", [128, 8], F32)
        posf = sb("posf", [128, NB], F32)
        posi = sb("posi", [32, 128], I32)
        posr = sb("posr", [32, 128], F32)
        invf = sb("invf", [128, 32], F32)
        negpi = sb("negpi", [128, 1], F32)

        PB = [es.enter_context(nc.psum_tensor("PB%d" % i, [128, 512], F32)) for i in range(7)]
        PT = es.enter_context(nc.psum_tensor("PT", [128, 1024], BF16))

        def b(name):
            return sch.buf(name)

        def bl(names):
            return [b(nm) for nm in names]

        condrep = ew[:, 2048:3072].rearrange("p (k m) -> p k m", k=8)

        neg_reg = nc.gpsimd.to_reg(NEG)
        zero_reg = nc.gpsimd.to_reg(0.0)
        for _pass in ("plan", "emit"):
            sch.reset(_pass)
            dma("sp", "ld0", ident_f[:], ident_d[:, :], W=[b("ident_f")])
            dma("sp", "ld1", invf[:], invf_d[:, :], W=[b("invf")])
            dma("sp", "ld0", pow2[:], pow2_d[:, :], W=[b("pow2")])
            dma("sp", "ld1", small[0:8, :], c_d[:, :], W=[b("small")])
            dma("sp", "ld0", posi[0:NB, :], pos_d[:, :], W=[b("posi")])
            op("dve", lambda e: e.tensor_copy(out=ident_b[:], in_=ident_f[:]), R=[b("ident_f")], W=[b("ident_b")])
            op("pool", lambda e: e.memset(ones33[:], 1.0), W=[b("ones33")])
            op("pool", lambda e: e.memset(rows[:], 0.0), W=[b("rows")])
            op("pool", lambda e: e.memset(Vst[:], 1.0), W=[b("Vst")])

            op("act", lambda e: e.activation(out=small[0:8, :], in_=small[0:8, :], func=AF.Silu),
               R=[b("small")], W=[b("small")])
            op("pe", lambda e: e.transpose(PB[0][:, 0:8], small[0:8, :], ident_f[0:8, 0:8]),
               R=[b("small"), b("ident_f")], W=[b("PB0")])
            for j in range(2):
                op("dve", lambda e, j=j: e.tensor_copy(out=condT2[:, :, j], in_=PB[0][:, 0:8]),
                   R=[b("PB0")], W=[b("condT2")])

            op("dve", lambda e: e.tensor_copy(out=posr[0:NB, :], in_=posi[0:NB, :]), R=[b("posi")], W=[b("posr")])
            op("pe", lambda e: e.transpose(PB[1][:, 0:NB], posr[0:NB, :], ident_f[0:NB, 0:NB]),
               R=[b("posr"), b("ident_f")], W=[b("PB1")])
            op("dve", lambda e: e.tensor_copy(out=posf[:], in_=PB[1][:, 0:NB]), R=[b("PB1")], W=[b("posf")])
            op("dve", lambda e: e.tensor_tensor(out=ang[:], in0=posf[:, :, None].to_broadcast([128, NB, 32]),
                                                in1=invf[:, None, :].to_broadcast([128, NB, 32]), op=ALU.mult),
               R=[b("posf"), b("invf")], W=bl(EW))
            TWO_PI = float(2.0 * np.pi)
            tmpf = ew[:, 2048:2048 + NB * 32].rearrange("p (n f) -> p n f", f=32)
            angi = Mbuf[0][:, 0:2 * NB * 32].bitcast(I32).rearrange("p (n f) -> p n f", f=32)
            op("dve", lambda e: e.tensor_scalar(out=ang2[:], in0=ang[:], scalar1=float(np.pi / 2), scalar2=None, op0=ALU.add),
               R=bl(EW), W=bl(EW))

            def reduce_angle(a_):
                op("dve", lambda e: e.tensor_scalar(out=tmpf, in0=a_, scalar1=float(1.0 / TWO_PI), scalar2=None, op0=ALU.mult),
                   R=bl(EW), W=bl(EW))
                op("dve", lambda e: e.tensor_copy(out=angi, in_=tmpf), R=bl(EW), W=bl(ATT))
                op("dve", lambda e: e.tensor_copy(out=tmpf, in_=angi), R=bl(ATT), W=bl(EW))
                op("dve", lambda e: e.scalar_tensor_tensor(out=a_, in0=tmpf, scalar=-TWO_PI, in1=a_, op0=ALU.mult, op1=ALU.add),
                   R=bl(EW), W=bl(EW))
                op("dve", lambda e: e.tensor_scalar(out=tmpf, in0=a_, scalar1=3.14159, scalar2=-TWO_PI, op0=ALU.is_gt, op1=ALU.mult),
                   R=bl(EW), W=bl(EW))
                op("dve", lambda e: e.tensor_tensor(out=a_, in0=a_, in1=tmpf, op=ALU.add), R=bl(EW), W=bl(EW))
                op("dve", lambda e: e.tensor_scalar(out=tmpf, in0=a_, scalar1=-3.14159, scalar2=TWO_PI, op0=ALU.is_lt, op1=ALU.mult),
                   R=bl(EW), W=bl(EW))
                op("dve", lambda e: e.tensor_tensor(out=a_, in0=a_, in1=tmpf, op=ALU.add), R=bl(EW), W=bl(EW))

            reduce_angle(ang)
            reduce_angle(ang2)
            op("act", lambda e: e.activation(out=sinT[:], in_=ang, func=AF.Sin), R=bl(EW), W=[b("sinT")])
            op("act", lambda e: e.activation(out=cosT[:], in_=ang2, func=AF.Sin), R=bl(EW), W=[b("cosT")])

            def layer_setup(l):
                dma("sp", "ld0", small[0:24, :], bada_d[l, :, :], W=[b("small")])
                op("pe", lambda e: e.transpose(PB[0][:, 0:24], small[0:24, :], ident_f[0:24, 0:24]),
                   R=[b("small"), b("ident_f")], W=[b("PB0")])
                op("dve", lambda e: e.tensor_copy(out=badaT[:], in_=PB[0][:, 0:24]), R=[b("PB0")], W=[b("badaT")])
                bg = bada_d[l, 16:24, :].rearrange("a b -> (a b)").unsqueeze(0).to_broadcast([128, D])
                op("dve", lambda e: e.tensor_copy(out=condrep, in_=condT2[:, :, 0:1].to_broadcast([128, 8, 128])),
                   R=[b("condT2")], W=bl(EW))
                dma("sp", "ld1", bgate_bc, bg, W=bl(EW))
                dma("sp", "ld0", vg_bc[:], vg_d[l:l + 1, :].to_broadcast([128, 512]), W=[b("vg_bc")])
                dma("sp", "ld1", vb_bc[:], vb_d[l:l + 1, :].to_broadcast([128, 512]), W=[b("vb_bc")])
                dma("sp", "ld0", lng_bc[:], lng_d[l:l + 1, :].to_broadcast([128, D]), W=[b("lng_bc")])
                dma("sp", "ld1", lnb_bc[:], lnb_d[l:l + 1, :].to_broadcast([128, D]), W=[b("lnb_bc")])
                for kc in range(8):
                    stg = stage[kc % 2]
                    SN = bl(STG[kc % 2])
                    dma("sp", "st%d" % (kc % 2), stg[:, 0:3 * D], wada_d[l, kc * 128:(kc + 1) * 128, :], W=SN)
                    for j in range(16):
                        op("pe", lambda e, j=j, kc=kc, stg=stg: e.matmul(
                            PB[0][:, 2 * j:2 * j + 2], lhsT=stg[:, j * 128:(j + 1) * 128], rhs=condT2[:, kc, :],
                            start=(kc == 0 and j == 0), stop=(kc == 7), skip_group_check=True),
                           R=SN + [b("condT2")], W=[b("PB0")])
                    for n2 in range(2):
                        op("pe", lambda e, n2=n2, kc=kc, stg=stg: e.matmul(
                            PB[1 + n2][:, :], lhsT=condrep[:, kc, :], rhs=stg[:, 2048 + n2 * 512:2048 + (n2 + 1) * 512],
                            start=(kc == 0), stop=(kc == 7)),
                           R=SN + bl(EW), W=[b("PB%d" % (1 + n2))])
                op("dve", lambda e: e.tensor_copy(out=modT[:], in_=PB[0][:, 0:32].rearrange("p (j t) -> p j t", t=2)[:, :, 0]),
                   R=[b("PB0")], W=[b("modT")])
                for j in range(2):
                    op("dve", lambda e, j=j: e.tensor_tensor(out=shiftT2[:, :, j], in0=modT[:, 0:8], in1=badaT[:, 0:8], op=ALU.add),
                       R=[b("modT"), b("badaT")], W=[b("shiftT2")])
                op("dve", lambda e: e.scalar_tensor_tensor(out=scaleT1[:], in0=modT[:, 8:16], scalar=1.0, in1=badaT[:, 8:16],
                                                           op0=ALU.add, op1=ALU.add),
                   R=[b("modT"), b("badaT")], W=[b("scaleT1")])
                for n2 in range(2):
                    op("dve", lambda e, n2=n2: e.tensor_tensor(out=gate_bc[:, n2 * 512:(n2 + 1) * 512], in0=PB[1 + n2][:, :],
                                                               in1=bgate_bc[:, n2 * 512:(n2 + 1) * 512], op=ALU.add),
                       R=[b("PB%d" % (1 + n2))] + bl(EW), W=bl(EW))
                for kc in range(8):
                    stg = stage[kc % 2]
                    SN = bl(STG[kc % 2])
                    dma("sp", "st%d" % (kc % 2), stg[:, 0:D], wout_d[l, kc * 128:(kc + 1) * 128, :], W=SN)
                    op(("dve", "pool")[kc % 2], lambda e, kc=kc, stg=stg: e.tensor_tensor(
                        out=w_out_bf[:, kc, :], in0=stg[:, 0:D], in1=gate_bc[:], op=ALU.mult),
                       R=SN + bl(EW), W=[b("w_out_bf")])
                ntl = [(i * 512, min(512, DIN - i * 512)) for i in range(7)]
                for kc in range(8):
                    stg = stage[kc % 2]
                    SN = bl(STG[kc % 2])
                    dma("sp", "st%d" % (kc % 2), stg[:, :], win_d[l, kc * 128:(kc + 1) * 128, :], W=SN)
                    for nt, (cs, cw) in enumerate(ntl):
                        op("pe", lambda e, nt=nt, cs=cs, cw=cw, kc=kc, stg=stg: e.matmul(
                            PB[nt][0:2, 0:cw], lhsT=shiftT2[:, kc, :], rhs=stg[:, cs:cs + cw],
                            start=(kc == 0), stop=(kc == 7)),
                           R=SN + [b("shiftT2")], W=[b("PB%d" % nt)])
                    for pi, (s0, d0, wd) in enumerate(PERM):
                        en = ("dve", "pool", "act")[pi % 3] if wd >= 512 else "pool"
                        if en == "act":
                            op("act", lambda e, s0=s0, d0=d0, wd=wd, kc=kc, stg=stg: e.activation(
                                out=w_in_bf[:, kc, d0:d0 + wd], in_=stg[:, s0:s0 + wd], func=AF.Copy, scale=scaleT1[:, kc:kc + 1]),
                               R=SN + [b("scaleT1")], W=[b("w_in_bf")])
                        else:
                            op(en, lambda e, s0=s0, d0=d0, wd=wd, kc=kc, stg=stg: e.tensor_scalar(
                                out=w_in_bf[:, kc, d0:d0 + wd], in0=stg[:, s0:s0 + wd], scalar1=scaleT1[:, kc:kc + 1],
                                scalar2=None, op0=ALU.mult),
                               R=SN + [b("scaleT1")], W=[b("w_in_bf")])
                for nt, (cs, cw) in enumerate(ntl):
                    op("dve", lambda e, nt=nt, cs=cs, cw=cw: e.tensor_copy(out=c0src[:, cs:cs + cw], in_=PB[nt][0:1, 0:cw]),
                       R=[b("PB%d" % nt)], W=[b("score")])
                for (s0, d0, wd) in PERM:
                    op("dve", lambda e, s0=s0, d0=d0, wd=wd: e.tensor_copy(out=c0dst[0:1, d0:d0 + wd], in_=c0src[:, s0:s0 + wd]),
                       R=[b("score")], W=[b("score")])
                op("dve", lambda e: e.tensor_copy(out=c0hi[0:1, :], in_=c0dst[0:1, :]), R=[b("score")], W=bl(ATT))
                op("dve", lambda e: e.tensor_copy(out=rows[0:1, :], in_=c0hi[0:1, :]), R=bl(ATT), W=[b("rows")])
                op("dve", lambda e: e.tensor_tensor(out=c0src[:, :], in0=c0dst[0:1, :], in1=c0hi[0:1, :], op=ALU.subtract),
                   R=[b("score")] + bl(ATT), W=[b("score")])
                op("dve", lambda e: e.tensor_copy(out=rows[32:33, :], in_=c0src[:, :]), R=[b("score")], W=[b("rows")])
                dma("sp", "ld0", wsp_f[:], wsp_d[l].rearrange("g t s -> t g s"), W=bl(EW))
                op("pool", lambda e: e.affine_select(out=wsp_f[:], in_=wsp_f[:], pattern=[[0, 8], [-1, 128]],
                                                     compare_op=ALU.is_ge, fill=zero_reg, base=0, channel_multiplier=1),
                   R=bl(EW), W=bl(EW))
                op("pool", lambda e: e.tensor_copy(out=wsp_b[:], in_=wsp_f[:]), R=bl(EW), W=[b("Z")])
                for g in range(8):
                    op("pe", lambda e, g=g: e.transpose(PT[:, g * 128:(g + 1) * 128], wsp_b[:, g, :], ident_b[:]),
                       R=[b("Z"), b("ident_b")], W=[b("PT")])
                op("dve", lambda e: e.tensor_copy(out=WT[:], in_=PT[:, :].rearrange("p (g t) -> p g t", g=8)),
                   R=[b("PT")], W=[b("WT")])
                dma("sp", "ld1", small[0:8, :], bsp_d[l, :, :], W=[b("small")])
                op("pe", lambda e: e.transpose(PB[0][:, 0:8], small[0:8, :], ident_f[0:8, 0:8]),
                   R=[b("small"), b("ident_f")], W=[b("PB0")])
                op("dve", lambda e: e.tensor_copy(out=bsT[:], in_=PB[0][:, 0:8]), R=[b("PB0")], W=[b("bsT")])

            def rope(src4, dst_lo, dst_hi, nh, blk):
                cb = cosT[:, blk:blk + 1, :].to_broadcast([128, nh, 32])
                sbb = sinT[:, blk:blk + 1, :].to_broadcast([128, nh, 32])
                x1, x2 = src4[:, :, 0, :], src4[:, :, 1, :]
                return cb, sbb, x1, x2

            def ln_stats(src_ap_halves, srcbufs, C=False):
                st_, mv_, rs_ = (st12C, mvC, rstdC) if C else (st12, mv, rstd)
                sn, mn, rsn = ("st12C", "mvC", "rstdC") if C else ("st12", "mv", "rstd")
                for hh, ap in enumerate(src_ap_halves):
                    op("dve", lambda e, hh=hh, ap=ap: e.bn_stats(out=st_[:, hh * 6:(hh + 1) * 6], in_=ap),
                       R=srcbufs, W=[b(sn)])
                nst = 6 * len(src_ap_halves)
                op("dve", lambda e: e.bn_aggr(out=mv_[:], in_=st_[:, 0:nst]), R=[b(sn)], W=[b(mn)])
                op("dve", lambda e: e.tensor_scalar(out=rs_[:], in0=mv_[:, 1:2], scalar1=LN_EPS, scalar2=None, op0=ALU.add),
                   R=[b(mn)], W=[b(rsn)])
                op("act", lambda e: e.activation(out=rs_[:], in_=rs_[:], func=AF.Sqrt), R=[b(rsn)], W=[b(rsn)])
                op("dve", lambda e: e.reciprocal(out=rs_[:], in_=rs_[:]), R=[b(rsn)], W=[b(rsn)])

            def kkn(blks):
                return [b("KK%d" % j) for j in blks]

            def stageAB(l, i, src_d):
                n = (i + 1) * 128
                p2 = i % 2
                xb, xbn = xblk[p2], "xblk%d" % p2
                qq, qqn = QQ[p2], "QQ%d" % p2
                szb_, szbn = szb[p2], "szb%d" % p2
                yb, ybn = y_bf[p2], "y_bf%d" % p2
                Mb, Mbn = Mbuf[p2], "Mbuf%d" % p2
                dma("sp", "xq%d" % p2, xb, src_d[i * 128:(i + 1) * 128, :], W=[b(xbn)])
                ln_stats([xb[:, 0:512], xb[:, 512:1024]], [b(xbn)])
                op("dve", lambda e: e.tensor_scalar(out=xn, in0=xb, scalar1=mv[:, 0:1], scalar2=rstd[:],
                                                    op0=ALU.subtract, op1=ALU.mult),
                   R=[b(xbn), b("mv"), b("rstd")], W=[b("Z")])
                for kc in range(8):
                    op("pe", lambda e, kc=kc: e.transpose(PT[:, kc * 128:(kc + 1) * 128], xn[:, kc * 128:(kc + 1) * 128], ident_b[:]),
                       R=[b("Z"), b("ident_b")], W=[b("PT")])
                op("act", lambda e: e.copy(out=xnT[:], in_=PT[:, :]), R=[b("PT")], W=[b("xnT")])
                yield 4.0

                def proj_tile(nt, cs, cw):
                    pb = PB[nt % 2]
                    pbn = "PB%d" % (nt % 2)
                    for kc in range(8):
                        op("pe", lambda e, kc=kc: e.matmul(pb[:, 0:cw], lhsT=xnT[:, kc * 128:(kc + 1) * 128],
                                                           rhs=w_in_bf[:, kc, cs:cs + cw], start=(kc == 0), stop=False),
                           R=[b("xnT"), b("w_in_bf")], W=[b(pbn)])
                    op("pe", lambda e: e.matmul(pb[:, 0:cw], lhsT=ones33[:, :], rhs=rows[:, cs:cs + cw], start=False, stop=True),
                       R=[b("ones33"), b("rows")], W=[b(pbn)])
                    return pb, pbn

                pb, pbn = proj_tile(0, C_U, 512)
                op("act", lambda e: e.copy(out=u_sb, in_=pb[:, :]), R=[b(pbn)], W=[b("u_sb")])
                yield 3.0
                pb, pbn = proj_tile(1, C_V, 512)
                ln_stats([pb[:, 0:512]], [b(pbn)])
                op("dve", lambda e: e.tensor_scalar(out=vn0, in0=pb[:, :], scalar1=mv[:, 0:1], scalar2=rstd[:],
                                                    op0=ALU.subtract, op1=ALU.mult),
                   R=[b(pbn), b("mv"), b("rstd")], W=[b("vn0")])
                op("pool", lambda e: e.tensor_tensor(out=vn0, in0=vn0, in1=vg_bc[:], op=ALU.mult),
                   R=[b("vn0"), b("vg_bc")], W=[b("vn0")])
                op("pool", lambda e: e.tensor_tensor(out=vn_bf[:], in0=vn0, in1=vb_bc[:], op=ALU.add),
                   R=[b("vn0"), b("vb_bc")], W=[b("vn_bf")])
                yield 4.0
                pb, pbn = proj_tile(2, C_ZA, 512)
                op("act", lambda e: e.activation(out=sza, in_=pb[:, :], func=AF.Silu), R=[b(pbn)], W=[b("sza")])
                for g in range(8):
                    op("pe", lambda e, g=g: e.matmul(PB[2][:, g * 64:(g + 1) * 64], lhsT=WT[:, g, :], rhs=vn_bf[:, g * 64:(g + 1) * 64],
                                                     start=True, stop=True),
                       R=[b("WT"), b("vn_bf")], W=[b("PB2")])
                op("dve", lambda e: e.tensor_tensor(out=t1.rearrange("p (g d) -> p g d", g=8),
                                                    in0=PB[2][:, :].rearrange("p (g d) -> p g d", g=8),
                                                    in1=bsT[:, :, None].to_broadcast([128, 8, 64]), op=ALU.add),
                   R=[b("PB2"), b("bsT")], W=[b("t1")])
                op("pool", lambda e: e.tensor_tensor(out=t1, in0=t1, in1=u_sb, op=ALU.mult),
                   R=[b("t1"), b("u_sb")], W=[b("t1")])
                op("pool", lambda e: e.tensor_tensor(out=yb[:, 0:512], in0=t1, in1=sza, op=ALU.mult),
                   R=[b("t1"), b("sza")], W=[b(ybn)])
                yield 4.0
                for (nt, cs, off) in ((3, C_Q, 0), (4, C_QI, 64)):
                    pb, pbn = proj_tile(nt, cs, 512)
                    s4 = pb[:, :].rearrange("p (h a d) -> p h a d", h=8, a=2)
                    cb, sbb, x1, x2 = rope(s4, None, None, 8, i)
                    op("dve", lambda e: e.tensor_tensor(out=ra[:], in0=x1, in1=cb, op=ALU.mult), R=[b(pbn), b("cosT")], W=[b("ra")])
                    op("dve", lambda e: e.tensor_tensor(out=rb[:], in0=x2, in1=sbb, op=ALU.mult), R=[b(pbn), b("sinT")], W=[b("rb")])
                    op("dve", lambda e: e.tensor_tensor(out=rc[:], in0=x2, in1=cb, op=ALU.mult), R=[b(pbn), b("cosT")], W=[b("rc")])
                    op("dve", lambda e: e.tensor_tensor(out=rd[:], in0=x1, in1=sbb, op=ALU.mult), R=[b(pbn), b("sinT")], W=[b("rd")])
                    op("pool", lambda e, off=off: e.tensor_tensor(out=Z[:, :, off:off + 32], in0=ra[:], in1=rb[:], op=ALU.subtract),
                       R=[b("ra"), b("rb")], W=[b("Z")])
                    op("pool", lambda e, off=off: e.tensor_tensor(out=Z[:, :, off + 32:off + 64], in0=rc[:], in1=rd[:], op=ALU.add),
                       R=[b("rc"), b("rd")], W=[b("Z")])
                    yield 4.0
                pb, pbn = proj_tile(5, C_ZB, 512)
                op("act", lambda e: e.activation(out=szb_, in_=pb[:, :], func=AF.Silu), R=[b(pbn)], W=[b(szbn)])
                yield 3.0
                pb, pbn = proj_tile(6, C_K, 200)
                s4 = pb[:, 0:128].rearrange("p (h a d) -> p h a d", h=2, a=2)
                cb, sbb, x1, x2 = rope(s4, None, None, 2, i)
                op("dve", lambda e: e.tensor_tensor(out=ra[:, 0:2, :], in0=x1, in1=cb, op=ALU.mult), R=[b(pbn), b("cosT")], W=[b("ra")])
                op("dve", lambda e: e.tensor_tensor(out=rb[:, 0:2, :], in0=x2, in1=sbb, op=ALU.mult), R=[b(pbn), b("sinT")], W=[b("rb")])
                op("dve", lambda e: e.tensor_tensor(out=rc[:, 0:2, :], in0=x2, in1=cb, op=ALU.mult), R=[b(pbn), b("cosT")], W=[b("rc")])
                op("dve", lambda e: e.tensor_tensor(out=rd[:, 0:2, :], in0=x1, in1=sbb, op=ALU.mult), R=[b(pbn), b("sinT")], W=[b("rd")])
                kz3 = KZ[:, :].rearrange("p (h d) -> p h d", h=2)
                op("pool", lambda e: e.tensor_tensor(out=kz3[:, :, 0:32], in0=ra[:, 0:2, :], in1=rb[:, 0:2, :], op=ALU.subtract),
                   R=[b("ra"), b("rb")], W=[b("KZ")])
                op("pool", lambda e: e.tensor_tensor(out=kz3[:, :, 32:64], in0=rc[:, 0:2, :], in1=rd[:, 0:2, :], op=ALU.add),
                   R=[b("rc"), b("rd")], W=[b("KZ")])
                op("act", lambda e: e.copy(out=Vst[:, i, 0:64], in_=pb[:, 128:192]), R=[b(pbn)], W=[b("V%d" % i)])
                op("dve", lambda e: e.tensor_copy(out=w8[:], in_=pb[:, 192:200]), R=[b(pbn)], W=[b("w8")])
                op("dve", lambda e: e.tensor_tensor(out=diagW[:], in0=ident_b[:, None, :].to_broadcast([128, 8, 128]),
                                                    in1=w8[:, :, None].to_broadcast([128, 8, 128]), op=ALU.mult),
                   R=[b("ident_b"), b("w8")], W=[b("diagW")])
                op("pe", lambda e: e.transpose(PT[:, 0:128], KZ[:, :], ident_b[:]), R=[b("KZ"), b("ident_b")], W=[b("PT")])
                op("act", lambda e: e.copy(out=KK[:, i * 128:(i + 1) * 128], in_=PT[:, 0:128]), R=[b("PT")], W=[b("KK%d" % i)])
                for h in range(8):
                    op("pe", lambda e, h=h: e.transpose(PT[:, h * 128:(h + 1) * 128], Z[:, h, :], ident_b[:]),
                       R=[b("Z"), b("ident_b")], W=[b("PT")])
                op("act", lambda e: e.copy(out=qq[:], in_=PT[:, :].rearrange("p (h t) -> p h t", h=8)), R=[b("PT")], W=[b(qqn)])
                yield 5.0

                nch = (n + 511) // 512
                steps = [(c, h) for c in range(nch) for h in range(8)]

                def ix_mm1(k):
                    c, h = steps[k]
                    cw = min(512, n - c * 512)
                    pb, pbn = PB[k % 2], "PB%d" % (k % 2)
                    op("pe", lambda e: e.matmul(pb[:, 0:cw], lhsT=qq[64:128, h, :], rhs=KK[64:128, c * 512:c * 512 + cw],
                                                start=True, stop=True),
                       R=[b(qqn)] + kkn(range(4 * c, 4 * c + cw // 128)), W=[b(pbn)])

                def ix_rest(k):
                    c, h = steps[k]
                    cw = min(512, n - c * 512)
                    pb, pbn = PB[k % 2], "PB%d" % (k % 2)
                    rr, rrn = Rr[k % 4], "Rr%d" % (k % 4)
                    if h % 4 != 3:
                        op("act", lambda e: e.activation(out=rr[:, 0:cw], in_=pb[:, 0:cw], func=AF.Relu), R=[b(pbn)], W=[b(rrn)])
                    else:
                        op("dve", lambda e: e.tensor_scalar(out=rr[:, 0:cw], in0=pb[:, 0:cw], scalar1=0.0, scalar2=None, op0=ALU.max),
                           R=[b(pbn)], W=[b(rrn)])
                    op("pe", lambda e: e.matmul(PB[2][:, 0:cw], lhsT=diagW[:, h, :], rhs=rr[:, 0:cw], start=(h == 0), stop=(h == 7)),
                       R=[b("diagW"), b(rrn)], W=[b("PB2")])
                    if h == 7:
                        op("dve", lambda e: e.tensor_scalar(out=score[:, c * 512:c * 512 + cw], in0=PB[2][:, 0:cw], scalar1=1.0, scalar2=-3.0e38,
                                                            op0=ALU.mult, op1=ALU.max, accum_out=cmax[:, c:c + 1]),
                           R=[b("PB2")], W=[b("score"), b("cmax")])
                        op("dve", lambda e: e.tensor_scalar(out=Mb[:, c * 512:c * 512 + cw], in0=score[:, c * 512:c * 512 + cw], scalar1=1.0,
                                                            scalar2=3.0e38, op0=ALU.mult, op1=ALU.min, accum_out=cmin[:, c:c + 1]),
                           R=[b("score")], W=[b(Mbn), b("cmin")])

                ix_mm1(0)
                for k in range(len(steps)):
                    if k + 1 < len(steps):
                        ix_mm1(k + 1)
                    ix_rest(k)
                    yield 0.7
                op("dve", lambda e: e.tensor_reduce(out=rmax[:], in_=cmax[:, 0:nch], axis=AX.X, op=ALU.max), R=[b("cmax")], W=[b("rmax")])
                op("dve", lambda e: e.tensor_reduce(out=rmin[:], in_=cmin[:, 0:nch], axis=AX.X, op=ALU.min), R=[b("cmin")], W=[b("rmin")])
                op("dve", lambda e: e.tensor_tensor(out=w0[:], in0=rmax[:], in1=rmin[:], op=ALU.subtract),
                   R=[b("rmax"), b("rmin")], W=[b("w0")])
                op("dve", lambda e: e.tensor_scalar(out=w0[:], in0=w0[:], scalar1=1.001, scalar2=1.0e-6, op0=ALU.mult, op1=ALU.add),
                   R=[b("w0")], W=[b("w0")])
                op("dve", lambda e: e.tensor_scalar(out=HWt[:], in0=pow2[:], scalar1=w0[:], scalar2=None, op0=ALU.mult),
                   R=[b("pow2"), b("w0")], W=[b("HWt")])
                op("dve", lambda e: e.tensor_tensor(out=mid[0][:], in0=rmin[:], in1=HWt[:, 0:1], op=ALU.add),
                   R=[b("rmin"), b("HWt")], W=[b("mid0")])
                op("pool", lambda e: e.affine_select(out=score[:, i * 128:(i + 1) * 128], in_=score[:, i * 128:(i + 1) * 128],
                                                     pattern=[[-1, 128]], compare_op=ALU.is_ge, fill=neg_reg, base=0, channel_multiplier=1),
                   R=[b("score")], W=[b("score")])
                yield 1.5
                yield -1.0
                for k in range(KITER):
                    mc, mn = mid[k % 2], mid[(k + 1) % 2]
                    mcn, mnn = "mid%d" % (k % 2), "mid%d" % ((k + 1) % 2)
                    op("dve", lambda e: e.tensor_scalar(out=Mb[:, 0:n], in0=score[:, 0:n], scalar1=mc[:], scalar2=0.0,
                                                        op0=ALU.is_ge, op1=ALU.add, accum_out=cnt[:]),
                       R=[b("score"), b(mcn)], W=[b(Mbn), b("cnt")])
                    op("dve", lambda e: e.tensor_scalar(out=stp[:], in0=cnt[:], scalar1=float(TOP) - 0.5, scalar2=0.5,
                                                        op0=ALU.is_ge, op1=ALU.subtract), R=[b("cnt")], W=[b("stp")])
                    op("dve", lambda e: e.scalar_tensor_tensor(out=mn[:], in0=stp[:], scalar=HWt[:, k:k + 1], in1=mc[:],
                                                               op0=ALU.mult, op1=ALU.add),
                       R=[b("stp"), b("HWt"), b(mcn)], W=[b(mnn)])
                    yield 0.5 + n / 1900.0
                mfin, mfinn = mid[KITER % 2], "mid%d" % (KITER % 2)
                op("dve", lambda e: e.tensor_tensor(out=tau[:], in0=mfin[:], in1=HWt[:, KITER:KITER + 1], op=ALU.subtract),
                   R=[b(mfinn), b("HWt")], W=[b("tau")])
                op("dve", lambda e: e.tensor_scalar(out=Mb[:, 0:n], in0=score[:, 0:n], scalar1=tau[:], scalar2=None, op0=ALU.is_ge),
                   R=[b("score"), b("tau")], W=[b(Mbn)])
                yield 0.5 + n / 1900.0

            def stageC(l, i, dst_d):
                n = (i + 1) * 128
                p2 = i % 2
                xb, xbn = xblk[p2], "xblk%d" % p2
                qq, qqn = QQ[p2], "QQ%d" % p2
                szb_, szbn = szb[p2], "szb%d" % p2
                yb, ybn = y_bf[p2], "y_bf%d" % p2
                Mb, Mbn = Mbuf[p2], "Mbuf%d" % p2
                nch = (n + 511) // 512

                def at_prep(c):
                    cw = min(512, n - c * 512)
                    nb = cw // 128
                    mt, mtn = MT[c % 2], "MT%d" % (c % 2)
                    for jj in range(nb):
                        op("pe", lambda e, jj=jj: e.transpose(PT[:, jj * 128:(jj + 1) * 128],
                                                              Mb[:, c * 512 + jj * 128:c * 512 + (jj + 1) * 128], ident_b[:]),
                           R=[b(Mbn), b("ident_b")], W=[b("PT")])
                    op("act", lambda e: e.activation(out=mt[:, 0:nb, :], in_=PT[:, 0:cw].rearrange("p (j t) -> p j t", j=nb),
                                                     func=AF.Identity, scale=30000.0, bias=-30000.0),
                       R=[b("PT")], W=[b(mtn)])

                def at_st(j):
                    ee, een = Ee[j % 2], "Ee%d" % (j % 2)
                    mt, mtn = MT[(j // 4) % 2], "MT%d" % ((j // 4) % 2)
                    jj = j % 4
                    for hf in range(2):
                        op("pe", lambda e, hf=hf: e.matmul(PB[3 + hf][:, :], lhsT=KK[0:64, j * 128:(j + 1) * 128],
                                                           rhs=qq[0:64, 4 * hf:4 * hf + 4, :], start=True, stop=False),
                           R=[b("KK%d" % j), b(qqn)], W=[b("PB%d" % (3 + hf))])
                        op("pe", lambda e, hf=hf: e.matmul(PB[3 + hf][:, :], lhsT=ident_b[:, :],
                                                           rhs=mt[:, jj:jj + 1, :].to_broadcast([128, 4, 128]), start=False, stop=True),
                           R=[b("ident_b"), b(mtn)], W=[b("PB%d" % (3 + hf))])
                        op("act", lambda e, hf=hf: e.activation(out=ee[:, 4 * hf:4 * hf + 4, :],
                                                                in_=PB[3 + hf][:, :].rearrange("p (h t) -> p h t", h=4),
                                                                func=AF.Exp, scale=0.125),
                           R=[b("PB%d" % (3 + hf))], W=[b(een)])

                def at_pv(j):
                    c, jj = j // 4, j % 4
                    ee, een = Ee[j % 2], "Ee%d" % (j % 2)
                    mt, mtn = MT[c % 2], "MT%d" % (c % 2)
                    for h in range(8):
                        ob_ = PB[5 + h // 4]
                        op("pe", lambda e, h=h, ob_=ob_: e.matmul(
                            ob_[:, (h % 4) * 65:(h % 4) * 65 + 65], lhsT=ee[:, h, :], rhs=Vst[:, j, :],
                            start=(j == 0 and h % 4 == 0), stop=(j == i), skip_group_check=True),
                           R=[b(een), b("V%d" % j)], W=[b("PB%d" % (5 + h // 4))])

                at_prep(0)
                at_st(0)
                yield 1.5
                for j in range(i + 1):
                    if j % 4 == 0 and j // 4 + 1 < nch:
                        at_prep(j // 4 + 1)
                    if j + 1 <= i:
                        at_st(j + 1)
                    at_pv(j)
                    yield 2.0
                for hf in range(2):
                    o3 = PB[5 + hf][:, 0:260].rearrange("p (h d) -> p h d", h=4)
                    op("dve", lambda e, hf=hf, o3=o3: e.reciprocal(out=rden[:, 4 * hf:4 * hf + 4], in_=o3[:, :, 64]),
                       R=[b("PB%d" % (5 + hf))], W=[b("rden")])
                    op("dve", lambda e, hf=hf, o3=o3: e.tensor_tensor(out=yb0[:, 4 * hf:4 * hf + 4, :], in0=o3[:, :, 0:64],
                                                                      in1=rden[:, 4 * hf:4 * hf + 4, None].to_broadcast([128, 4, 64]),
                                                                      op=ALU.mult),
                       R=[b("PB%d" % (5 + hf)), b("rden")], W=[b("res")])
                op("dve", lambda e: e.tensor_tensor(out=yb[:, 512:1024], in0=res[:, 0:512], in1=szb_, op=ALU.mult),
                   R=[b("res"), b(szbn)], W=[b(ybn)])
                yield 2.0
                for kc in range(8):
                    op("pe", lambda e, kc=kc: e.transpose(PT[:, kc * 128:(kc + 1) * 128], yb[:, kc * 128:(kc + 1) * 128], ident_b[:]),
                       R=[b(ybn), b("ident_b")], W=[b("PT")])
                op("act", lambda e: e.copy(out=yT[:], in_=PT[:, :]), R=[b("PT")], W=[b("yT")])
                yield 2.0
                for nt in range(2):
                    for kc in range(8):
                        op("pe", lambda e, kc=kc, nt=nt: e.matmul(PB[3 + nt][:, :], lhsT=yT[:, kc * 128:(kc + 1) * 128],
                                                                  rhs=w_out_bf[:, kc, nt * 512:(nt + 1) * 512], start=(kc == 0), stop=(kc == 7)),
                           R=[b("yT"), b("w_out_bf")], W=[b("PB%d" % (3 + nt))])
                    op("dve", lambda e, nt=nt: e.scalar_tensor_tensor(out=res[:, nt * 512:(nt + 1) * 512], in0=xb[:, nt * 512:(nt + 1) * 512],
                                                                      scalar=ALPHA, in1=PB[3 + nt][:, :], op0=ALU.mult, op1=ALU.add),
                       R=[b(xbn), b("PB%d" % (3 + nt))], W=[b("res")])
                    yield 2.5
                ln_stats([res[:, 0:512], res[:, 512:1024]], [b("res")], C=True)
                op("dve", lambda e: e.tensor_scalar(out=rn, in0=res, scalar1=mvC[:, 0:1], scalar2=rstdC[:],
                                                    op0=ALU.subtract, op1=ALU.mult),
                   R=[b("res"), b("mvC"), b("rstdC")], W=[b("rn")])
                op("dve", lambda e: e.tensor_tensor(out=res, in0=rn, in1=lng_bc[:], op=ALU.mult),
                   R=[b("rn"), b("lng_bc")], W=[b("res")])
                op("dve", lambda e: e.tensor_tensor(out=rn, in0=res, in1=lnb_bc[:], op=ALU.add),
                   R=[b("res"), b("lnb_bc")], W=[b("rn")])
                dma("sp", "oq0", dst_d[i * 128:(i + 1) * 128, :], rn, R=[b("rn")], W=[b("hbm_out")])
                yield 4.0

            def dry_costs(mk):
                sch.dry = True
                cs = list(mk())
                sch.dry = False
                return cs

            def run_seq(g):
                for c in g:
                    if c < 0:
                        return True
                return False

            def interleave(gens, tots):
                acc = [0.0] * len(gens)
                live = [True] * len(gens)
                while any(live):
                    k = min((j for j in range(len(gens)) if live[j]), key=lambda j: acc[j] / tots[j])
                    try:
                        c = next(gens[k])
                        if c > 0:
                            acc[k] += c
                    except StopIteration:
                        live[k] = False

            def pipeline_step(mkC, mkAB):
                gAB = mkAB() if mkAB is not None else None
                if _SEQ or gAB is None or mkC is None:
                    if mkC is not None:
                        run_seq(mkC())
                    if gAB is not None:
                        run_seq(gAB)
                        run_seq(gAB)
                    return
                if _MODE == 1:
                    cAB = [c for c in dry_costs(mkAB) if c > 0]
                    cC = dry_costs(mkC)
                    interleave([mkC(), gAB], [max(sum(cC), 1e-6), max(sum(cAB), 1e-6)])
                    return
                cs = dry_costs(mkAB)
                kb = cs.index(-1.0)
                totB = max(sum(cs[kb + 1:]), 1e-6)
                totC = max(sum(dry_costs(mkC)), 1e-6)
                run_seq(gAB)
                interleave([mkC(), gAB], [totC, totB])

            for l in range(NL):
                layer_setup(l)
                src = x_d if l == 0 else x1_d
                dst = x1_d if l == 0 else out_d
                if l == 1:
                    sch.wait_all("sp", "oq0")
                pipeline_step(None, lambda: stageAB(l, 0, src))
                for i in range(1, NB):
                    pipeline_step(lambda: stageC(l, i - 1, dst), lambda: stageAB(l, i, src))
                pipeline_step(lambda: stageC(l, NB - 1, dst), None)
            sch.engs["sp"].waited.pop("oq0", None)
            sch.wait_all("sp", "oq0")
    return nc


def _consts():
    ident = np.eye(128, dtype=np.float32)
    invf = (np.float32(10000.0) ** (-np.arange(0, 64, 2, dtype=np.float32) / np.float32(64))).astype(np.float32)
    invf = np.ascontiguousarray(np.broadcast_to(invf[None, :], (128, 32))).astype(np.float32)
    pow2 = np.ascontiguousarray(np.broadcast_to((2.0 ** -(np.arange(KITER + 1) + 1.0))[None, :], (128, KITER + 1))).astype(np.float32)
    return ident, invf, pow2


def make_in_maps(NB, x, c, positions, w_ada, b_ada, w_in, v_norm_g, v_norm_b, w_spatial, b_spatial, w_out, ln_g, ln_b, ncores):
    S = NB * 128
    ident, invf, pow2 = _consts()
    f = lambda a: np.ascontiguousarray(np.asarray(a, dtype=np.float32))
    shared = {
        "w_ada": f(w_ada), "b_ada": f(b_ada).reshape(NL, 24, 128), "w_in": f(w_in),
        "v_norm_g": f(v_norm_g), "v_norm_b": f(v_norm_b), "w_spatial": f(w_spatial), "b_spatial": f(b_spatial),
        "w_out": f(w_out), "ln_g": f(ln_g), "ln_b": f(ln_b), "ident": ident, "invf": invf, "pow2": pow2,
    }
    maps = []
    for bi in range(ncores):
        m = dict(shared)
        m["x"] = f(x[bi, :S])
        m["c"] = f(c[bi]).reshape(8, 128)
        m["pos"] = np.ascontiguousarray(np.asarray(positions[bi, :S], dtype=np.int32)).reshape(NB, 128)
        maps.append(m)
    return maps


_NC_CACHE = {}


def kernel(x, c, positions, w_ada, b_ada, w_in, v_norm_g, v_norm_b, w_spatial, b_spatial, w_out, ln_g, ln_b):
    x = np.asarray(x)
    Bn, S, _ = x.shape
    NB = S // 128
    if NB not in _NC_CACHE:
        _NC_CACHE[NB] = build(NB)
    nc = _NC_CACHE[NB]
    maps = make_in_maps(NB, x, c, positions, w_ada, b_ada, w_in, v_norm_g, v_norm_b, w_spatial, b_spatial, w_out, ln_g, ln_b, Bn)
    res = run_bass_kernel_spmd(nc, maps, core_ids=list(range(Bn)))
    out = np.stack([np.asarray(r["out"]) for r in res.results], axis=0).astype(np.float32)
    return out
```

```python
import numpy as np
from contextlib import ExitStack
import concourse.bass as bass
import concourse.mybir as mybir
from concourse.bass_utils import run_bass_kernel_spmd

F32 = mybir.dt.float32
BF16 = mybir.dt.bfloat16
I32 = mybir.dt.int32
ALU = mybir.AluOpType
AF = mybir.ActivationFunctionType
AX = mybir.AxisListType

D = 1024
DIN = 3272
NL = 2
NCORES = 8
LN_EPS = 1e-5
ALPHA = (2.0 * NL) ** 0.25
KITER = 20
import os as _os
_SEQ = _os.environ.get('KERNEL_SEQ', '0') == '1'
_PSER = int(_os.environ.get('KERNEL_PSER', '3'))
_ALLINC = _os.environ.get('KERNEL_ALLINC', '0') == '1'
_MODE = int(_os.environ.get('KERNEL_MODE', '0'))
NEG = -1.0e30

C_U, C_V, C_ZA, C_Q, C_QI, C_ZB, C_K, C_KI, C_VAL, C_W = 0, 512, 1024, 1536, 2048, 2560, 3072, 3136, 3200, 3264
PERM = [(0, 0, 2048), (2688, C_QI, 512), (2176, C_ZB, 512), (2048, C_K, 64),
        (3200, C_KI, 64), (2112, C_VAL, 64), (3264, C_W, 8)]


class Eng:
    def __init__(self, name, eng, sem):
        self.name, self.eng, self.sem = name, eng, sem
        self.cnt = 0
        self.waited = {}
        self.real = 0
        self.rmap = {}


class Buf:
    __slots__ = ("name", "w", "r")

    def __init__(self, name):
        self.name = name
        self.w = None
        self.r = {}


class Sch:
    def __init__(self, nc, es):
        self.nc = nc
        self.es = es
        self.engs = {}
        for name, eng in (("pe", nc.tensor), ("act", nc.scalar), ("dve", nc.vector),
                          ("pool", nc.gpsimd), ("sp", nc.sync)):
            self.engs[name] = Eng(name, eng, es.enter_context(nc.semaphore("s_" + name)))
        self.dq = {}
        self.dry = False
        self.mode = "plan"
        self.needed = set()
        self.B = {}

    def reset(self, mode):
        self.mode = mode
        self.B = {}
        for E in list(self.engs.values()) + list(self.dq.values()):
            E.cnt = 0
            E.waited = {}
            E.real = 0
            E.rmap = {}

    def buf(self, name):
        if name not in self.B:
            self.B[name] = Buf(name)
        return self.B[name]

    def dmaq(self, name):
        if name not in self.dq:
            q = Eng(name, None, self.es.enter_context(self.nc.semaphore("q_" + name)))
            q.real = 0
            q.rmap = {}
            self.dq[name] = q
        return self.dq[name]

    def _need(self, E, deps, dep):
        e, c = dep
        if deps.get(e.name, (None, 0))[1] < c:
            deps[e.name] = (e, c)

    def _deps(self, E, R, W):
        deps = {}
        for b in R:
            if b.w is not None:
                self._need(E, deps, b.w)
        for b in W:
            if b.w is not None and b.w[0] is not E:
                self._need(E, deps, b.w)
            for (e, c) in b.r.values():
                if e is not E:
                    self._need(E, deps, (e, c))
        return deps

    def _emit_waits(self, E, deps):
        for (e, c) in deps.values():
            if E.waited.get(e.name, 0) >= c:
                continue
            if self.mode == "plan":
                self.needed.add((e.name, c))
            else:
                E.eng.wait_ge(e.sem, e.rmap[c])
            E.waited[e.name] = c

    def wait_all(self, ename, qname):
        E = self.engs[ename]
        Q = self.dmaq(qname)
        if Q.cnt > 0:
            self._emit_waits(E, {Q.name: (Q, Q.cnt)})

    def op(self, ename, fn, R=(), W=()):
        if self.dry:
            return
        E = self.engs[ename]
        if _PSER:
            ps = [x for x in list(R) + list(W) if x.name.startswith("PB") or x.name == "PT"]
            if ps and _PSER == 1:
                W = list(W) + [self.buf("PSUM_ALL")]
            elif ps and _PSER == 2 and ename != "pe":
                W = list(W) + [self.buf("PSUM_RD")]
            elif ps and _PSER == 3 and ename != "pe":
                W = list(W) + [self.buf("RD_" + x.name) for x in ps]
        self._emit_waits(E, self._deps(E, R, W))
        E.cnt += 1
        if self.mode == "emit":
            inst = fn(E.eng)
            if _ALLINC or (E.name, E.cnt) in self.needed:
                E.real += 1
                E.rmap[E.cnt] = E.real
                inst.then_inc(E.sem, 1)
        for b in R:
            b.r[E.name] = (E, E.cnt)
        for b in W:
            b.w = (E, E.cnt)
            b.r = {}

    def dma(self, issuer, qname, out, in_, R=(), W=()):
        if self.dry:
            return
        E = self.engs[issuer]
        Q = self.dmaq(qname)
        deps = self._deps(Q, R, W)
        if Q.cnt > 0:
            deps[Q.name] = (Q, Q.cnt)
        self._emit_waits(E, deps)
        Q.cnt += 1
        if self.mode == "emit":
            Q.real += 16
            Q.rmap[Q.cnt] = Q.real
            E.eng.dma_start(out=out, in_=in_).then_inc(Q.sem, 16)
        for b in R:
            b.r[Q.name] = (Q, Q.cnt)
        for b in W:
            b.w = (Q, Q.cnt)
            b.r = {}


def build(NB, dbg=False):
    S = NB * 128
    TOP = min(256, S // 4)
    nc = bass.Bass("TRN2", target_bir_lowering=False)

    def din(name, shape, dt=F32):
        return nc.dram_tensor(name, list(shape), dt, kind="ExternalInput").ap()

    x_d = din("x", [S, D])
    c_d = din("c", [8, 128])
    pos_d = din("pos", [NB, 128], I32)
    wada_d = din("w_ada", [NL, D, 3 * D])
    bada_d = din("b_ada", [NL, 24, 128])
    win_d = din("w_in", [NL, D, DIN])
    vg_d = din("v_norm_g", [NL, 512])
    vb_d = din("v_norm_b", [NL, 512])
    wsp_d = din("w_spatial", [NL, 8, 128, 128])
    bsp_d = din("b_spatial", [NL, 8, 128])
    wout_d = din("w_out", [NL, D, D])
    lng_d = din("ln_g", [NL, D])
    lnb_d = din("ln_b", [NL, D])
    ident_d = din("ident", [128, 128])
    invf_d = din("invf", [128, 32])
    pow2_d = din("pow2", [128, KITER + 1])
    out_d = nc.dram_tensor("out", [S, D], F32, kind="ExternalOutput").ap()
    x1_d = nc.dram_tensor("x1_scratch", [S, D], F32, kind="Internal").ap()

    es = ExitStack()
    with es:
        sch = Sch(nc, es)
        op, dma = sch.op, sch.dma

        def sb(name, shape, dt=F32):
            return es.enter_context(nc.sbuf_tensor("sb_" + name, list(shape), dt))

        w_in_bf = sb("w_in_bf", [128, 8, DIN], BF16)
        w_out_bf = sb("w_out_bf", [128, 8, D], BF16)
        rows = sb("rows", [33, DIN], BF16)
        ones33 = sb("ones33", [33, 128], BF16)
        KK = sb("KK", [128, S], BF16)
        Vst = sb("Vst", [128, NB, 65], BF16)
        WT = sb("WT", [128, 8, 128], BF16)
        bsT = sb("bsT", [128, 8], F32)
        cosT = sb("cosT", [128, NB, 32], F32)
        sinT = sb("sinT", [128, NB, 32], F32)
        vg_bc = sb("vg_bc", [128, 512], F32)
        vb_bc = sb("vb_bc", [128, 512], F32)
        lng_bc = sb("lng_bc", [128, D], F32)
        lnb_bc = sb("lnb_bc", [128, D], F32)
        ident_f = sb("ident_f", [128, 128], F32)
        ident_b = sb("ident_b", [128, 128], BF16)
        pow2 = sb("pow2", [128, KITER + 1], F32)
        condT2 = sb("condT2", [128, 8, 2], F32)
        SW = max(S, DIN)
        io = sb("io", [128, 4096], F32)
        xblk = [io[:, 0:1024], io[:, 1024:2048]]
        res = io[:, 2048:3072]
        rn = io[:, 3072:4096]
        IO = ["xblk0", "xblk1", "res", "rn"]
        ew = sb("ew", [128, 6 * 512], F32)
        u_sb, vn0, sza, szb0, szb1, t1 = [ew[:, k * 512:(k + 1) * 512] for k in range(6)]
        szb = [szb0, szb1]
        EW = ["u_sb", "vn0", "sza", "szb0", "szb1", "t1"]
        att = sb("att", [128, 3072], BF16)
        MT = [att[:, 0:512].rearrange("p (j t) -> p j t", j=4), att[:, 512:1024].rearrange("p (j t) -> p j t", j=4)]
        Ee = [att[:, 1024:2048].rearrange("p (h t) -> p h t", h=8), att[:, 2048:3072].rearrange("p (h t) -> p h t", h=8)]
        Mbuf = [sb("Mbuf%d" % k, [128, SW], BF16) for k in range(2)]
        ATT = ["Mbuf0"]
        score = sb("score", [128, SW], F32)
        xnT = sb("xnT", [128, D], BF16)
        st12 = sb("st12", [128, 12], F32)
        mv = sb("mv", [128, 2], F32)
        rstd = sb("rstd", [128, 1], F32)
        vn_bf = sb("vn_bf", [128, 512], BF16)
        ra = sb("ra", [128, 8, 32], F32)
        rb = sb("rb", [128, 8, 32], F32)
        rc = sb("rc", [128, 8, 32], F32)
        rd = sb("rd", [128, 8, 32], F32)
        Z = sb("Z", [128, 8, 128], BF16)
        xn = Z[:].rearrange("p h d -> p (h d)")
        KZ = sb("KZ", [128, 128], BF16)
        QQ = [sb("QQ%d" % k, [128, 8, 128], BF16) for k in range(2)]
        w8 = sb("w8", [128, 8], F32)
        diagW = sb("diagW", [128, 8, 128], BF16)
        Rr = [sb("Rr%d" % i, [128, 512], BF16) for i in range(4)]
        st12C = sb("st12C", [128, 12], F32)
        mvC = sb("mvC", [128, 2], F32)
        rstdC = sb("rstdC", [128, 1], F32)
        cmax = sb("cmax", [128, 8], F32)
        cmin = sb("cmin", [128, 8], F32)
        rmax = sb("rmax", [128, 1], F32)
        rmin = sb("rmin", [128, 1], F32)
        w0 = sb("w0", [128, 1], F32)
        HWt = sb("HWt", [128, KITER + 1], F32)
        mid = [sb("mid%d" % i, [128, 1], F32) for i in range(2)]
        cnt = sb("cnt", [128, 1], F32)
        stp = sb("stp", [128, 1], F32)
        tau = sb("tau", [128, 1], F32)
        rden = sb("rden", [128, 8], F32)
        yb0 = res[:, 0:512].rearrange("p (h d) -> p h d", h=8)
        y_bf = [sb("y_bf%d" % k, [128, D], BF16) for k in range(2)]
        yT = sb("yT", [128, D], BF16)
        stage = [score[:, 0:DIN], io[:, 0:DIN]]
        STG = [["score"], IO]
        gate_bc = ew[:, 0:1024]
        bgate_bc = ew[:, 1024:2048]
        wsp_f = ew[:, 2048:3072].rearrange("p (g s) -> p g s", g=8)
        wsp_b = Z
        c0src = score[32:33, 0:DIN]
        c0dst = score[0:1, 0:DIN]
        c0hi = Mbuf[0][0:1, 0:DIN]
        ang = ew[:, 0:NB * 32].rearrange("p (n f) -> p n f", f=32)
        ang2 = ew[:, 1024:1024 + NB * 32].rearrange("p (n f) -> p n f", f=32)
        small = sb("small", [32, 128], F32)
        modT = sb("modT", [128, 16], F32)
        badaT = sb("badaT", [128, 24], F32)
        shiftT2 = sb("shiftT2", [128, 8, 2], F32)
        scaleT1 = sb("scaleT1", [128, 8], F32)
        posf = sb("posf", [128, NB], F32)
        posi = sb("posi", [32, 128], I32)
        posr = sb("posr", [32, 128], F32)
        invf = sb("invf", [128, 32], F32)
        negpi = sb("negpi", [128, 1], F32)

        PB = [es.enter_context(nc.psum_tensor("PB%d" % i, [128, 512], F32)) for i in range(7)]
        PT = es.enter_context(nc.psum_tensor("PT", [128, 1024], BF16))

        def b(name):
            return sch.buf(name)

        def bl(names):
            return [b(nm) for nm in names]

        condrep = ew[:, 2048:3072].rearrange("p (k m) -> p k m", k=8)

        neg_reg = nc.gpsimd.to_reg(NEG)
        zero_reg = nc.gpsimd.to_reg(0.0)
        for _pass in ("plan", "emit"):
            sch.reset(_pass)
            dma("sp", "ld0", ident_f[:], ident_d[:, :], W=[b("ident_f")])
            dma("sp", "ld1", invf[:], invf_d[:, :], W=[b("invf")])
            dma("sp", "ld0", pow2[:], pow2_d[:, :], W=[b("pow2")])
            dma("sp", "ld1", small[0:8, :], c_d[:, :], W=[b("small")])
            dma("sp", "ld0", posi[0:NB, :], pos_d[:, :], W=[b("posi")])
            op("dve", lambda e: e.tensor_copy(out=ident_b[:], in_=ident_f[:]), R=[b("ident_f")], W=[b("ident_b")])
            op("pool", lambda e: e.memset(ones33[:], 1.0), W=[b("ones33")])
            op("pool", lambda e: e.memset(rows[:], 0.0), W=[b("rows")])
            op("pool", lambda e: e.memset(Vst[:], 1.0), W=[b("Vst")])

            op("act", lambda e: e.activation(out=small[0:8, :], in_=small[0:8, :], func=AF.Silu),
               R=[b("small")], W=[b("small")])
            op("pe", lambda e: e.transpose(PB[0][:, 0:8], small[0:8, :], ident_f[0:8, 0:8]),
               R=[b("small"), b("ident_f")], W=[b("PB0")])
            for j in range(2):
                op("dve", lambda e, j=j: e.tensor_copy(out=condT2[:, :, j], in_=PB[0][:, 0:8]),
                   R=[b("PB0")], W=[b("condT2")])

            op("dve", lambda e: e.tensor_copy(out=posr[0:NB, :], in_=posi[0:NB, :]), R=[b("posi")], W=[b("posr")])
            op("pe", lambda e: e.transpose(PB[1][:, 0:NB], posr[0:NB, :], ident_f[0:NB, 0:NB]),
               R=[b("posr"), b("ident_f")], W=[b("PB1")])
            op("dve", lambda e: e.tensor_copy(out=posf[:], in_=PB[1][:, 0:NB]), R=[b("PB1")], W=[b("posf")])
            op("dve", lambda e: e.tensor_tensor(out=ang[:], in0=posf[:, :, None].to_broadcast([128, NB, 32]),
                                                in1=invf[:, None, :].to_broadcast([128, NB, 32]), op=ALU.mult),
               R=[b("posf"), b("invf")], W=bl(EW))
            TWO_PI = float(2.0 * np.pi)
            tmpf = ew[:, 2048:2048 + NB * 32].rearrange("p (n f) -> p n f", f=32)
            angi = Mbuf[0][:, 0:2 * NB * 32].bitcast(I32).rearrange("p (n f) -> p n f", f=32)
            op("dve", lambda e: e.tensor_scalar(out=ang2[:], in0=ang[:], scalar1=float(np.pi / 2), scalar2=None, op0=ALU.add),
               R=bl(EW), W=bl(EW))

            def reduce_angle(a_):
                op("dve", lambda e: e.tensor_scalar(out=tmpf, in0=a_, scalar1=float(1.0 / TWO_PI), scalar2=None, op0=ALU.mult),
                   R=bl(EW), W=bl(EW))
                op("dve", lambda e: e.tensor_copy(out=angi, in_=tmpf), R=bl(EW), W=bl(ATT))
                op("dve", lambda e: e.tensor_copy(out=tmpf, in_=angi), R=bl(ATT), W=bl(EW))
                op("dve", lambda e: e.scalar_tensor_tensor(out=a_, in0=tmpf, scalar=-TWO_PI, in1=a_, op0=ALU.mult, op1=ALU.add),
                   R=bl(EW), W=bl(EW))
                op("dve", lambda e: e.tensor_scalar(out=tmpf, in0=a_, scalar1=3.14159, scalar2=-TWO_PI, op0=ALU.is_gt, op1=ALU.mult),
                   R=bl(EW), W=bl(EW))
                op("dve", lambda e: e.tensor_tensor(out=a_, in0=a_, in1=tmpf, op=ALU.add), R=bl(EW), W=bl(EW))
                op("dve", lambda e: e.tensor_scalar(out=tmpf, in0=a_, scalar1=-3.14159, scalar2=TWO_PI, op0=ALU.is_lt, op1=ALU.mult),
                   R=bl(EW), W=bl(EW))
                op("dve", lambda e: e.tensor_tensor(out=a_, in0=a_, in1=tmpf, op=ALU.add), R=bl(EW), W=bl(EW))

            reduce_angle(ang)
            reduce_angle(ang2)
            op("act", lambda e: e.activation(out=sinT[:], in_=ang, func=AF.Sin), R=bl(EW), W=[b("sinT")])
            op("act", lambda e: e.activation(out=cosT[:], in_=ang2, func=AF.Sin), R=bl(EW), W=[b("cosT")])

            def layer_setup(l):
                dma("sp", "ld0", small[0:24, :], bada_d[l, :, :], W=[b("small")])
                op("pe", lambda e: e.transpose(PB[0][:, 0:24], small[0:24, :], ident_f[0:24, 0:24]),
                   R=[b("small"), b("ident_f")], W=[b("PB0")])
                op("dve", lambda e: e.tensor_copy(out=badaT[:], in_=PB[0][:, 0:24]), R=[b("PB0")], W=[b("badaT")])
                bg = bada_d[l, 16:24, :].rearrange("a b -> (a b)").unsqueeze(0).to_broadcast([128, D])
                op("dve", lambda e: e.tensor_copy(out=condrep, in_=condT2[:, :, 0:1].to_broadcast([128, 8, 128])),
                   R=[b("condT2")], W=bl(EW))
                dma("sp", "ld1", bgate_bc, bg, W=bl(EW))
                dma("sp", "ld0", vg_bc[:], vg_d[l:l + 1, :].to_broadcast([128, 512]), W=[b("vg_bc")])
                dma("sp", "ld1", vb_bc[:], vb_d[l:l + 1, :].to_broadcast([128, 512]), W=[b("vb_bc")])
                dma("sp", "ld0", lng_bc[:], lng_d[l:l + 1, :].to_broadcast([128, D]), W=[b("lng_bc")])
                dma("sp", "ld1", lnb_bc[:], lnb_d[l:l + 1, :].to_broadcast([128, D]), W=[b("lnb_bc")])
                for kc in range(8):
                    stg = stage[kc % 2]
                    SN = bl(STG[kc % 2])
                    dma("sp", "st%d" % (kc % 2), stg[:, 0:3 * D], wada_d[l, kc * 128:(kc + 1) * 128, :], W=SN)
                    for j in range(16):
                        op("pe", lambda e, j=j, kc=kc, stg=stg: e.matmul(
                            PB[0][:, 2 * j:2 * j + 2], lhsT=stg[:, j * 128:(j + 1) * 128], rhs=condT2[:, kc, :],
                            start=(kc == 0 and j == 0), stop=(kc == 7), skip_group_check=True),
                           R=SN + [b("condT2")], W=[b("PB0")])
                    for n2 in range(2):
                        op("pe", lambda e, n2=n2, kc=kc, stg=stg: e.matmul(
                            PB[1 + n2][:, :], lhsT=condrep[:, kc, :], rhs=stg[:, 2048 + n2 * 512:2048 + (n2 + 1) * 512],
                            start=(kc == 0), stop=(kc == 7)),
                           R=SN + bl(EW), W=[b("PB%d" % (1 + n2))])
                op("dve", lambda e: e.tensor_copy(out=modT[:], in_=PB[0][:, 0:32].rearrange("p (j t) -> p j t", t=2)[:, :, 0]),
                   R=[b("PB0")], W=[b("modT")])
                for j in range(2):
                    op("dve", lambda e, j=j: e.tensor_tensor(out=shiftT2[:, :, j], in0=modT[:, 0:8], in1=badaT[:, 0:8], op=ALU.add),
                       R=[b("modT"), b("badaT")], W=[b("shiftT2")])
                op("dve", lambda e: e.scalar_tensor_tensor(out=scaleT1[:], in0=modT[:, 8:16], scalar=1.0, in1=badaT[:, 8:16],
                                                           op0=ALU.add, op1=ALU.add),
                   R=[b("modT"), b("badaT")], W=[b("scaleT1")])
                for n2 in range(2):
                    op("dve", lambda e, n2=n2: e.tensor_tensor(out=gate_bc[:, n2 * 512:(n2 + 1) * 512], in0=PB[1 + n2][:, :],
                                                               in1=bgate_bc[:, n2 * 512:(n2 + 1) * 512], op=ALU.add),
                       R=[b("PB%d" % (1 + n2))] + bl(EW), W=bl(EW))
                for kc in range(8):
                    stg = stage[kc % 2]
                    SN = bl(STG[kc % 2])
                    dma("sp", "st%d" % (kc % 2), stg[:, 0:D], wout_d[l, kc * 128:(kc + 1) * 128, :], W=SN)
                    op(("dve", "pool")[kc % 2], lambda e, kc=kc, stg=stg: e.tensor_tensor(
                        out=w_out_bf[:, kc, :], in0=stg[:, 0:D], in1=gate_bc[:], op=ALU.mult),
                       R=SN + bl(EW), W=[b("w_out_bf")])
                ntl = [(i * 512, min(512, DIN - i * 512)) for i in range(7)]
                for kc in range(8):
                    stg = stage[kc % 2]
                    SN = bl(STG[kc % 2])
                    dma("sp", "st%d" % (kc % 2), stg[:, :], win_d[l, kc * 128:(kc + 1) * 128, :], W=SN)
                    for nt, (cs, cw) in enumerate(ntl):
                        op("pe", lambda e, nt=nt, cs=cs, cw=cw, kc=kc, stg=stg: e.matmul(
                            PB[nt][0:2, 0:cw], lhsT=shiftT2[:, kc, :], rhs=stg[:, cs:cs + cw],
                            start=(kc == 0), stop=(kc == 7)),
                           R=SN + [b("shiftT2")], W=[b("PB%d" % nt)])
                    for pi, (s0, d0, wd) in enumerate(PERM):
                        en = ("dve", "pool", "act")[pi % 3] if wd >= 512 else "pool"
                        if en == "act":
                            op("act", lambda e, s0=s0, d0=d0, wd=wd, kc=kc, stg=stg: e.activation(
                                out=w_in_bf[:, kc, d0:d0 + wd], in_=stg[:, s0:s0 + wd], func=AF.Copy, scale=scaleT1[:, kc:kc + 1]),
                               R=SN + [b("scaleT1")], W=[b("w_in_bf")])
                        else:
                            op(en, lambda e, s0=s0, d0=d0, wd=wd, kc=kc, stg=stg: e.tensor_scalar(
                                out=w_in_bf[:, kc, d0:d0 + wd], in0=stg[:, s0:s0 + wd], scalar1=scaleT1[:, kc:kc + 1],
                                scalar2=None, op0=ALU.mult),
                               R=SN + [b("scaleT1")], W=[b("w_in_bf")])
                for nt, (cs, cw) in enumerate(ntl):
                    op("dve", lambda e, nt=nt, cs=cs, cw=cw: e.tensor_copy(out=c0src[:, cs:cs + cw], in_=PB[nt][0:1, 0:cw]),
                       R=[b("PB%d" % nt)], W=[b("score")])
                for (s0, d0, wd) in PERM:
                    op("dve", lambda e, s0=s0, d0=d0, wd=wd: e.tensor_copy(out=c0dst[0:1, d0:d0 + wd], in_=c0src[:, s0:s0 + wd]),
                       R=[b("score")], W=[b("score")])
                op("dve", lambda e: e.tensor_copy(out=c0hi[0:1, :], in_=c0dst[0:1, :]), R=[b("score")], W=bl(ATT))
                op("dve", lambda e: e.tensor_copy(out=rows[0:1, :], in_=c0hi[0:1, :]), R=bl(ATT), W=[b("rows")])
                op("dve", lambda e: e.tensor_tensor(out=c0src[:, :], in0=c0dst[0:1, :], in1=c0hi[0:1, :], op=ALU.subtract),
                   R=[b("score")] + bl(ATT), W=[b("score")])
                op("dve", lambda e: e.tensor_copy(out=rows[32:33, :], in_=c0src[:, :]), R=[b("score")], W=[b("rows")])
                dma("sp", "ld0", wsp_f[:], wsp_d[l].rearrange("g t s -> t g s"), W=bl(EW))
                op("pool", lambda e: e.affine_select(out=wsp_f[:], in_=wsp_f[:], pattern=[[0, 8], [-1, 128]],
                                                     compare_op=ALU.is_ge, fill=zero_reg, base=0, channel_multiplier=1),
                   R=bl(EW), W=bl(EW))
                op("pool", lambda e: e.tensor_copy(out=wsp_b[:], in_=wsp_f[:]), R=bl(EW), W=[b("Z")])
                for g in range(8):
                    op("pe", lambda e, g=g: e.transpose(PT[:, g * 128:(g + 1) * 128], wsp_b[:, g, :], ident_b[:]),
                       R=[b("Z"), b("ident_b")], W=[b("PT")])
                op("dve", lambda e: e.tensor_copy(out=WT[:], in_=PT[:, :].rearrange("p (g t) -> p g t", g=8)),
                   R=[b("PT")], W=[b("WT")])
                dma("sp", "ld1", small[0:8, :], bsp_d[l, :, :], W=[b("small")])
                op("pe", lambda e: e.transpose(PB[0][:, 0:8], small[0:8, :], ident_f[0:8, 0:8]),
                   R=[b("small"), b("ident_f")], W=[b("PB0")])
                op("dve", lambda e: e.tensor_copy(out=bsT[:], in_=PB[0][:, 0:8]), R=[b("PB0")], W=[b("bsT")])

            def rope(src4, dst_lo, dst_hi, nh, blk):
                cb = cosT[:, blk:blk + 1, :].to_broadcast([128, nh, 32])
                sbb = sinT[:, blk:blk + 1, :].to_broadcast([128, nh, 32])
                x1, x2 = src4[:, :, 0, :], src4[:, :, 1, :]
                return cb, sbb, x1, x2

            def ln_stats(src_ap_halves, srcbufs, C=False):
                st_, mv_, rs_ = (st12C, mvC, rstdC) if C else (st12, mv, rstd)
                sn, mn, rsn = ("st12C", "mvC", "rstdC") if C else ("st12", "mv", "rstd")
                for hh, ap in enumerate(src_ap_halves):
                    op("dve", lambda e, hh=hh, ap=ap: e.bn_stats(out=st_[:, hh * 6:(hh + 1) * 6], in_=ap),
                       R=srcbufs, W=[b(sn)])
                nst = 6 * len(src_ap_halves)
                op("dve", lambda e: e.bn_aggr(out=mv_[:], in_=st_[:, 0:nst]), R=[b(sn)], W=[b(mn)])
                op("dve", lambda e: e.tensor_scalar(out=rs_[:], in0=mv_[:, 1:2], scalar1=LN_EPS, scalar2=None, op0=ALU.add),
                   R=[b(mn)], W=[b(rsn)])
                op("act", lambda e: e.activation(out=rs_[:], in_=rs_[:], func=AF.Sqrt), R=[b(rsn)], W=[b(rsn)])
                op("dve", lambda e: e.reciprocal(out=rs_[:], in_=rs_[:]), R=[b(rsn)], W=[b(rsn)])

            def kkn(blks):
                return [b("KK%d" % j) for j in blks]

            def stageAB(l, i, src_d):
                n = (i + 1) * 128
                p2 = i % 2
                xb, xbn = xblk[p2], "xblk%d" % p2
                qq, qqn = QQ[p2], "QQ%d" % p2
                szb_, szbn = szb[p2], "szb%d" % p2
                yb, ybn = y_bf[p2], "y_bf%d" % p2
                Mb, Mbn = Mbuf[p2], "Mbuf%d" % p2
                dma("sp", "xq%d" % p2, xb, src_d[i * 128:(i + 1) * 128, :], W=[b(xbn)])
                ln_stats([xb[:, 0:512], xb[:, 512:1024]], [b(xbn)])
                op("dve", lambda e: e.tensor_scalar(out=xn, in0=xb, scalar1=mv[:, 0:1], scalar2=rstd[:],
                                                    op0=ALU.subtract, op1=ALU.mult),
                   R=[b(xbn), b("mv"), b("rstd")], W=[b("Z")])
                for kc in range(8):
                    op("pe", lambda e, kc=kc: e.transpose(PT[:, kc * 128:(kc + 1) * 128], xn[:, kc * 128:(kc + 1) * 128], ident_b[:]),
                       R=[b("Z"), b("ident_b")], W=[b("PT")])
                op("act", lambda e: e.copy(out=xnT[:], in_=PT[:, :]), R=[b("PT")], W=[b("xnT")])
                yield 4.0

                def proj_tile(nt, cs, cw):
                    pb = PB[nt % 2]
                    pbn = "PB%d" % (nt % 2)
                    for kc in range(8):
                        op("pe", lambda e, kc=kc: e.matmul(pb[:, 0:cw], lhsT=xnT[:, kc * 128:(kc + 1) * 128],
                                                           rhs=w_in_bf[:, kc, cs:cs + cw], start=(kc == 0), stop=False),
                           R=[b("xnT"), b("w_in_bf")], W=[b(pbn)])
                    op("pe", lambda e: e.matmul(pb[:, 0:cw], lhsT=ones33[:, :], rhs=rows[:, cs:cs + cw], start=False, stop=True),
                       R=[b("ones33"), b("rows")], W=[b(pbn)])
                    return pb, pbn

                pb, pbn = proj_tile(0, C_U, 512)
                op("act", lambda e: e.copy(out=u_sb, in_=pb[:, :]), R=[b(pbn)], W=[b("u_sb")])
                yield 3.0
                pb, pbn = proj_tile(1, C_V, 512)
                ln_stats([pb[:, 0:512]], [b(pbn)])
                op("dve", lambda e: e.tensor_scalar(out=vn0, in0=pb[:, :], scalar1=mv[:, 0:1], scalar2=rstd[:],
                                                    op0=ALU.subtract, op1=ALU.mult),
                   R=[b(pbn), b("mv"), b("rstd")], W=[b("vn0")])
                op("pool", lambda e: e.tensor_tensor(out=vn0, in0=vn0, in1=vg_bc[:], op=ALU.mult),
                   R=[b("vn0"), b("vg_bc")], W=[b("vn0")])
                op("pool", lambda e: e.tensor_tensor(out=vn_bf[:], in0=vn0, in1=vb_bc[:], op=ALU.add),
                   R=[b("vn0"), b("vb_bc")], W=[b("vn_bf")])
                yield 4.0
                pb, pbn = proj_tile(2, C_ZA, 512)
                op("act", lambda e: e.activation(out=sza, in_=pb[:, :], func=AF.Silu), R=[b(pbn)], W=[b("sza")])
                for g in range(8):
                    op("pe", lambda e, g=g: e.matmul(PB[2][:, g * 64:(g + 1) * 64], lhsT=WT[:, g, :], rhs=vn_bf[:, g * 64:(g + 1) * 64],
                                                     start=True, stop=True),
                       R=[b("WT"), b("vn_bf")], W=[b("PB2")])
                op("dve", lambda e: e.tensor_tensor(out=t1.rearrange("p (g d) -> p g d", g=8),
                                                    in0=PB[2][:, :].rearrange("p (g d) -> p g d", g=8),
                                                    in1=bsT[:, :, None].to_broadcast([128, 8, 64]), op=ALU.add),
                   R=[b("PB2"), b("bsT")], W=[b("t1")])
                op("pool", lambda e: e.tensor_tensor(out=t1, in0=t1, in1=u_sb, op=ALU.mult),
                   R=[b("t1"), b("u_sb")], W=[b("t1")])
                op("pool", lambda e: e.tensor_tensor(out=yb[:, 0:512], in0=t1, in1=sza, op=ALU.mult),
                   R=[b("t1"), b("sza")], W=[b(ybn)])
                yield 4.0
                for (nt, cs, off) in ((3, C_Q, 0), (4, C_QI, 64)):
                    pb, pbn = proj_tile(nt, cs, 512)
                    s4 = pb[:, :].rearrange("p (h a d) -> p h a d", h=8, a=2)
                    cb, sbb, x1, x2 = rope(s4, None, None, 8, i)
                    op("dve", lambda e: e.tensor_tensor(out=ra[:], in0=x1, in1=cb, op=ALU.mult), R=[b(pbn), b("cosT")], W=[b("ra")])
                    op("dve", lambda e: e.tensor_tensor(out=rb[:], in0=x2, in1=sbb, op=ALU.mult), R=[b(pbn), b("sinT")], W=[b("rb")])
                    op("dve", lambda e: e.tensor_tensor(out=rc[:], in0=x2, in1=cb, op=ALU.mult), R=[b(pbn), b("cosT")], W=[b("rc")])
                    op("dve", lambda e: e.tensor_tensor(out=rd[:], in0=x1, in1=sbb, op=ALU.mult), R=[b(pbn), b("sinT")], W=[b("rd")])
                    op("pool", lambda e, off=off: e.tensor_tensor(out=Z[:, :, off:off + 32], in0=ra[:], in1=rb[:], op=ALU.subtract),
                       R=[b("ra"), b("rb")], W=[b("Z")])
                    op("pool", lambda e, off=off: e.tensor_tensor(out=Z[:, :, off + 32:off + 64], in0=rc[:], in1=rd[:], op=ALU.add),
                       R=[b("rc"), b("rd")], W=[b("Z")])
                    yield 4.0
                pb, pbn = proj_tile(5, C_ZB, 512)
                op("act", lambda e: e.activation(out=szb_, in_=pb[:, :], func=AF.Silu), R=[b(pbn)], W=[b(szbn)])
                yield 3.0
                pb, pbn = proj_tile(6, C_K, 200)
                s4 = pb[:, 0:128].rearrange("p (h a d) -> p h a d", h=2, a=2)
                cb, sbb, x1, x2 = rope(s4, None, None, 2, i)
                op("dve", lambda e: e.tensor_tensor(out=ra[:, 0:2, :], in0=x1, in1=cb, op=ALU.mult), R=[b(pbn), b("cosT")], W=[b("ra")])
                op("dve", lambda e: e.tensor_tensor(out=rb[:, 0:2, :], in0=x2, in1=sbb, op=ALU.mult), R=[b(pbn), b("sinT")], W=[b("rb")])
                op("dve", lambda e: e.tensor_tensor(out=rc[:, 0:2, :], in0=x2, in1=cb, op=ALU.mult), R=[b(pbn), b("cosT")], W=[b("rc")])
                op("dve", lambda e: e.tensor_tensor(out=rd[:, 0:2, :], in0=x1, in1=sbb, op=ALU.mult), R=[b(pbn), b("sinT")], W=[b("rd")])
                kz3 = KZ[:, :].rearrange("p (h d) -> p h d", h=2)
                op("pool", lambda e: e.tensor_tensor(out=kz3[:, :, 0:32], in0=ra[:, 0:2, :], in1=rb[:, 0:2, :], op=ALU.subtract),
                   R=[b("ra"), b("rb")], W=[b("KZ")])
                op("pool", lambda e: e.tensor_tensor(out=kz3[:, :, 32:64], in0=rc[:, 0:2, :], in1=rd[:, 0:2, :], op=ALU.add),
                   R=[b("rc"), b("rd")], W=[b("KZ")])
                op("act", lambda e: e.copy(out=Vst[:, i, 0:64], in_=pb[:, 128:192]), R=[b(pbn)], W=[b("V%d" % i)])
                op("dve", lambda e: e.tensor_copy(out=w8[:], in_=pb[:, 192:200]), R=[b(pbn)], W=[b("w8")])
                op("dve", lambda e: e.tensor_tensor(out=diagW[:], in0=ident_b[:, None, :].to_broadcast([128, 8, 128]),
                                                    in1=w8[:, :, None].to_broadcast([128, 8, 128]), op=ALU.mult),
                   R=[b("ident_b"), b("w8")], W=[b("diagW")])
                op("pe", lambda e: e.transpose(PT[:, 0:128], KZ[:, :], ident_b[:]), R=[b("KZ"), b("ident_b")], W=[b("PT")])
                op("act", lambda e: e.copy(out=KK[:, i * 128:(i + 1) * 128], in_=PT[:, 0:128]), R=[b("PT")], W=[b("KK%d" % i)])
                for h in range(8):
                    op("pe", lambda e, h=h: e.transpose(PT[:, h * 128:(h + 1) * 128], Z[:, h, :], ident_b[:]),
                       R=[b("Z"), b("ident_b")], W=[b("PT")])
                op("act", lambda e: e.copy(out=qq[:], in_=PT[:, :].rearrange("p (h t) -> p h t", h=8)), R=[b("PT")], W=[b(qqn)])
                yield 5.0

                nch = (n + 511) // 512
                steps = [(c, h) for c in range(nch) for h in range(8)]

                def ix_mm1(k):
                    c, h = steps[k]
                    cw = min(512, n - c * 512)
                    pb, pbn = PB[k % 2], "PB%d" % (k % 2)
                    op("pe", lambda e: e.matmul(pb[:, 0:cw], lhsT=qq[64:128, h, :], rhs=KK[64:128, c * 512:c * 512 + cw],
                                                start=True, stop=True),
                       R=[b(qqn)] + kkn(range(4 * c, 4 * c + cw // 128)), W=[b(pbn)])

                def ix_rest(k):
                    c, h = steps[k]
                    cw = min(512, n - c * 512)
                    pb, pbn = PB[k % 2], "PB%d" % (k % 2)
                    rr, rrn = Rr[k % 4], "Rr%d" % (k % 4)
                    if h % 4 != 3:
                        op("act", lambda e: e.activation(out=rr[:, 0:cw], in_=pb[:, 0:cw], func=AF.Relu), R=[b(pbn)], W=[b(rrn)])
                    else:
                        op("dve", lambda e: e.tensor_scalar(out=rr[:, 0:cw], in0=pb[:, 0:cw], scalar1=0.0, scalar2=None, op0=ALU.max),
                           R=[b(pbn)], W=[b(rrn)])
                    op("pe", lambda e: e.matmul(PB[2][:, 0:cw], lhsT=diagW[:, h, :], rhs=rr[:, 0:cw], start=(h == 0), stop=(h == 7)),
                       R=[b("diagW"), b(rrn)], W=[b("PB2")])
                    if h == 7:
                        op("dve", lambda e: e.tensor_scalar(out=score[:, c * 512:c * 512 + cw], in0=PB[2][:, 0:cw], scalar1=1.0, scalar2=-3.0e38,
                                                            op0=ALU.mult, op1=ALU.max, accum_out=cmax[:, c:c + 1]),
                           R=[b("PB2")], W=[b("score"), b("cmax")])
                        op("dve", lambda e: e.tensor_scalar(out=Mb[:, c * 512:c * 512 + cw], in0=score[:, c * 512:c * 512 + cw], scalar1=1.0,
                                                            scalar2=3.0e38, op0=ALU.mult, op1=ALU.min, accum_out=cmin[:, c:c + 1]),
                           R=[b("score")], W=[b(Mbn), b("cmin")])

                ix_mm1(0)
                for k in range(len(steps)):
                    if k + 1 < len(steps):
                        ix_mm1(k + 1)
                    ix_rest(k)
                    yield 0.7
                op("dve", lambda e: e.tensor_reduce(out=rmax[:], in_=cmax[:, 0:nch], axis=AX.X, op=ALU.max), R=[b("cmax")], W=[b("rmax")])
                op("dve", lambda e: e.tensor_reduce(out=rmin[:], in_=cmin[:, 0:nch], axis=AX.X, op=ALU.min), R=[b("cmin")], W=[b("rmin")])
                op("dve", lambda e: e.tensor_tensor(out=w0[:], in0=rmax[:], in1=rmin[:], op=ALU.subtract),
                   R=[b("rmax"), b("rmin")], W=[b("w0")])
                op("dve", lambda e: e.tensor_scalar(out=w0[:], in0=w0[:], scalar1=1.001, scalar2=1.0e-6, op0=ALU.mult, op1=ALU.add),
                   R=[b("w0")], W=[b("w0")])
                op("dve", lambda e: e.tensor_scalar(out=HWt[:], in0=pow2[:], scalar1=w0[:], scalar2=None, op0=ALU.mult),
                   R=[b("pow2"), b("w0")], W=[b("HWt")])
                op("dve", lambda e: e.tensor_tensor(out=mid[0][:], in0=rmin[:], in1=HWt[:, 0:1], op=ALU.add),
                   R=[b("rmin"), b("HWt")], W=[b("mid0")])
                op("pool", lambda e: e.affine_select(out=score[:, i * 128:(i + 1) * 128], in_=score[:, i * 128:(i + 1) * 128],
                                                     pattern=[[-1, 128]], compare_op=ALU.is_ge, fill=neg_reg, base=0, channel_multiplier=1),
                   R=[b("score")], W=[b("score")])
                yield 1.5
                yield -1.0
                for k in range(KITER):
                    mc, mn = mid[k % 2], mid[(k + 1) % 2]
                    mcn, mnn = "mid%d" % (k % 2), "mid%d" % ((k + 1) % 2)
                    op("dve", lambda e: e.tensor_scalar(out=Mb[:, 0:n], in0=score[:, 0:n], scalar1=mc[:], scalar2=0.0,
                                                        op0=ALU.is_ge, op1=ALU.add, accum_out=cnt[:]),
                       R=[b("score"), b(mcn)], W=[b(Mbn), b("cnt")])
                    op("dve", lambda e: e.tensor_scalar(out=stp[:], in0=cnt[:], scalar1=float(TOP) - 0.5, scalar2=0.5,
                                                        op0=ALU.is_ge, op1=ALU.subtract), R=[b("cnt")], W=[b("stp")])
                    op("dve", lambda e: e.scalar_tensor_tensor(out=mn[:], in0=stp[:], scalar=HWt[:, k:k + 1], in1=mc[:],
                                                               op0=ALU.mult, op1=ALU.add),
                       R=[b("stp"), b("HWt"), b(mcn)], W=[b(mnn)])
                    yield 0.5 + n / 1900.0
                mfin, mfinn = mid[KITER % 2], "mid%d" % (KITER % 2)
                op("dve", lambda e: e.tensor_tensor(out=tau[:], in0=mfin[:], in1=HWt[:, KITER:KITER + 1], op=ALU.subtract),
                   R=[b(mfinn), b("HWt")], W=[b("tau")])
                op("dve", lambda e: e.tensor_scalar(out=Mb[:, 0:n], in0=score[:, 0:n], scalar1=tau[:], scalar2=None, op0=ALU.is_ge),
                   R=[b("score"), b("tau")], W=[b(Mbn)])
                yield 0.5 + n / 1900.0

            def stageC(l, i, dst_d):
                n = (i + 1) * 128
                p2 = i % 2
                xb, xbn = xblk[p2], "xblk%d" % p2
                qq, qqn = QQ[p2], "QQ%d" % p2
                szb_, szbn = szb[p2], "szb%d" % p2
                yb, ybn = y_bf[p2], "y_bf%d" % p2
                Mb, Mbn = Mbuf[p2], "Mbuf%d" % p2
                nch = (n + 511) // 512

                def at_prep(c):
                    cw = min(512, n - c * 512)
                    nb = cw // 128
                    mt, mtn = MT[c % 2], "MT%d" % (c % 2)
                    for jj in range(nb):
                        op("pe", lambda e, jj=jj: e.transpose(PT[:, jj * 128:(jj + 1) * 128],
                                                              Mb[:, c * 512 + jj * 128:c * 512 + (jj + 1) * 128], ident_b[:]),
                           R=[b(Mbn), b("ident_b")], W=[b("PT")])
                    op("act", lambda e: e.activation(out=mt[:, 0:nb, :], in_=PT[:, 0:cw].rearrange("p (j t) -> p j t", j=nb),
                                                     func=AF.Identity, scale=30000.0, bias=-30000.0),
                       R=[b("PT")], W=[b(mtn)])

                def at_st(j):
                    ee, een = Ee[j % 2], "Ee%d" % (j % 2)
                    mt, mtn = MT[(j // 4) % 2], "MT%d" % ((j // 4) % 2)
                    jj = j % 4
                    for hf in range(2):
                        op("pe", lambda e, hf=hf: e.matmul(PB[3 + hf][:, :], lhsT=KK[0:64, j * 128:(j + 1) * 128],
                                                           rhs=qq[0:64, 4 * hf:4 * hf + 4, :], start=True, stop=False),
                           R=[b("KK%d" % j), b(qqn)], W=[b("PB%d" % (3 + hf))])
                        op("pe", lambda e, hf=hf: e.matmul(PB[3 + hf][:, :], lhsT=ident_b[:, :],
                                                           rhs=mt[:, jj:jj + 1, :].to_broadcast([128, 4, 128]), start=False, stop=True),
                           R=[b("ident_b"), b(mtn)], W=[b("PB%d" % (3 + hf))])
                        op("act", lambda e, hf=hf: e.activation(out=ee[:, 4 * hf:4 * hf + 4, :],
                                                                in_=PB[3 + hf][:, :].rearrange("p (h t) -> p h t", h=4),
                                                                func=AF.Exp, scale=0.125),
                           R=[b("PB%d" % (3 + hf))], W=[b(een)])

                def at_pv(j):
                    c, jj = j // 4, j % 4
                    ee, een = Ee[j % 2], "Ee%d" % (j % 2)
                    mt, mtn = MT[c % 2], "MT%d" % (c % 2)
                    for h in range(8):
                        ob_ = PB[5 + h // 4]
                        op("pe", lambda e, h=h, ob_=ob_: e.matmul(
                            ob_[:, (h % 4) * 65:(h % 4) * 65 + 65], lhsT=ee[:, h, :], rhs=Vst[:, j, :],
                            start=(j == 0 and h % 4 == 0), stop=(j == i), skip_group_check=True),
                           R=[b(een), b("V%d" % j)], W=[b("PB%d" % (5 + h // 4))])

                at_prep(0)
                at_st(0)
                yield 1.5
                for j in range(i + 1):
                    if j % 4 == 0 and j // 4 + 1 < nch:
                        at_prep(j // 4 + 1)
                    if j + 1 <= i:
                        at_st(j + 1)
                    at_pv(j)
                    yield 2.0
                for hf in range(2):
                    o3 = PB[5 + hf][:, 0:260].rearrange("p (h d) -> p h d", h=4)
                    op("dve", lambda e, hf=hf, o3=o3: e.reciprocal(out=rden[:, 4 * hf:4 * hf + 4], in_=o3[:, :, 64]),
                       R=[b("PB%d" % (5 + hf))], W=[b("rden")])
                    op("dve", lambda e, hf=hf, o3=o3: e.tensor_tensor(out=yb0[:, 4 * hf:4 * hf + 4, :], in0=o3[:, :, 0:64],
                                                                      in1=rden[:, 4 * hf:4 * hf + 4, None].to_broadcast([128, 4, 64]),
                                                                      op=ALU.mult),
                       R=[b("PB%d" % (5 + hf)), b("rden")], W=[b("res")])
                op("dve", lambda e: e.tensor_tensor(out=yb[:, 512:1024], in0=res[:, 0:512], in1=szb_, op=ALU.mult),
                   R=[b("res"), b(szbn)], W=[b(ybn)])
                yield 2.0
                for kc in range(8):
                    op("pe", lambda e, kc=kc: e.transpose(PT[:, kc * 128:(kc + 1) * 128], yb[:, kc * 128:(kc + 1) * 128], ident_b[:]),
                       R=[b(ybn), b("ident_b")], W=[b("PT")])
                op("act", lambda e: e.copy(out=yT[:], in_=PT[:, :]), R=[b("PT")], W=[b("yT")])
                yield 2.0
                for nt in range(2):
                    for kc in range(8):
                        op("pe", lambda e, kc=kc, nt=nt: e.matmul(PB[3 + nt][:, :], lhsT=yT[:, kc * 128:(kc + 1) * 128],
                                                                  rhs=w_out_bf[:, kc, nt * 512:(nt + 1) * 512], start=(kc == 0), stop=(kc == 7)),
                           R=[b("yT"), b("w_out_bf")], W=[b("PB%d" % (3 + nt))])
                    op("dve", lambda e, nt=nt: e.scalar_tensor_tensor(out=res[:, nt * 512:(nt + 1) * 512], in0=xb[:, nt * 512:(nt + 1) * 512],
                                                                      scalar=ALPHA, in1=PB[3 + nt][:, :], op0=ALU.mult, op1=ALU.add),
                       R=[b(xbn), b("PB%d" % (3 + nt))], W=[b("res")])
                    yield 2.5
                ln_stats([res[:, 0:512], res[:, 512:1024]], [b("res")], C=True)
                op("dve", lambda e: e.tensor_scalar(out=rn, in0=res, scalar1=mvC[:, 0:1], scalar2=rstdC[:],
                                                    op0=ALU.subtract, op1=ALU.mult),
                   R=[b("res"), b("mvC"), b("rstdC")], W=[b("rn")])
                op("dve", lambda e: e.tensor_tensor(out=res, in0=rn, in1=lng_bc[:], op=ALU.mult),
                   R=[b("rn"), b("lng_bc")], W=[b("res")])
                op("dve", lambda e: e.tensor_tensor(out=rn, in0=res, in1=lnb_bc[:], op=ALU.add),
                   R=[b("res"), b("lnb_bc")], W=[b("rn")])
                dma("sp", "oq0", dst_d[i * 128:(i + 1) * 128, :], rn, R=[b("rn")], W=[b("hbm_out")])
                yield 4.0

            def dry_costs(mk):
                sch.dry = True
                cs = list(mk())
                sch.dry = False
                return cs

            def run_seq(g):
                for c in g:
                    if c < 0:
                        return True
                return False

            def interleave(gens, tots):
                acc = [0.0] * len(gens)
                live = [True] * len(gens)
                while any(live):
                    k = min((j for j in range(len(gens)) if live[j]), key=lambda j: acc[j] / tots[j])
                    try:
                        c = next(gens[k])
                        if c > 0:
                            acc[k] += c
                    except StopIteration:
                        live[k] = False

            def pipeline_step(mkC, mkAB):
                gAB = mkAB() if mkAB is not None else None
                if _SEQ or gAB is None or mkC is None:
                    if mkC is not None:
                        run_seq(mkC())
                    if gAB is not None:
                        run_seq(gAB)
                        run_seq(gAB)
                    return
                if _MODE == 1:
                    cAB = [c for c in dry_costs(mkAB) if c > 0]
                    cC = dry_costs(mkC)
                    interleave([mkC(), gAB], [max(sum(cC), 1e-6), max(sum(cAB), 1e-6)])
                    return
                cs = dry_costs(mkAB)
                kb = cs.index(-1.0)
                totB = max(sum(cs[kb + 1:]), 1e-6)
                totC = max(sum(dry_costs(mkC)), 1e-6)
                run_seq(gAB)
                interleave([mkC(), gAB], [totC, totB])

            for l in range(NL):
                layer_setup(l)
                src = x_d if l == 0 else x1_d
                dst = x1_d if l == 0 else out_d
                if l == 1:
                    sch.wait_all("sp", "oq0")
                pipeline_step(None, lambda: stageAB(l, 0, src))
                for i in range(1, NB):
                    pipeline_step(lambda: stageC(l, i - 1, dst), lambda: stageAB(l, i, src))
                pipeline_step(lambda: stageC(l, NB - 1, dst), None)
            sch.engs["sp"].waited.pop("oq0", None)
            sch.wait_all("sp", "oq0")
    return nc


def _consts():
    ident = np.eye(128, dtype=np.float32)
    invf = (np.float32(10000.0) ** (-np.arange(0, 64, 2, dtype=np.float32) / np.float32(64))).astype(np.float32)
    invf = np.ascontiguousarray(np.broadcast_to(invf[None, :], (128, 32))).astype(np.float32)
    pow2 = np.ascontiguousarray(np.broadcast_to((2.0 ** -(np.arange(KITER + 1) + 1.0))[None, :], (128, KITER + 1))).astype(np.float32)
    return ident, invf, pow2


def make_in_maps(NB, x, c, positions, w_ada, b_ada, w_in, v_norm_g, v_norm_b, w_spatial, b_spatial, w_out, ln_g, ln_b, ncores):
    S = NB * 128
    ident, invf, pow2 = _consts()
    f = lambda a: np.ascontiguousarray(np.asarray(a, dtype=np.float32))
    shared = {
        "w_ada": f(w_ada), "b_ada": f(b_ada).reshape(NL, 24, 128), "w_in": f(w_in),
        "v_norm_g": f(v_norm_g), "v_norm_b": f(v_norm_b), "w_spatial": f(w_spatial), "b_spatial": f(b_spatial),
        "w_out": f(w_out), "ln_g": f(ln_g), "ln_b": f(ln_b), "ident": ident, "invf": invf, "pow2": pow2,
    }
    maps = []
    for bi in range(ncores):
        m = dict(shared)
        m["x"] = f(x[bi, :S])
        m["c"] = f(c[bi]).reshape(8, 128)
        m["pos"] = np.ascontiguousarray(np.asarray(positions[bi, :S], dtype=np.int32)).reshape(NB, 128)
        maps.append(m)
    return maps


_NC_CACHE = {}


def kernel(x, c, positions, w_ada, b_ada, w_in, v_norm_g, v_norm_b, w_spatial, b_spatial, w_out, ln_g, ln_b):
    x = np.asarray(x)
    Bn, S, _ = x.shape
    NB = S // 128
    if NB not in _NC_CACHE:
        _NC_CACHE[NB] = build(NB)
    nc = _NC_CACHE[NB]
    maps = make_in_maps(NB, x, c, positions, w_ada, b_ada, w_in, v_norm_g, v_norm_b, w_spatial, b_spatial, w_out, ln_g, ln_b, Bn)
    res = run_bass_kernel_spmd(nc, maps, core_ids=list(range(Bn)))
    out = np.stack([np.asarray(r["out"]) for r in res.results], axis=0).astype(np.float32)
    return out
```

```python
import numpy as np
from contextlib import ExitStack
import concourse.bass as bass
import concourse.mybir as mybir
from concourse.bass_utils import run_bass_kernel_spmd

F32 = mybir.dt.float32
BF16 = mybir.dt.bfloat16
I32 = mybir.dt.int32
ALU = mybir.AluOpType
AF = mybir.ActivationFunctionType
AX = mybir.AxisListType

D = 1024
DIN = 3272
NL = 2
NCORES = 8
LN_EPS = 1e-5
ALPHA = (2.0 * NL) ** 0.25
KITER = 20
import os as _os
_SEQ = _os.environ.get('KERNEL_SEQ', '0') == '1'
_PSER = int(_os.environ.get('KERNEL_PSER', '3'))
_ALLINC = _os.environ.get('KERNEL_ALLINC', '0') == '1'
_MODE = int(_os.environ.get('KERNEL_MODE', '0'))
NEG = -1.0e30

C_U, C_V, C_ZA, C_Q, C_QI, C_ZB, C_K, C_KI, C_VAL, C_W = 0, 512, 1024, 1536, 2048, 2560, 3072, 3136, 3200, 3264
PERM = [(0, 0, 2048), (2688, C_QI, 512), (2176, C_ZB, 512), (2048, C_K, 64),
        (3200, C_KI, 64), (2112, C_VAL, 64), (3264, C_W, 8)]


class Eng:
    def __init__(self, name, eng, sem):
        self.name, self.eng, self.sem = name, eng, sem
        self.cnt = 0
        self.waited = {}
        self.real = 0
        self.rmap = {}


class Buf:
    __slots__ = ("name", "w", "r")

    def __init__(self, name):
        self.name = name
        self.w = None
        self.r = {}


class Sch:
    def __init__(self, nc, es):
        self.nc = nc
        self.es = es
        self.engs = {}
        for name, eng in (("pe", nc.tensor), ("act", nc.scalar), ("dve", nc.vector),
                          ("pool", nc.gpsimd), ("sp", nc.sync)):
            self.engs[name] = Eng(name, eng, es.enter_context(nc.semaphore("s_" + name)))
        self.dq = {}
        self.dry = False
        self.mode = "plan"
        self.needed = set()
        self.B = {}

    def reset(self, mode):
        self.mode = mode
        self.B = {}
        for E in list(self.engs.values()) + list(self.dq.values()):
            E.cnt = 0
            E.waited = {}
            E.real = 0
            E.rmap = {}

    def buf(self, name):
        if name not in self.B:
            self.B[name] = Buf(name)
        return self.B[name]

    def dmaq(self, name):
        if name not in self.dq:
            q = Eng(name, None, self.es.enter_context(self.nc.semaphore("q_" + name)))
            q.real = 0
            q.rmap = {}
            self.dq[name] = q
        return self.dq[name]

    def _need(self, E, deps, dep):
        e, c = dep
        if deps.get(e.name, (None, 0))[1] < c:
            deps[e.name] = (e, c)

    def _deps(self, E, R, W):
        deps = {}
        for b in R:
            if b.w is not None:
                self._need(E, deps, b.w)
        for b in W:
            if b.w is not None and b.w[0] is not E:
                self._need(E, deps, b.w)
            for (e, c) in b.r.values():
                if e is not E:
                    self._need(E, deps, (e, c))
        return deps

    def _emit_waits(self, E, deps):
        for (e, c) in deps.values():
            if E.waited.get(e.name, 0) >= c:
                continue
            if self.mode == "plan":
                self.needed.add((e.name, c))
            else:
                E.eng.wait_ge(e.sem, e.rmap[c])
            E.waited[e.name] = c

    def wait_all(self, ename, qname):
        E = self.engs[ename]
        Q = self.dmaq(qname)
        if Q.cnt > 0:
            self._emit_waits(E, {Q.name: (Q, Q.cnt)})

    def op(self, ename, fn, R=(), W=()):
        if self.dry:
            return
        E = self.engs[ename]
        if _PSER:
            ps = [x for x in list(R) + list(W) if x.name.startswith("PB") or x.name == "PT"]
            if ps and _PSER == 1:
                W = list(W) + [self.buf("PSUM_ALL")]
            elif ps and _PSER == 2 and ename != "pe":
                W = list(W) + [self.buf("PSUM_RD")]
            elif ps and _PSER == 3 and ename != "pe":
                W = list(W) + [self.buf("RD_" + x.name) for x in ps]
        self._emit_waits(E, self._deps(E, R, W))
        E.cnt += 1
        if self.mode == "emit":
            inst = fn(E.eng)
            if _ALLINC or (E.name, E.cnt) in self.needed:
                E.real += 1
                E.rmap[E.cnt] = E.real
                inst.then_inc(E.sem, 1)
        for b in R:
            b.r[E.name] = (E, E.cnt)
        for b in W:
            b.w = (E, E.cnt)
            b.r = {}

    def dma(self, issuer, qname, out, in_, R=(), W=()):
        if self.dry:
            return
        E = self.engs[issuer]
        Q = self.dmaq(qname)
        deps = self._deps(Q, R, W)
        if Q.cnt > 0:
            deps[Q.name] = (Q, Q.cnt)
        self._emit_waits(E, deps)
        Q.cnt += 1
        if self.mode == "emit":
            Q.real += 16
            Q.rmap[Q.cnt] = Q.real
            E.eng.dma_start(out=out, in_=in_).then_inc(Q.sem, 16)
        for b in R:
            b.r[Q.name] = (Q, Q.cnt)
        for b in W:
            b.w = (Q, Q.cnt)
            b.r = {}


def build(NB, dbg=False):
    S = NB * 128
    TOP = min(256, S // 4)
    nc = bass.Bass("TRN2", target_bir_lowering=False)

    def din(name, shape, dt=F32):
        return nc.dram_tensor(name, list(shape), dt, kind="ExternalInput").ap()

    x_d = din("x", [S, D])
    c_d = din("c", [8, 128])
    pos_d = din("pos", [NB, 128], I32)
    wada_d = din("w_ada", [NL, D, 3 * D])
    bada_d = din("b_ada", [NL, 24, 128])
    win_d = din("w_in", [NL, D, DIN])
    vg_d = din("v_norm_g", [NL, 512])
    vb_d = din("v_norm_b", [NL, 512])
    wsp_d = din("w_spatial", [NL, 8, 128, 128])
    bsp_d = din("b_spatial", [NL, 8, 128])
    wout_d = din("w_out", [NL, D, D])
    lng_d = din("ln_g", [NL, D])
    lnb_d = din("ln_b", [NL, D])
    ident_d = din("ident", [128, 128])
    invf_d = din("invf", [128, 32])
    pow2_d = din("pow2", [128, KITER + 1])
    out_d = nc.dram_tensor("out", [S, D], F32, kind="ExternalOutput").ap()
    x1_d = nc.dram_tensor("x1_scratch", [S, D], F32, kind="Internal").ap()

    es = ExitStack()
    with es:
        sch = Sch(nc, es)
        op, dma = sch.op, sch.dma

        def sb(name, shape, dt=F32):
            return es.enter_context(nc.sbuf_tensor("sb_" + name, list(shape), dt))

        w_in_bf = sb("w_in_bf", [128, 8, DIN], BF16)
        w_out_bf = sb("w_out_bf", [128, 8, D], BF16)
        rows = sb("rows", [33, DIN], BF16)
        ones33 = sb("ones33", [33, 128], BF16)
        KK = sb("KK", [128, S], BF16)
        Vst = sb("Vst", [128, NB, 65], BF16)
        WT = sb("WT", [128, 8, 128], BF16)
        bsT = sb("bsT", [128, 8], F32)
        cosT = sb("cosT", [128, NB, 32], F32)
        sinT = sb("sinT", [128, NB, 32], F32)
        vg_bc = sb("vg_bc", [128, 512], F32)
        vb_bc = sb("vb_bc", [128, 512], F32)
        lng_bc = sb("lng_bc", [128, D], F32)
        lnb_bc = sb("lnb_bc", [128, D], F32)
        ident_f = sb("ident_f", [128, 128], F32)
        ident_b = sb("ident_b", [128, 128], BF16)
        pow2 = sb("pow2", [128, KITER + 1], F32)
        condT2 = sb("condT2", [128, 8, 2], F32)
        SW = max(S, DIN)
        io = sb("io", [128, 4096], F32)
        xblk = [io[:, 0:1024], io[:, 1024:2048]]
        res = io[:, 2048:3072]
        rn = io[:, 3072:4096]
        IO = ["xblk0", "xblk1", "res", "rn"]
        ew = sb("ew", [128, 6 * 512], F32)
        u_sb, vn0, sza, szb0, szb1, t1 = [ew[:, k * 512:(k + 1) * 512] for k in range(6)]
        szb = [szb0, szb1]
        EW = ["u_sb", "vn0", "sza", "szb0", "szb1", "t1"]
        att = sb("att", [128, 3072], BF16)
        MT = [att[:, 0:512].rearrange("p (j t) -> p j t", j=4), att[:, 512:1024].rearrange("p (j t) -> p j t", j=4)]
        Ee = [att[:, 1024:2048].rearrange("p (h t) -> p h t", h=8), att[:, 2048:3072].rearrange("p (h t) -> p h t", h=8)]
        Mbuf = [sb("Mbuf%d" % k, [128, SW], BF16) for k in range(2)]
        ATT = ["Mbuf0"]
        score = sb("score", [128, SW], F32)
        xnT = sb("xnT", [128, D], BF16)
        st12 = sb("st12", [128, 12], F32)
        mv = sb("mv", [128, 2], F32)
        rstd = sb("rstd", [128, 1], F32)
        vn_bf = sb("vn_bf", [128, 512], BF16)
        ra = sb("ra", [128, 8, 32], F32)
        rb = sb("rb", [128, 8, 32], F32)
        rc = sb("rc", [128, 8, 32], F32)
        rd = sb("rd", [128, 8, 32], F32)
        Z = sb("Z", [128, 8, 128], BF16)
        xn = Z[:].rearrange("p h d -> p (h d)")
        KZ = sb("KZ", [128, 128], BF16)
        QQ = [sb("QQ%d" % k, [128, 8, 128], BF16) for k in range(2)]
        w8 = sb("w8", [128, 8], F32)
        diagW = sb("diagW", [128, 8, 128], BF16)
        Rr = [sb("Rr%d" % i, [128, 512], BF16) for i in range(4)]
        st12C = sb("st12C", [128, 12], F32)
        mvC = sb("mvC", [128, 2], F32)
        rstdC = sb("rstdC", [128, 1], F32)
        cmax = sb("cmax", [128, 8], F32)
        cmin = sb("cmin", [128, 8], F32)
        rmax = sb("rmax", [128, 1], F32)
        rmin = sb("rmin", [128, 1], F32)
        w0 = sb("w0", [128, 1], F32)
        HWt = sb("HWt", [128, KITER + 1], F32)
        mid = [sb("mid%d" % i, [128, 1], F32) for i in range(2)]
        cnt = sb("cnt", [128, 1], F32)
        stp = sb("stp", [128, 1], F32)
        tau = sb("tau", [128, 1], F32)
        rden = sb("rden", [128, 8], F32)
        yb0 = res[:, 0:512].rearrange("p (h d) -> p h d", h=8)
        y_bf = [sb("y_bf%d" % k, [128, D], BF16) for k in range(2)]
        yT = sb("yT", [128, D], BF16)
        stage = [score[:, 0:DIN], io[:, 0:DIN]]
        STG = [["score"], IO]
        gate_bc = ew[:, 0:1024]
        bgate_bc = ew[:, 1024:2048]
        wsp_f = ew[:, 2048:3072].rearrange("p (g s) -> p g s", g=8)
        wsp_b = Z
        c0src = score[32:33, 0:DIN]
        c0dst = score[0:1, 0:DIN]
        c0hi = Mbuf[0][0:1, 0:DIN]
        ang = ew[:, 0:NB * 32].rearrange("p (n f) -> p n f", f=32)
        ang2 = ew[:, 1024:1024 + NB * 32].rearrange("p (n f) -> p n f", f=32)
        small = sb("small", [32, 128], F32)
        modT = sb("modT", [128, 16], F32)
        badaT = sb("badaT", [128, 24], F32)
        shiftT2 = sb("shiftT2", [128, 8, 2], F32)
        scaleT1 = sb("scaleT1", [128, 8], F32)
        posf = sb("posf", [128, NB], F32)
        posi = sb("posi", [32, 128], I32)
        posr = sb("posr", [32, 128], F32)
        invf = sb("invf", [128, 32], F32)
        negpi = sb("negpi", [128, 1], F32)

        PB = [es.enter_context(nc.psum_tensor("PB%d" % i, [128, 512], F32)) for i in range(7)]
        PT = es.enter_context(nc.psum_tensor("PT", [128, 1024], BF16))

        def b(name):
            return sch.buf(name)

        def bl(names):
            return [b(nm) for nm in names]

        condrep = ew[:, 2048:3072].rearrange("p (k m) -> p k m", k=8)

        neg_reg = nc.gpsimd.to_reg(NEG)
        zero_reg = nc.gpsimd.to_reg(0.0)
        for _pass in ("plan", "emit"):
            sch.reset(_pass)
            dma("sp", "ld0", ident_f[:], ident_d[:, :], W=[b("ident_f")])
            dma("sp", "ld1", invf[:], invf_d[:, :], W=[b("invf")])
            dma("sp", "ld0", pow2[:], pow2_d[:, :], W=[b("pow2")])
            dma("sp", "ld1", small[0:8, :], c_d[:, :], W=[b("small")])
            dma("sp", "ld0", posi[0:NB, :], pos_d[:, :], W=[b("posi")])
            op("dve", lambda e: e.tensor_copy(out=ident_b[:], in_=ident_f[:]), R=[b("ident_f")], W=[b("ident_b")])
            op("pool", lambda e: e.memset(ones33[:], 1.0), W=[b("ones33")])
            op("pool", lambda e: e.memset(rows[:], 0.0), W=[b("rows")])
            op("pool", lambda e: e.memset(Vst[:], 1.0), W=[b("Vst")])

            op("act", lambda e: e.activation(out=small[0:8, :], in_=small[0:8, :], func=AF.Silu),
               R=[b("small")], W=[b("small")])
            op("pe", lambda e: e.transpose(PB[0][:, 0:8], small[0:8, :], ident_f[0:8, 0:8]),
               R=[b("small"), b("ident_f")], W=[b("PB0")])
            for j in range(2):
                op("dve", lambda e, j=j: e.tensor_copy(out=condT2[:, :, j], in_=PB[0][:, 0:8]),
                   R=[b("PB0")], W=[b("condT2")])

            op("dve", lambda e: e.tensor_copy(out=posr[0:NB, :], in_=posi[0:NB, :]), R=[b("posi")], W=[b("posr")])
            op("pe", lambda e: e.transpose(PB[1][:, 0:NB], posr[0:NB, :], ident_f[0:NB, 0:NB]),
               R=[b("posr"), b("ident_f")], W=[b("PB1")])
            op("dve", lambda e: e.tensor_copy(out=posf[:], in_=PB[1][:, 0:NB]), R=[b("PB1")], W=[b("posf")])
            op("dve", lambda e: e.tensor_tensor(out=ang[:], in0=posf[:, :, None].to_broadcast([128, NB, 32]),
                                                in1=invf[:, None, :].to_broadcast([128, NB, 32]), op=ALU.mult),
               R=[b("posf"), b("invf")], W=bl(EW))
            TWO_PI = float(2.0 * np.pi)
            tmpf = ew[:, 2048:2048 + NB * 32].rearrange("p (n f) -> p n f", f=32)
            angi = Mbuf[0][:, 0:2 * NB * 32].bitcast(I32).rearrange("p (n f) -> p n f", f=32)
            op("dve", lambda e: e.tensor_scalar(out=ang2[:], in0=ang[:], scalar1=float(np.pi / 2), scalar2=None, op0=ALU.add),
               R=bl(EW), W=bl(EW))

            def reduce_angle(a_):
                op("dve", lambda e: e.tensor_scalar(out=tmpf, in0=a_, scalar1=float(1.0 / TWO_PI), scalar2=None, op0=ALU.mult),
                   R=bl(EW), W=bl(EW))
                op("dve", lambda e: e.tensor_copy(out=angi, in_=tmpf), R=bl(EW), W=bl(ATT))
                op("dve", lambda e: e.tensor_copy(out=tmpf, in_=angi), R=bl(ATT), W=bl(EW))
                op("dve", lambda e: e.scalar_tensor_tensor(out=a_, in0=tmpf, scalar=-TWO_PI, in1=a_, op0=ALU.mult, op1=ALU.add),
                   R=bl(EW), W=bl(EW))
                op("dve", lambda e: e.tensor_scalar(out=tmpf, in0=a_, scalar1=3.14159, scalar2=-TWO_PI, op0=ALU.is_gt, op1=ALU.mult),
                   R=bl(EW), W=bl(EW))
                op("dve", lambda e: e.tensor_tensor(out=a_, in0=a_, in1=tmpf, op=ALU.add), R=bl(EW), W=bl(EW))
                op("dve", lambda e: e.tensor_scalar(out=tmpf, in0=a_, scalar1=-3.14159, scalar2=TWO_PI, op0=ALU.is_lt, op1=ALU.mult),
                   R=bl(EW), W=bl(EW))
                op("dve", lambda e: e.tensor_tensor(out=a_, in0=a_, in1=tmpf, op=ALU.add), R=bl(EW), W=bl(EW))

            reduce_angle(ang)
            reduce_angle(ang2)
            op("act", lambda e: e.activation(out=sinT[:], in_=ang, func=AF.Sin), R=bl(EW), W=[b("sinT")])
            op("act", lambda e: e.activation(out=cosT[:], in_=ang2, func=AF.Sin), R=bl(EW), W=[b("cosT")])

            def layer_setup(l):
                dma("sp", "ld0", small[0:24, :], bada_d[l, :, :], W=[b("small")])
                op("pe", lambda e: e.transpose(PB[0][:, 0:24], small[0:24, :], ident_f[0:24, 0:24]),
                   R=[b("small"), b("ident_f")], W=[b("PB0")])
                op("dve", lambda e: e.tensor_copy(out=badaT[:], in_=PB[0][:, 0:24]), R=[b("PB0")], W=[b("badaT")])
                bg = bada_d[l, 16:24, :].rearrange("a b -> (a b)").unsqueeze(0).to_broadcast([128, D])
                op("dve", lambda e: e.tensor_copy(out=condrep, in_=condT2[:, :, 0:1].to_broadcast([128, 8, 128])),
                   R=[b("condT2")], W=bl(EW))
                dma("sp", "ld1", bgate_bc, bg, W=bl(EW))
                dma("sp", "ld0", vg_bc[:], vg_d[l:l + 1, :].to_broadcast([128, 512]), W=[b("vg_bc")])
                dma("sp", "ld1", vb_bc[:], vb_d[l:l + 1, :].to_broadcast([128, 512]), W=[b("vb_bc")])
                dma("sp", "ld0", lng_bc[:], lng_d[l:l + 1, :].to_broadcast([128, D]), W=[b("lng_bc")])
                dma("sp", "ld1", lnb_bc[:], lnb_d[l:l + 1, :].to_broadcast([128, D]), W=[b("lnb_bc")])
                for kc in range(8):
                    stg = stage[kc % 2]
                    SN = bl(STG[kc % 2])
                    dma("sp", "st%d" % (kc % 2), stg[:, 0:3 * D], wada_d[l, kc * 128:(kc + 1) * 128, :], W=SN)
                    for j in range(16):
                        op("pe", lambda e, j=j, kc=kc, stg=stg: e.matmul(
                            PB[0][:, 2 * j:2 * j + 2], lhsT=stg[:, j * 128:(j + 1) * 128], rhs=condT2[:, kc, :],
                            start=(kc == 0 and j == 0), stop=(kc == 7), skip_group_check=True),
                           R=SN + [b("condT2")], W=[b("PB0")])
                    for n2 in range(2):
                        op("pe", lambda e, n2=n2, kc=kc, stg=stg: e.matmul(
                            PB[1 + n2][:, :], lhsT=condrep[:, kc, :], rhs=stg[:, 2048 + n2 * 512:2048 + (n2 + 1) * 512],
                            start=(kc == 0), stop=(kc == 7)),
                           R=SN + bl(EW), W=[b("PB%d" % (1 + n2))])
                op("dve", lambda e: e.tensor_copy(out=modT[:], in_=PB[0][:, 0:32].rearrange("p (j t) -> p j t", t=2)[:, :, 0]),
                   R=[b("PB0")], W=[b("modT")])
                for j in range(2):
                    op("dve", lambda e, j=j: e.tensor_tensor(out=shiftT2[:, :, j], in0=modT[:, 0:8], in1=badaT[:, 0:8], op=ALU.add),
                       R=[b("modT"), b("badaT")], W=[b("shiftT2")])
                op("dve", lambda e: e.scalar_tensor_tensor(out=scaleT1[:], in0=modT[:, 8:16], scalar=1.0, in1=badaT[:, 8:16],
                                                           op0=ALU.add, op1=ALU.add),
                   R=[b("modT"), b("badaT")], W=[b("scaleT1")])
                for n2 in range(2):
                    op("dve", lambda e, n2=n2: e.tensor_tensor(out=gate_bc[:, n2 * 512:(n2 + 1) * 512], in0=PB[1 + n2][:, :],
                                                               in1=bgate_bc[:, n2 * 512:(n2 + 1) * 512], op=ALU.add),
                       R=[b("PB%d" % (1 + n2))] + bl(EW), W=bl(EW))
                for kc in range(8):
                    stg = stage[kc % 2]
                    SN = bl(STG[kc % 2])
                    dma("sp", "st%d" % (kc % 2), stg[:, 0:D], wout_d[l, kc * 128:(kc + 1) * 128, :], W=SN)
                    op(("dve", "pool")[kc % 2], lambda e, kc=kc, stg=stg: e.tensor_tensor(
                        out=w_out_bf[:, kc, :], in0=stg[:, 0:D], in1=gate_bc[:], op=ALU.mult),
                       R=SN + bl(EW), W=[b("w_out_bf")])
                ntl = [(i * 512, min(512, DIN - i * 512)) for i in range(7)]
                for kc in range(8):
                    stg = stage[kc % 2]
                    SN = bl(STG[kc % 2])
                    dma("sp", "st%d" % (kc % 2), stg[:, :], win_d[l, kc * 128:(kc + 1) * 128, :], W=SN)
                    for nt, (cs, cw) in enumerate(ntl):
                        op("pe", lambda e, nt=nt, cs=cs, cw=cw, kc=kc, stg=stg: e.matmul(
                            PB[nt][0:2, 0:cw], lhsT=shiftT2[:, kc, :], rhs=stg[:, cs:cs + cw],
                            start=(kc == 0), stop=(kc == 7)),
                           R=SN + [b("shiftT2")], W=[b("PB%d" % nt)])
                    for pi, (s0, d0, wd) in enumerate(PERM):
                        en = ("dve", "pool", "act")[pi % 3] if wd >= 512 else "pool"
                        if en == "act":
                            op("act", lambda e, s0=s0, d0=d0, wd=wd, kc=kc, stg=stg: e.activation(
                                out=w_in_bf[:, kc, d0:d0 + wd], in_=stg[:, s0:s0 + wd], func=AF.Copy, scale=scaleT1[:, kc:kc + 1]),
                               R=SN + [b("scaleT1")], W=[b("w_in_bf")])
                        else:
                            op(en, lambda e, s0=s0, d0=d0, wd=wd, kc=kc, stg=stg: e.tensor_scalar(
                                out=w_in_bf[:, kc, d0:d0 + wd], in0=stg[:, s0:s0 + wd], scalar1=scaleT1[:, kc:kc + 1],
                                scalar2=None, op0=ALU.mult),
                               R=SN + [b("scaleT1")], W=[b("w_in_bf")])
                for nt, (cs, cw) in enumerate(ntl):
                    op("dve", lambda e, nt=nt, cs=cs, cw=cw: e.tensor_copy(out=c0src[:, cs:cs + cw], in_=PB[nt][0:1, 0:cw]),
                       R=[b("PB%d" % nt)], W=[b("score")])
                for (s0, d0, wd) in PERM:
                    op("dve", lambda e, s0=s0, d0=d0, wd=wd: e.tensor_copy(out=c0dst[0:1, d0:d0 + wd], in_=c0src[:, s0:s0 + wd]),
                       R=[b("score")], W=[b("score")])
                op("dve", lambda e: e.tensor_copy(out=c0hi[0:1, :], in_=c0dst[0:1, :]), R=[b("score")], W=bl(ATT))
                op("dve", lambda e: e.tensor_copy(out=rows[0:1, :], in_=c0hi[0:1, :]), R=bl(ATT), W=[b("rows")])
                op("dve", lambda e: e.tensor_tensor(out=c0src[:, :], in0=c0dst[0:1, :], in1=c0hi[0:1, :], op=ALU.subtract),
                   R=[b("score")] + bl(ATT), W=[b("score")])
                op("dve", lambda e: e.tensor_copy(out=rows[32:33, :], in_=c0src[:, :]), R=[b("score")], W=[b("rows")])
                dma("sp", "ld0", wsp_f[:], wsp_d[l].rearrange("g t s -> t g s"), W=bl(EW))
                op("pool", lambda e: e.affine_select(out=wsp_f[:], in_=wsp_f[:], pattern=[[0, 8], [-1, 128]],
                                                     compare_op=ALU.is_ge, fill=zero_reg, base=0, channel_multiplier=1),
                   R=bl(EW), W=bl(EW))
                op("pool", lambda e: e.tensor_copy(out=wsp_b[:], in_=wsp_f[:]), R=bl(EW), W=[b("Z")])
                for g in range(8):
                    op("pe", lambda e, g=g: e.transpose(PT[:, g * 128:(g + 1) * 128], wsp_b[:, g, :], ident_b[:]),
                       R=[b("Z"), b("ident_b")], W=[b("PT")])
                op("dve", lambda e: e.tensor_copy(out=WT[:], in_=PT[:, :].rearrange("p (g t) -> p g t", g=8)),
                   R=[b("PT")], W=[b("WT")])
                dma("sp", "ld1", small[0:8, :], bsp_d[l, :, :], W=[b("small")])
                op("pe", lambda e: e.transpose(PB[0][:, 0:8], small[0:8, :], ident_f[0:8, 0:8]),
                   R=[b("small"), b("ident_f")], W=[b("PB0")])
                op("dve", lambda e: e.tensor_copy(out=bsT[:], in_=PB[0][:, 0:8]), R=[b("PB0")], W=[b("bsT")])

            def rope(src4, dst_lo, dst_hi, nh, blk):
                cb = cosT[:, blk:blk + 1, :].to_broadcast([128, nh, 32])
                sbb = sinT[:, blk:blk + 1, :].to_broadcast([128, nh, 32])
                x1, x2 = src4[:, :, 0, :], src4[:, :, 1, :]
                return cb, sbb, x1, x2

            def ln_stats(src_ap_halves, srcbufs, C=False):
                st_, mv_, rs_ = (st12C, mvC, rstdC) if C else (st12, mv, rstd)
                sn, mn, rsn = ("st12C", "mvC", "rstdC") if C else ("st12", "mv", "rstd")
                for hh, ap in enumerate(src_ap_halves):
                    op("dve", lambda e, hh=hh, ap=ap: e.bn_stats(out=st_[:, hh * 6:(hh + 1) * 6], in_=ap),
                       R=srcbufs, W=[b(sn)])
                nst = 6 * len(src_ap_halves)
                op("dve", lambda e: e.bn_aggr(out=mv_[:], in_=st_[:, 0:nst]), R=[b(sn)], W=[b(mn)])
                op("dve", lambda e: e.tensor_scalar(out=rs_[:], in0=mv_[:, 1:2], scalar1=LN_EPS, scalar2=None, op0=ALU.add),
                   R=[b(mn)], W=[b(rsn)])
                op("act", lambda e: e.activation(out=rs_[:], in_=rs_[:], func=AF.Sqrt), R=[b(rsn)], W=[b(rsn)])
                op("dve", lambda e: e.reciprocal(out=rs_[:], in_=rs_[:]), R=[b(rsn)], W=[b(rsn)])

            def kkn(blks):
                return [b("KK%d" % j) for j in blks]

            def stageAB(l, i, src_d):
                n = (i + 1) * 128
                p2 = i % 2
                xb, xbn = xblk[p2], "xblk%d" % p2
                qq, qqn = QQ[p2], "QQ%d" % p2
                szb_, szbn = szb[p2], "szb%d" % p2
                yb, ybn = y_bf[p2], "y_bf%d" % p2
                Mb, Mbn = Mbuf[p2], "Mbuf%d" % p2
                dma("sp", "xq%d" % p2, xb, src_d[i * 128:(i + 1) * 128, :], W=[b(xbn)])
                ln_stats([xb[:, 0:512], xb[:, 512:1024]], [b(xbn)])
                op("dve", lambda e: e.tensor_scalar(out=xn, in0=xb, scalar1=mv[:, 0:1], scalar2=rstd[:],
                                                    op0=ALU.subtract, op1=ALU.mult),
                   R=[b(xbn), b("mv"), b("rstd")], W=[b("Z")])
                for kc in range(8):
                    op("pe", lambda e, kc=kc: e.transpose(PT[:, kc * 128:(kc + 1) * 128], xn[:, kc * 128:(kc + 1) * 128], ident_b[:]),
                       R=[b("Z"), b("ident_b")], W=[b("PT")])
                op("act", lambda e: e.copy(out=xnT[:], in_=PT[:, :]), R=[b("PT")], W=[b("xnT")])
                yield 4.0

                def proj_tile(nt, cs, cw):
                    pb = PB[nt % 2]
                    pbn = "PB%d" % (nt % 2)
                    for kc in range(8):
                        op("pe", lambda e, kc=kc: e.matmul(pb[:, 0:cw], lhsT=xnT[:, kc * 128:(kc + 1) * 128],
                                                           rhs=w_in_bf[:, kc, cs:cs + cw], start=(kc == 0), stop=False),
                           R=[b("xnT"), b("w_in_bf")], W=[b(pbn)])
                    op("pe", lambda e: e.matmul(pb[:, 0:cw], lhsT=ones33[:, :], rhs=rows[:, cs:cs + cw], start=False, stop=True),
                       R=[b("ones33"), b("rows")], W=[b(pbn)])
                    return pb, pbn

                pb, pbn = proj_tile(0, C_U, 512)
                op("act", lambda e: e.copy(out=u_sb, in_=pb[:, :]), R=[b(pbn)], W=[b("u_sb")])
                yield 3.0
                pb, pbn = proj_tile(1, C_V, 512)
                ln_stats([pb[:, 0:512]], [b(pbn)])
                op("dve", lambda e: e.tensor_scalar(out=vn0, in0=pb[:, :], scalar1=mv[:, 0:1], scalar2=rstd[:],
                                                    op0=ALU.subtract, op1=ALU.mult),
                   R=[b(pbn), b("mv"), b("rstd")], W=[b("vn0")])
                op("pool", lambda e: e.tensor_tensor(out=vn0, in0=vn0, in1=vg_bc[:], op=ALU.mult),
                   R=[b("vn0"), b("vg_bc")], W=[b("vn0")])
                op("pool", lambda e: e.tensor_tensor(out=vn_bf[:], in0=vn0, in1=vb_bc[:], op=ALU.add),
                   R=[b("vn0"), b("vb_bc")], W=[b("vn_bf")])
                yield 4.0
                pb, pbn = proj_tile(2, C_ZA, 512)
                op("act", lambda e: e.activation(out=sza, in_=pb[:, :], func=AF.Silu), R=[b(pbn)], W=[b("sza")])
                for g in range(8):
                    op("pe", lambda e, g=g: e.matmul(PB[2][:, g * 64:(g + 1) * 64], lhsT=WT[:, g, :], rhs=vn_bf[:, g * 64:(g + 1) * 64],
                                                     start=True, stop=True),
                       R=[b("WT"), b("vn_bf")], W=[b("PB2")])
                op("dve", lambda e: e.tensor_tensor(out=t1.rearrange("p (g d) -> p g d", g=8),
                                                    in0=PB[2][:, :].rearrange("p (g d) -> p g d", g=8),
                                                    in1=bsT[:, :, None].to_broadcast([128, 8, 64]), op=ALU.add),
                   R=[b("PB2"), b("bsT")], W=[b("t1")])
                op("pool", lambda e: e.tensor_tensor(out=t1, in0=t1, in1=u_sb, op=ALU.mult),
                   R=[b("t1"), b("u_sb")], W=[b("t1")])
                op("pool", lambda e: e.tensor_tensor(out=yb[:, 0:512], in0=t1, in1=sza, op=ALU.mult),
                   R=[b("t1"), b("sza")], W=[b(ybn)])
                yield 4.0
                for (nt, cs, off) in ((3, C_Q, 0), (4, C_QI, 64)):
                    pb, pbn = proj_tile(nt, cs, 512)
                    s4 = pb[:, :].rearrange("p (h a d) -> p h a d", h=8, a=2)
                    cb, sbb, x1, x2 = rope(s4, None, None, 8, i)
                    op("dve", lambda e: e.tensor_tensor(out=ra[:], in0=x1, in1=cb, op=ALU.mult), R=[b(pbn), b("cosT")], W=[b("ra")])
                    op("dve", lambda e: e.tensor_tensor(out=rb[:], in0=x2, in1=sbb, op=ALU.mult), R=[b(pbn), b("sinT")], W=[b("rb")])
                    op("dve", lambda e: e.tensor_tensor(out=rc[:], in0=x2, in1=cb, op=ALU.mult), R=[b(pbn), b("cosT")], W=[b("rc")])
                    op("dve", lambda e: e.tensor_tensor(out=rd[:], in0=x1, in1=sbb, op=ALU.mult), R=[b(pbn), b("sinT")], W=[b("rd")])
                    op("pool", lambda e, off=off: e.tensor_tensor(out=Z[:, :, off:off + 32], in0=ra[:], in1=rb[:], op=ALU.subtract),
                       R=[b("ra"), b("rb")], W=[b("Z")])
                    op("pool", lambda e, off=off: e.tensor_tensor(out=Z[:, :, off + 32:off + 64], in0=rc[:], in1=rd[:], op=ALU.add),
                       R=[b("rc"), b("rd")], W=[b("Z")])
                    yield 4.0
                pb, pbn = proj_tile(5, C_ZB, 512)
                op("act", lambda e: e.activation(out=szb_, in_=pb[:, :], func=AF.Silu), R=[b(pbn)], W=[b(szbn)])
                yield 3.0
                pb, pbn = proj_tile(6, C_K, 200)
                s4 = pb[:, 0:128].rearrange("p (h a d) -> p h a d", h=2, a=2)
                cb, sbb, x1, x2 = rope(s4, None, None, 2, i)
                op("dve", lambda e: e.tensor_tensor(out=ra[:, 0:2, :], in0=x1, in1=cb, op=ALU.mult), R=[b(pbn), b("cosT")], W=[b("ra")])
                op("dve", lambda e: e.tensor_tensor(out=rb[:, 0:2, :], in0=x2, in1=sbb, op=ALU.mult), R=[b(pbn), b("sinT")], W=[b("rb")])
                op("dve", lambda e: e.tensor_tensor(out=rc[:, 0:2, :], in0=x2, in1=cb, op=ALU.mult), R=[b(pbn), b("cosT")], W=[b("rc")])
                op("dve", lambda e: e.tensor_tensor(out=rd[:, 0:2, :], in0=x1, in1=sbb, op=ALU.mult), R=[b(pbn), b("sinT")], W=[b("rd")])
                kz3 = KZ[:, :].rearrange("p (h d) -> p h d", h=2)
                op("pool", lambda e: e.tensor_tensor(out=kz3[:, :, 0:32], in0=ra[:, 0:2, :], in1=rb[:, 0:2, :], op=ALU.subtract),
                   R=[b("ra"), b("rb")], W=[b("KZ")])
                op("pool", lambda e: e.tensor_tensor(out=kz3[:, :, 32:64], in0=rc[:, 0:2, :], in1=rd[:, 0:2, :], op=ALU.add),
                   R=[b("rc"), b("rd")], W=[b("KZ")])
                op("act", lambda e: e.copy(out=Vst[:, i, 0:64], in_=pb[:, 128:192]), R=[b(pbn)], W=[b("V%d" % i)])
                op("dve", lambda e: e.tensor_copy(out=w8[:], in_=pb[:, 192:200]), R=[b(pbn)], W=[b("w8")])
                op("dve", lambda e: e.tensor_tensor(out=diagW[:], in0=ident_b[:, None, :].to_broadcast([128, 8, 128]),
                                                    in1=w8[:, :, None].to_broadcast([128, 8, 128]), op=ALU.mult),
                   R=[b("ident_b"), b("w8")], W=[b("diagW")])
                op("pe", lambda e: e.transpose(PT[:, 0:128], KZ[:, :], ident_b[:]), R=[b("KZ"), b("ident_b")], W=[b("PT")])
                op("act", lambda e: e.copy(out=KK[:, i * 128:(i + 1) * 128], in_=PT[:, 0:128]), R=[b("PT")], W=[b("KK%d" % i)])
                for h in range(8):
                    op("pe", lambda e, h=h: e.transpose(PT[:, h * 128:(h + 1) * 128], Z[:, h, :], ident_b[:]),
                       R=[b("Z"), b("ident_b")], W=[b("PT")])
                op("act", lambda e: e.copy(out=qq[:], in_=PT[:, :].rearrange("p (h t) -> p h t", h=8)), R=[b("PT")], W=[b(qqn)])
                yield 5.0

                nch = (n + 511) // 512
                steps = [(c, h) for c in range(nch) for h in range(8)]

                LBK = (0, 1, 3, 4)

                def ix_mm1(k):
                    c, h = steps[k]
                    cw = min(512, n - c * 512)
                    pb, pbn = PB[LBK[k % 4]], "PB%d" % LBK[k % 4]
                    op("pe", lambda e: e.matmul(pb[:, 0:cw], lhsT=qq[64:128, h, :], rhs=KK[64:128, c * 512:c * 512 + cw],
                                                start=True, stop=True),
                       R=[b(qqn)] + kkn(range(4 * c, 4 * c + cw // 128)), W=[b(pbn)])

                def ix_rest(k):
                    c, h = steps[k]
                    cw = min(512, n - c * 512)
                    pb, pbn = PB[LBK[k % 4]], "PB%d" % LBK[k % 4]
                    rr, rrn = Rr[k % 4], "Rr%d" % (k % 4)
                    if h % 2 == 0:
                        op("act", lambda e: e.activation(out=rr[:, 0:cw], in_=pb[:, 0:cw], func=AF.Relu), R=[b(pbn)], W=[b(rrn)])
                    else:
                        op("dve", lambda e: e.tensor_scalar(out=rr[:, 0:cw], in0=pb[:, 0:cw], scalar1=0.0, scalar2=None, op0=ALU.max),
                           R=[b(pbn)], W=[b(rrn)])
                    op("pe", lambda e: e.matmul(PB[2][:, 0:cw], lhsT=diagW[:, h, :], rhs=rr[:, 0:cw], start=(h == 0), stop=(h == 7)),
                       R=[b("diagW"), b(rrn)], W=[b("PB2")])
                    if h == 7:
                        op("dve", lambda e: e.tensor_scalar(out=score[:, c * 512:c * 512 + cw], in0=PB[2][:, 0:cw], scalar1=1.0, scalar2=-3.0e38,
                                                            op0=ALU.mult, op1=ALU.max, accum_out=cmax[:, c:c + 1]),
                           R=[b("PB2")], W=[b("score"), b("cmax")])
                        op("dve", lambda e: e.tensor_scalar(out=Mb[:, c * 512:c * 512 + cw], in0=score[:, c * 512:c * 512 + cw], scalar1=1.0,
                                                            scalar2=3.0e38, op0=ALU.mult, op1=ALU.min, accum_out=cmin[:, c:c + 1]),
                           R=[b("score")], W=[b(Mbn), b("cmin")])

                for k in range(min(3, len(steps))):
                    ix_mm1(k)
                for k in range(len(steps)):
                    if k + 3 < len(steps):
                        ix_mm1(k + 3)
                    ix_rest(k)
                    yield 0.7
                op("dve", lambda e: e.tensor_reduce(out=rmax[:], in_=cmax[:, 0:nch], axis=AX.X, op=ALU.max), R=[b("cmax")], W=[b("rmax")])
                op("dve", lambda e: e.tensor_reduce(out=rmin[:], in_=cmin[:, 0:nch], axis=AX.X, op=ALU.min), R=[b("cmin")], W=[b("rmin")])
                op("dve", lambda e: e.tensor_tensor(out=w0[:], in0=rmax[:], in1=rmin[:], op=ALU.subtract),
                   R=[b("rmax"), b("rmin")], W=[b("w0")])
                op("dve", lambda e: e.tensor_scalar(out=w0[:], in0=w0[:], scalar1=1.001, scalar2=1.0e-6, op0=ALU.mult, op1=ALU.add),
                   R=[b("w0")], W=[b("w0")])
                op("dve", lambda e: e.tensor_scalar(out=HWt[:], in0=pow2[:], scalar1=w0[:], scalar2=None, op0=ALU.mult),
                   R=[b("pow2"), b("w0")], W=[b("HWt")])
                op("dve", lambda e: e.tensor_tensor(out=mid[0][:], in0=rmin[:], in1=HWt[:, 0:1], op=ALU.add),
                   R=[b("rmin"), b("HWt")], W=[b("mid0")])
                op("pool", lambda e: e.affine_select(out=score[:, i * 128:(i + 1) * 128], in_=score[:, i * 128:(i + 1) * 128],
                                                     pattern=[[-1, 128]], compare_op=ALU.is_ge, fill=neg_reg, base=0, channel_multiplier=1),
                   R=[b("score")], W=[b("score")])
                yield 1.5
                yield -1.0
                for k in range(KITER):
                    mc, mn = mid[k % 2], mid[(k + 1) % 2]
                    mcn, mnn = "mid%d" % (k % 2), "mid%d" % ((k + 1) % 2)
                    op("dve", lambda e: e.tensor_scalar(out=Mb[:, 0:n], in0=score[:, 0:n], scalar1=mc[:], scalar2=0.0,
                                                        op0=ALU.is_ge, op1=ALU.add, accum_out=cnt[:]),
                       R=[b("score"), b(mcn)], W=[b(Mbn), b("cnt")])
                    op("dve", lambda e: e.tensor_scalar(out=stp[:], in0=cnt[:], scalar1=float(TOP) - 0.5, scalar2=0.5,
                                                        op0=ALU.is_ge, op1=ALU.subtract), R=[b("cnt")], W=[b("stp")])
                    op("dve", lambda e: e.scalar_tensor_tensor(out=mn[:], in0=stp[:], scalar=HWt[:, k:k + 1], in1=mc[:],
                                                               op0=ALU.mult, op1=ALU.add),
                       R=[b("stp"), b("HWt"), b(mcn)], W=[b(mnn)])
                    yield 0.5 + n / 1900.0
                mfin, mfinn = mid[KITER % 2], "mid%d" % (KITER % 2)
                op("dve", lambda e: e.tensor_tensor(out=tau[:], in0=mfin[:], in1=HWt[:, KITER:KITER + 1], op=ALU.subtract),
                   R=[b(mfinn), b("HWt")], W=[b("tau")])
                op("dve", lambda e: e.tensor_scalar(out=Mb[:, 0:n], in0=score[:, 0:n], scalar1=tau[:], scalar2=None, op0=ALU.is_ge),
                   R=[b("score"), b("tau")], W=[b(Mbn)])
                yield 0.5 + n / 1900.0

            def stageC(l, i, dst_d):
                n = (i + 1) * 128
                p2 = i % 2
                xb, xbn = xblk[p2], "xblk%d" % p2
                qq, qqn = QQ[p2], "QQ%d" % p2
                szb_, szbn = szb[p2], "szb%d" % p2
                yb, ybn = y_bf[p2], "y_bf%d" % p2
                Mb, Mbn = Mbuf[p2], "Mbuf%d" % p2
                nch = (n + 511) // 512

                def at_prep(c):
                    cw = min(512, n - c * 512)
                    nb = cw // 128
                    mt, mtn = MT[c % 2], "MT%d" % (c % 2)
                    for jj in range(nb):
                        op("pe", lambda e, jj=jj: e.transpose(PT[:, jj * 128:(jj + 1) * 128],
                                                              Mb[:, c * 512 + jj * 128:c * 512 + (jj + 1) * 128], ident_b[:]),
                           R=[b(Mbn), b("ident_b")], W=[b("PT")])
                    op("act", lambda e: e.activation(out=mt[:, 0:nb, :], in_=PT[:, 0:cw].rearrange("p (j t) -> p j t", j=nb),
                                                     func=AF.Identity, scale=30000.0, bias=-30000.0),
                       R=[b("PT")], W=[b(mtn)])

                def at_st(j):
                    ee, een = Ee[j % 2], "Ee%d" % (j % 2)
                    mt, mtn = MT[(j // 4) % 2], "MT%d" % ((j // 4) % 2)
                    jj = j % 4
                    for hf in range(2):
                        op("pe", lambda e, hf=hf: e.matmul(PB[3 + hf][:, :], lhsT=KK[0:64, j * 128:(j + 1) * 128],
                                                           rhs=qq[0:64, 4 * hf:4 * hf + 4, :], start=True, stop=False),
                           R=[b("KK%d" % j), b(qqn)], W=[b("PB%d" % (3 + hf))])
                        op("pe", lambda e, hf=hf: e.matmul(PB[3 + hf][:, :], lhsT=ident_b[:, :],
                                                           rhs=mt[:, jj:jj + 1, :].to_broadcast([128, 4, 128]), start=False, stop=True),
                           R=[b("ident_b"), b(mtn)], W=[b("PB%d" % (3 + hf))])
                        op("act", lambda e, hf=hf: e.activation(out=ee[:, 4 * hf:4 * hf + 4, :],
                                                                in_=PB[3 + hf][:, :].rearrange("p (h t) -> p h t", h=4),
                                                                func=AF.Exp, scale=0.125),
                           R=[b("PB%d" % (3 + hf))], W=[b(een)])

                def at_pv(j):
                    c, jj = j // 4, j % 4
                    ee, een = Ee[j % 2], "Ee%d" % (j % 2)
                    mt, mtn = MT[c % 2], "MT%d" % (c % 2)
                    for h in range(8):
                        ob_ = PB[5 + h // 4]
                        op("pe", lambda e, h=h, ob_=ob_: e.matmul(
                            ob_[:, (h % 4) * 65:(h % 4) * 65 + 65], lhsT=ee[:, h, :], rhs=Vst[:, j, :],
                            start=(j == 0 and h % 4 == 0), stop=(j == i), skip_group_check=True),
                           R=[b(een), b("V%d" % j)], W=[b("PB%d" % (5 + h // 4))])

                at_prep(0)
                at_st(0)
                yield 1.5
                for j in range(i + 1):
                    if j % 4 == 0 and j // 4 + 1 < nch:
                        at_prep(j // 4 + 1)
                    if j + 1 <= i:
                        at_st(j + 1)
                    at_pv(j)
                    yield 2.0
                for hf in range(2):
                    o3 = PB[5 + hf][:, 0:260].rearrange("p (h d) -> p h d", h=4)
                    op("dve", lambda e, hf=hf, o3=o3: e.reciprocal(out=rden[:, 4 * hf:4 * hf + 4], in_=o3[:, :, 64]),
                       R=[b("PB%d" % (5 + hf))], W=[b("rden")])
                    op("dve", lambda e, hf=hf, o3=o3: e.tensor_tensor(out=yb0[:, 4 * hf:4 * hf + 4, :], in0=o3[:, :, 0:64],
                                                                      in1=rden[:, 4 * hf:4 * hf + 4, None].to_broadcast([128, 4, 64]),
                                                                      op=ALU.mult),
                       R=[b("PB%d" % (5 + hf)), b("rden")], W=[b("res")])
                op("dve", lambda e: e.tensor_tensor(out=yb[:, 512:1024], in0=res[:, 0:512], in1=szb_, op=ALU.mult),
                   R=[b("res"), b(szbn)], W=[b(ybn)])
                yield 2.0
                for kc in range(8):
                    op("pe", lambda e, kc=kc: e.transpose(PT[:, kc * 128:(kc + 1) * 128], yb[:, kc * 128:(kc + 1) * 128], ident_b[:]),
                       R=[b(ybn), b("ident_b")], W=[b("PT")])
                op("act", lambda e: e.copy(out=yT[:], in_=PT[:, :]), R=[b("PT")], W=[b("yT")])
                yield 2.0
                for nt in range(2):
                    for kc in range(8):
                        op("pe", lambda e, kc=kc, nt=nt: e.matmul(PB[3 + nt][:, :], lhsT=yT[:, kc * 128:(kc + 1) * 128],
                                                                  rhs=w_out_bf[:, kc, nt * 512:(nt + 1) * 512], start=(kc == 0), stop=(kc == 7)),
                           R=[b("yT"), b("w_out_bf")], W=[b("PB%d" % (3 + nt))])
                    op("dve", lambda e, nt=nt: e.scalar_tensor_tensor(out=res[:, nt * 512:(nt + 1) * 512], in0=xb[:, nt * 512:(nt + 1) * 512],
                                                                      scalar=ALPHA, in1=PB[3 + nt][:, :], op0=ALU.mult, op1=ALU.add),
                       R=[b(xbn), b("PB%d" % (3 + nt))], W=[b("res")])
                    yield 2.5
                ln_stats([res[:, 0:512], res[:, 512:1024]], [b("res")], C=True)
                op("dve", lambda e: e.tensor_scalar(out=rn, in0=res, scalar1=mvC[:, 0:1], scalar2=rstdC[:],
                                                    op0=ALU.subtract, op1=ALU.mult),
                   R=[b("res"), b("mvC"), b("rstdC")], W=[b("rn")])
                op("dve", lambda e: e.tensor_tensor(out=res, in0=rn, in1=lng_bc[:], op=ALU.mult),
                   R=[b("rn"), b("lng_bc")], W=[b("res")])
                op("dve", lambda e: e.tensor_tensor(out=rn, in0=res, in1=lnb_bc[:], op=ALU.add),
                   R=[b("res"), b("lnb_bc")], W=[b("rn")])
                dma("sp", "oq0", dst_d[i * 128:(i + 1) * 128, :], rn, R=[b("rn")], W=[b("hbm_out")])
                yield 4.0

            def dry_costs(mk):
                sch.dry = True
                cs = list(mk())
                sch.dry = False
                return cs

            def run_seq(g):
                for c in g:
                    if c < 0:
                        return True
                return False

            def interleave(gens, tots):
                acc = [0.0] * len(gens)
                live = [True] * len(gens)
                while any(live):
                    k = min((j for j in range(len(gens)) if live[j]), key=lambda j: acc[j] / tots[j])
                    try:
                        c = next(gens[k])
                        if c > 0:
                            acc[k] += c
                    except StopIteration:
                        live[k] = False

            def pipeline_step(mkC, mkAB):
                gAB = mkAB() if mkAB is not None else None
                if _SEQ or gAB is None or mkC is None:
                    if mkC is not None:
                        run_seq(mkC())
                    if gAB is not None:
                        run_seq(gAB)
                        run_seq(gAB)
                    return
                if _MODE == 1:
                    cAB = [c for c in dry_costs(mkAB) if c > 0]
                    cC = dry_costs(mkC)
                    interleave([mkC(), gAB], [max(sum(cC), 1e-6), max(sum(cAB), 1e-6)])
                    return
                cs = dry_costs(mkAB)
                kb = cs.index(-1.0)
                totB = max(sum(cs[kb + 1:]), 1e-6)
                totC = max(sum(dry_costs(mkC)), 1e-6)
                run_seq(gAB)
                interleave([mkC(), gAB], [totC, totB])

            for l in range(NL):
                layer_setup(l)
                src = x_d if l == 0 else x1_d
                dst = x1_d if l == 0 else out_d
                if l == 1:
                    sch.wait_all("sp", "oq0")
                pipeline_step(None, lambda: stageAB(l, 0, src))
                for i in range(1, NB):
                    pipeline_step(lambda: stageC(l, i - 1, dst), lambda: stageAB(l, i, src))
                pipeline_step(lambda: stageC(l, NB - 1, dst), None)
            sch.engs["sp"].waited.pop("oq0", None)
            sch.wait_all("sp", "oq0")
    return nc


def _consts():
    ident = np.eye(128, dtype=np.float32)
    invf = (np.float32(10000.0) ** (-np.arange(0, 64, 2, dtype=np.float32) / np.float32(64))).astype(np.float32)
    invf = np.ascontiguousarray(np.broadcast_to(invf[None, :], (128, 32))).astype(np.float32)
    pow2 = np.ascontiguousarray(np.broadcast_to((2.0 ** -(np.arange(KITER + 1) + 1.0))[None, :], (128, KITER + 1))).astype(np.float32)
    return ident, invf, pow2


def make_in_maps(NB, x, c, positions, w_ada, b_ada, w_in, v_norm_g, v_norm_b, w_spatial, b_spatial, w_out, ln_g, ln_b, ncores):
    S = NB * 128
    ident, invf, pow2 = _consts()
    f = lambda a: np.ascontiguousarray(np.asarray(a, dtype=np.float32))
    shared = {
        "w_ada": f(w_ada), "b_ada": f(b_ada).reshape(NL, 24, 128), "w_in": f(w_in),
        "v_norm_g": f(v_norm_g), "v_norm_b": f(v_norm_b), "w_spatial": f(w_spatial), "b_spatial": f(b_spatial),
        "w_out": f(w_out), "ln_g": f(ln_g), "ln_b": f(ln_b), "ident": ident, "invf": invf, "pow2": pow2,
    }
    maps = []
    for bi in range(ncores):
        m = dict(shared)
        m["x"] = f(x[bi, :S])
        m["c"] = f(c[bi]).reshape(8, 128)
        m["pos"] = np.ascontiguousarray(np.asarray(positions[bi, :S], dtype=np.int32)).reshape(NB, 128)
        maps.append(m)
    return maps


_NC_CACHE = {}


def kernel(x, c, positions, w_ada, b_ada, w_in, v_norm_g, v_norm_b, w_spatial, b_spatial, w_out, ln_g, ln_b):
    x = np.asarray(x)
    Bn, S, _ = x.shape
    NB = S // 128
    if NB not in _NC_CACHE:
        _NC_CACHE[NB] = build(NB)
    nc = _NC_CACHE[NB]
    maps = make_in_maps(NB, x, c, positions, w_ada, b_ada, w_in, v_norm_g, v_norm_b, w_spatial, b_spatial, w_out, ln_g, ln_b, Bn)
    res = run_bass_kernel_spmd(nc, maps, core_ids=list(range(Bn)))
    out = np.stack([np.asarray(r["out"]) for r in res.results], axis=0).astype(np.float32)
    return out
```
